# Optimizing a Trainium2 kernel written in Bass

```python
import math
import jax
import jax.numpy as jnp
from jax import lax
import numpy as np

D_MODEL = 1024
BATCH = 8
SEQ = 4096
DEPTH = 2

D_MIX = D_MODEL
N_MIXERS = 4
G_WIDTH = D_MIX // N_MIXERS
HEAD_DIM = 64
G_HEADS = G_WIDTH // HEAD_DIM
N_DIR = 2
CONV_K = 4
LRU_BLOCKS = G_HEADS
LRU_C = 8.0
CHUNK = 64
GATE_CAP = 15.0
RWKV_W_RANK = 32
RWKV_A_RANK = 32
RWKV_G_RANK = 64
RWKV_GN_EPS = 64e-5
RMS_EPS = 1e-6
D_FF = -(-(8 * D_MODEL) // (3 * 256)) * 256

A_COLS = 2 * G_WIDTH
B_COLS = 4 * G_WIDTH + 2 * N_DIR * G_HEADS
C_COLS = 4 * G_WIDTH + 2 * N_DIR * G_HEADS
D_COLS = 3 * G_WIDTH + N_DIR * RWKV_W_RANK + N_DIR * RWKV_A_RANK + RWKV_G_RANK
P_IN = A_COLS + B_COLS + C_COLS + D_COLS

kernel_name = 'hymba_style_bidir_hybrid_block'


def rms_norm(x, g, eps=RMS_EPS):
    xf = x.astype(jnp.float32)
    y = xf * lax.rsqrt(jnp.mean(xf * xf, axis=-1, keepdims=True) + eps)
    return (y * g.astype(jnp.float32)).astype(x.dtype)


def head_rms(x, eps=RMS_EPS):
    return x * lax.rsqrt(jnp.mean(x * x, axis=-1, keepdims=True) + eps)


def l2_normalize(x, eps=1e-6):
    return x * lax.rsqrt(jnp.sum(x * x, axis=-1, keepdims=True) + eps)


def soft_cap(x):
    return GATE_CAP * jnp.tanh(x / GATE_CAP)


def flip_seq(t):
    return jnp.flip(t, axis=1)


def split_heads(t):
    return t.reshape(*t.shape[:-1], G_HEADS, HEAD_DIM)


def conv_centred(x, w):
    k_w, ch = w.shape
    return lax.conv_general_dilated(
        x, w[:, None, :].astype(x.dtype), window_strides=(1,),
        padding=[(k_w // 2, k_w - 1 - k_w // 2)],
        dimension_numbers=('NWC', 'WIO', 'NWC'), feature_group_count=ch)


def to_chunks(t, size):
    bsz, seq, nh = t.shape[:3]
    t = t.reshape(bsz, seq // size, size, nh, *t.shape[3:])
    return jnp.moveaxis(t, 3, 1)


def from_chunks(o):
    nc, bsz, nh, size, d = o.shape
    return jnp.transpose(o, (1, 0, 3, 2, 4)).reshape(bsz, nc * size, nh, d)


def linear_scan(a, b, reverse):
    def combine(left, right):
        return left[0] * right[0], right[0] * left[1] + right[1]
    return lax.associative_scan(combine, (a, b), reverse=reverse, axis=1)[1]


def rglru_mixer(u, conv_w, conv_b, gate_w, gate_b, lam):
    bsz, seq, _ = u.shape
    xb, yb = jnp.split(u, 2, axis=-1)
    xc = conv_centred(xb, conv_w) + conv_b
    xr = xc.reshape(bsz, seq, LRU_BLOCKS, G_WIDTH // LRU_BLOCKS)
    pre = jnp.einsum('bsni,dgnio->bsdgno', xr, gate_w).reshape(bsz, seq, N_DIR, 2, G_WIDTH) + gate_b
    r = jax.nn.sigmoid(pre[:, :, :, 0])
    i = jax.nn.sigmoid(pre[:, :, :, 1])
    log_a = -LRU_C * r * jax.nn.softplus(-lam)
    a = jnp.exp(log_a)
    b = jnp.sqrt(-jnp.expm1(2.0 * log_a)) * i * xc[:, :, None, :]
    h = linear_scan(a[:, :, 0], b[:, :, 0], False) + linear_scan(a[:, :, 1], b[:, :, 1], True)
    return h * jax.nn.gelu(yb)


def gated_delta_chunked(q, k, v, g, beta):
    dk = q.shape[-1]
    dv = v.shape[-1]
    q = q * dk ** -0.5
    q, k, v = (to_chunks(t, CHUNK) for t in (q, k, v))
    g, beta = (to_chunks(t, CHUNK) for t in (g, beta))
    gc = jnp.cumsum(g, axis=-1)
    tri = jnp.tril(jnp.ones((CHUNK, CHUNK), bool))
    strict = jnp.tril(jnp.ones((CHUNK, CHUNK), bool), -1)
    decay = jnp.exp(jnp.where(tri, gc[..., :, None] - gc[..., None, :], -jnp.inf))
    kb = k * beta[..., None]
    lower = jnp.where(strict, jnp.einsum('bhcld,bhcmd->bhclm', kb, k) * decay, 0.0)
    a_mat = lower + jnp.eye(CHUNK, dtype=lower.dtype)
    rhs = jnp.concatenate([v * beta[..., None], kb * jnp.exp(gc)[..., None]], axis=-1)
    sol = lax.linalg.triangular_solve(a_mat, rhs, left_side=True, lower=True, unit_diagonal=True)
    u_c, w_c = sol[..., :dv], sol[..., dv:]
    qk = jnp.einsum('bhcld,bhcmd->bhclm', q, k) * decay
    xs = tuple(jnp.moveaxis(t, 2, 0) for t in (q, k, u_c, w_c, qk, gc))

    def step(state, inp):
        q_c, k_c, u_i, w_i, qk_c, g_c = inp
        v_new = u_i - jnp.einsum('bhlk,bhkv->bhlv', w_i, state)
        o = (jnp.einsum('bhlk,bhkv->bhlv', q_c * jnp.exp(g_c)[..., None], state)
             + jnp.einsum('bhlm,bhmv->bhlv', qk_c, v_new))
        g_last = g_c[..., -1]
        state = (state * jnp.exp(g_last)[..., None, None]
                 + jnp.einsum('bhlk,bhlv->bhkv', k_c * jnp.exp(g_last[..., None] - g_c)[..., None], v_new))
        return state, o

    bsz, nh = q.shape[:2]
    s0 = jnp.zeros((bsz, nh, dk, dv), jnp.float32)
    _, o = lax.scan(step, s0, xs)
    return from_chunks(o)


def gdn_mixer(u, conv_w, a_log, dt_bias, norm_g):
    bsz, seq, _ = u.shape
    nh = N_DIR * G_HEADS
    qkv, z, alpha, beta = jnp.split(u, [3 * G_WIDTH, 4 * G_WIDTH, 4 * G_WIDTH + nh], axis=-1)
    qkv = jax.nn.silu(conv_centred(qkv, conv_w))
    q, k, v = (split_heads(t) for t in jnp.split(qkv, 3, axis=-1))
    q, k = l2_normalize(q), l2_normalize(k)
    g = -jnp.exp(a_log) * jax.nn.softplus(alpha.reshape(bsz, seq, N_DIR, G_HEADS) + dt_bias)
    beta = jax.nn.sigmoid(beta.reshape(bsz, seq, N_DIR, G_HEADS))
    o_f = gated_delta_chunked(q, k, v, g[:, :, 0], beta[:, :, 0])
    o_b = flip_seq(gated_delta_chunked(flip_seq(q), flip_seq(k), flip_seq(v),
                                       flip_seq(g[:, :, 1]), flip_seq(beta[:, :, 1])))
    o = head_rms(o_f + o_b) * norm_g * jax.nn.silu(split_heads(z))
    return o.reshape(bsz, seq, G_WIDTH)


def mlstm_chunked(q, k, v, ig, lf):
    dk = q.shape[-1]
    dv = v.shape[-1]
    k = k * dk ** -0.5
    q, k, v = (to_chunks(t, CHUNK) for t in (q, k, v))
    ig, lf = (to_chunks(t, CHUNK) for t in (ig, lf))
    b = jnp.cumsum(lf, axis=-1)
    tri = jnp.tril(jnp.ones((CHUNK, CHUNK), bool))
    d_log = jnp.where(tri, b[..., :, None] - b[..., None, :] + ig[..., None, :], -jnp.inf)
    d_max = jnp.max(d_log, axis=-1)
    qk = jnp.einsum('bhcld,bhcmd->bhclm', q, k)
    w_end = b[..., -1:] - b + ig
    w_end_max = jnp.max(w_end, axis=-1)
    xs = tuple(jnp.moveaxis(t, 2, 0) for t in (q, k, v, b, d_log, d_max, qk, w_end, w_end_max))

    def step(carry, inp):
        c_st, n_st, m_st = carry
        q_c, k_c, v_c, b_c, dl_c, dm_c, qk_c, we_c, wem_c = inp
        inter = b_c + m_st[..., None]
        m_t = jnp.maximum(inter, dm_c)
        p = qk_c * jnp.exp(dl_c - m_t[..., None])
        s_inter = jnp.exp(inter - m_t)
        num = (jnp.einsum('bhlm,bhmd->bhld', p, v_c)
               + s_inter[..., None] * jnp.einsum('bhlk,bhkv->bhlv', q_c, c_st))
        den = jnp.sum(p, axis=-1) + s_inter * jnp.einsum('bhlk,bhk->bhl', q_c, n_st)
        h = num / jnp.maximum(jnp.abs(den), jnp.exp(-m_t))[..., None]
        m_new = jnp.maximum(b_c[..., -1] + m_st, wem_c)
        dec = jnp.exp(b_c[..., -1] + m_st - m_new)
        sc = jnp.exp(we_c - m_new[..., None])
        c_st = dec[..., None, None] * c_st + jnp.einsum('bhl,bhlk,bhlv->bhkv', sc, k_c, v_c)
        n_st = dec[..., None] * n_st + jnp.einsum('bhl,bhlk->bhk', sc, k_c)
        return (c_st, n_st, m_new), h

    bsz, nh = q.shape[:2]
    init = (jnp.zeros((bsz, nh, dk, dv), jnp.float32),
            jnp.zeros((bsz, nh, dk), jnp.float32),
            jnp.zeros((bsz, nh), jnp.float32))
    _, h = lax.scan(step, init, xs)
    return from_chunks(h)


def mlstm_mixer(u, gate_bias, norm_g):
    bsz, seq, _ = u.shape
    nh = N_DIR * G_HEADS
    q, k, v, o, gi, gf = jnp.split(
        u, [G_WIDTH, 2 * G_WIDTH, 3 * G_WIDTH, 4 * G_WIDTH, 4 * G_WIDTH + nh], axis=-1)
    q, k, v = split_heads(q), split_heads(k), split_heads(v)
    ig = soft_cap(gi.reshape(bsz, seq, N_DIR, G_HEADS) + gate_bias[:, 0])
    lf = jax.nn.log_sigmoid(soft_cap(gf.reshape(bsz, seq, N_DIR, G_HEADS) + gate_bias[:, 1]))
    h_f = mlstm_chunked(q, k, v, ig[:, :, 0], lf[:, :, 0])
    h_b = flip_seq(mlstm_chunked(flip_seq(q), flip_seq(k), flip_seq(v),
                                 flip_seq(ig[:, :, 1]), flip_seq(lf[:, :, 1])))
    h = head_rms(h_f + h_b) * split_heads(norm_g) * jax.nn.sigmoid(split_heads(o))
    return h.reshape(bsz, seq, G_WIDTH)


def rwkv_scan(r, w, k, v, kk, a, reverse):
    bsz, _, nh, d = r.shape

    def step(state, inp):
        r_t, w_t, k_t, v_t, kk_t, a_t = inp
        sa = jnp.einsum('bhij,bhj->bhi', state, -kk_t)
        state = (state * w_t[:, :, None, :] + sa[..., None] * (kk_t * a_t)[:, :, None, :]
                 + v_t[..., None] * k_t[:, :, None, :])
        return state, jnp.einsum('bhij,bhj->bhi', state, r_t)

    s0 = jnp.zeros((bsz, nh, d, d), jnp.float32)
    xs = tuple(jnp.moveaxis(t, 1, 0) for t in (r, w, k, v, kk, a))
    _, y = lax.scan(step, s0, xs, reverse=reverse)
    return jnp.moveaxis(y, 0, 1)


def rwkv_mixer(u, mu, w0, w_up, a0, a_up, g_up, k_k, k_a, r_k, gn_w, gn_b):
    bsz, seq, _ = u.shape
    u_prev = jnp.pad(u, ((0, 0), (1, 0), (0, 0)))[:, :-1]
    u_next = jnp.pad(u, ((0, 0), (0, 1), (0, 0)))[:, 1:]
    u = u + mu[0] * (u_prev - u) + mu[1] * (u_next - u)
    c1 = 3 * G_WIDTH + N_DIR * RWKV_W_RANK
    c2 = c1 + N_DIR * RWKV_A_RANK
    r, k, v, wd, ad, gd = jnp.split(u, [G_WIDTH, 2 * G_WIDTH, 3 * G_WIDTH, c1, c2], axis=-1)
    wd = wd.reshape(bsz, seq, N_DIR, RWKV_W_RANK)
    ad = ad.reshape(bsz, seq, N_DIR, RWKV_A_RANK)
    z_w = w0 + jnp.einsum('bsnr,nrc->bsnc', jnp.tanh(wd), w_up)
    w = jnp.exp(-jnp.exp(-jax.nn.softplus(-z_w) - 0.5))
    a = jax.nn.sigmoid(a0 + jnp.einsum('bsnr,nrc->bsnc', ad, a_up))
    gate = jnp.einsum('bsr,rc->bsc', jax.nn.sigmoid(gd), g_up)
    kk = l2_normalize(split_heads(k * k_k))
    k_dir = k[:, :, None, :] * (1.0 + (a - 1.0) * k_a)
    rh, vh = split_heads(r), split_heads(v)
    y_f = rwkv_scan(rh, split_heads(w[:, :, 0]), split_heads(k_dir[:, :, 0]), vh, kk,
                    split_heads(a[:, :, 0]), False)
    y_b = rwkv_scan(rh, split_heads(w[:, :, 1]), split_heads(k_dir[:, :, 1]), vh, kk,
                    split_heads(a[:, :, 1]), True)
    y = y_f + y_b
    mean = jnp.mean(y, axis=-1, keepdims=True)
    var = jnp.mean(jnp.square(y - mean), axis=-1, keepdims=True)
    y = (y - mean) * lax.rsqrt(var + RWKV_GN_EPS) * split_heads(gn_w) + split_heads(gn_b)
    bonus = jnp.sum(rh * split_heads(k) * r_k, axis=-1, keepdims=True) * vh
    return (y + bonus).reshape(bsz, seq, G_WIDTH) * gate


def hybrid_layer(x, n_mix_pre, n_mix_post, n_ffn_pre, n_ffn_post, w_in, w_out,
                 lru_conv_w, lru_conv_b, lru_gate_w, lru_gate_b, lru_lambda,
                 gdn_conv_w, gdn_a_log, gdn_dt_bias, gdn_norm,
                 mlstm_gate_bias, mlstm_norm,
                 rwkv_mu, rwkv_w0, rwkv_w_up, rwkv_a0, rwkv_a_up, rwkv_g_up,
                 rwkv_k_k, rwkv_k_a, rwkv_r_k, rwkv_gn_w, rwkv_gn_b,
                 ffn_w_in, ffn_w_out):
    h = rms_norm(x, n_mix_pre)
    u = jnp.einsum('bsd,dp->bsp', h, w_in).astype(jnp.float32)
    u_a, u_b, u_c, u_d = jnp.split(u, [A_COLS, A_COLS + B_COLS, A_COLS + B_COLS + C_COLS], axis=-1)
    y = jnp.concatenate([
        rglru_mixer(u_a, lru_conv_w, lru_conv_b, lru_gate_w, lru_gate_b, lru_lambda),
        gdn_mixer(u_b, gdn_conv_w, gdn_a_log, gdn_dt_bias, gdn_norm),
        mlstm_mixer(u_c, mlstm_gate_bias, mlstm_norm),
        rwkv_mixer(u_d, rwkv_mu, rwkv_w0, rwkv_w_up, rwkv_a0, rwkv_a_up, rwkv_g_up,
                   rwkv_k_k, rwkv_k_a, rwkv_r_k, rwkv_gn_w, rwkv_gn_b),
    ], axis=-1).astype(x.dtype)
    x = x + rms_norm(jnp.einsum('bsm,md->bsd', y, w_out), n_mix_post)
    h = rms_norm(x, n_ffn_pre)
    gate, up = jnp.split(jnp.einsum('bsd,df->bsf', h, ffn_w_in), 2, axis=-1)
    ff = jnp.einsum('bsf,fd->bsd', jax.nn.silu(gate) * up, ffn_w_out)
    return x + rms_norm(ff, n_ffn_post)


def setup_inputs(seed: int = 0) -> dict:
    key = jax.random.key(seed)
    ks = iter(jax.random.split(key, 40))
    L = DEPTH

    def nrm(shape, scale):
        return scale * jax.random.normal(next(ks), shape, jnp.float32)

    def uni(shape, lo, hi):
        return jax.random.uniform(next(ks), shape, jnp.float32, lo, hi)

    def gain(shape):
        return 1.0 + nrm(shape, 0.02)

    x = nrm((BATCH, SEQ, D_MODEL), 1.0)
    a_init = uni((L, N_DIR, G_WIDTH), 0.9, 0.999)
    dt = jnp.exp(uni((L, N_DIR, G_HEADS), math.log(1e-3), math.log(1e-1)))
    mlstm_gate_bias = jnp.stack([nrm((L, N_DIR, G_HEADS), 0.1),
                                 uni((L, N_DIR, G_HEADS), 3.0, 6.0)], axis=2)
    bw = G_WIDTH // LRU_BLOCKS
    return {
        'x': x,
        'norm_mix_pre': gain((L, D_MODEL)),
        'norm_mix_post': gain((L, D_MODEL)),
        'norm_ffn_pre': gain((L, D_MODEL)),
        'norm_ffn_post': gain((L, D_MODEL)),
        'w_in': nrm((L, D_MODEL, P_IN), D_MODEL ** -0.5),
        'w_out': nrm((L, D_MIX, D_MODEL), D_MIX ** -0.5),
        'lru_conv_w': nrm((L, CONV_K, G_WIDTH), CONV_K ** -0.5),
        'lru_conv_b': nrm((L, G_WIDTH), 0.02),
        'lru_gate_w': nrm((L, N_DIR, 2, LRU_BLOCKS, bw, bw), bw ** -0.5),
        'lru_gate_b': nrm((L, N_DIR, 2, G_WIDTH), 0.02),
        'lru_lambda': jnp.log(a_init) - jnp.log1p(-a_init),
        'gdn_conv_w': nrm((L, CONV_K, 3 * G_WIDTH), CONV_K ** -0.5),
        'gdn_a_log': jnp.log(uni((L, N_DIR, G_HEADS), 1.0, 16.0)),
        'gdn_dt_bias': dt + jnp.log(-jnp.expm1(-dt)),
        'gdn_norm': gain((L, HEAD_DIM)),
        'mlstm_gate_bias': mlstm_gate_bias,
        'mlstm_norm': gain((L, G_WIDTH)),
        'rwkv_mu': uni((L, 2, D_COLS), 0.0, 0.5),
        'rwkv_w0': uni((L, N_DIR, G_WIDTH), -6.0, 1.0),
        'rwkv_w_up': nrm((L, N_DIR, RWKV_W_RANK, G_WIDTH), 0.1 * RWKV_W_RANK ** -0.5),
        'rwkv_a0': nrm((L, N_DIR, G_WIDTH), 0.1),
        'rwkv_a_up': nrm((L, N_DIR, RWKV_A_RANK, G_WIDTH), 0.1 * RWKV_A_RANK ** -0.5),
        'rwkv_g_up': nrm((L, RWKV_G_RANK, G_WIDTH), RWKV_G_RANK ** -0.5),
        'rwkv_k_k': 0.85 + nrm((L, G_WIDTH), 0.02),
        'rwkv_k_a': gain((L, G_WIDTH)),
        'rwkv_r_k': nrm((L, G_HEADS, HEAD_DIM), 0.1),
        'rwkv_gn_w': gain((L, G_WIDTH)),
        'rwkv_gn_b': nrm((L, G_WIDTH), 0.02),
        'ffn_w_in': nrm((L, D_MODEL, 2 * D_FF), D_MODEL ** -0.5),
        'ffn_w_out': nrm((L, D_FF, D_MODEL), D_FF ** -0.5),
    }


def reference(x, norm_mix_pre, norm_mix_post, norm_ffn_pre, norm_ffn_post, w_in, w_out,
              lru_conv_w, lru_conv_b, lru_gate_w, lru_gate_b, lru_lambda,
              gdn_conv_w, gdn_a_log, gdn_dt_bias, gdn_norm,
              mlstm_gate_bias, mlstm_norm,
              rwkv_mu, rwkv_w0, rwkv_w_up, rwkv_a0, rwkv_a_up, rwkv_g_up,
              rwkv_k_k, rwkv_k_a, rwkv_r_k, rwkv_gn_w, rwkv_gn_b,
              ffn_w_in, ffn_w_out):
    for l in range(DEPTH):
        x = hybrid_layer(
            x, norm_mix_pre[l], norm_mix_post[l], norm_ffn_pre[l], norm_ffn_post[l], w_in[l], w_out[l],
            lru_conv_w[l], lru_conv_b[l], lru_gate_w[l], lru_gate_b[l], lru_lambda[l],
            gdn_conv_w[l], gdn_a_log[l], gdn_dt_bias[l], gdn_norm[l],
            mlstm_gate_bias[l], mlstm_norm[l],
            rwkv_mu[l], rwkv_w0[l], rwkv_w_up[l], rwkv_a0[l], rwkv_a_up[l], rwkv_g_up[l],
            rwkv_k_k[l], rwkv_k_a[l], rwkv_r_k[l], rwkv_gn_w[l], rwkv_gn_b[l],
            ffn_w_in[l], ffn_w_out[l])
    return x
```

```python
import os
import numpy as np
from contextlib import ExitStack
import concourse.bass as bass
import concourse.mybir as mybir
from concourse.bass_utils import run_bass_kernel_spmd

F32 = mybir.dt.float32
BF16 = mybir.dt.bfloat16
AF = mybir.ActivationFunctionType
ALU = mybir.AluOpType

ENGS = ("pe", "act", "dve", "pool", "sp")
ENGATTR = {"pe": "tensor", "act": "scalar", "dve": "vector", "pool": "gpsimd", "sp": "sync"}


class _Op:
    __slots__ = ("eng", "fn", "waits", "signal", "dma_key", "dwaits")

    def __init__(self, eng, fn):
        self.eng = eng
        self.fn = fn
        self.waits = []
        self.dwaits = []
        self.signal = False
        self.dma_key = None


class Sched:
    def __init__(self):
        self.q = {e: [] for e in ENGS}
        self.last_w = {}
        self.readers = {}
        self.waited = {e: {p: -1 for p in ENGS} for e in ENGS}
        self.dwaited = {e: {} for e in ENGS}
        self.dma_cnt = {}
        self.lane = None
        self.local = set()

    def op(self, eng, fn, reads=(), writes=(), dma_key=None):
        if self.lane is not None:
            sfx = "@%d" % self.lane
            reads = [r + sfx if r in self.local else r for r in reads]
            writes = [w + sfx if w in self.local else w for w in writes]
        o = _Op(eng, fn)
        idx = len(self.q[eng])
        pr = [r for r in reads if r.startswith(("bank", "psb", "pf"))]
        if pr:
            writes = list(writes) + [r for r in pr if r not in writes]
        deps = []
        raw_same = -1
        for r in reads:
            lw = self.last_w.get(r)
            if lw is not None:
                deps.append(lw)
                if lw[0] == eng and lw[1] > raw_same:
                    raw_same = lw[1]
        for w in writes:
            lw = self.last_w.get(w)
            if lw is not None:
                deps.append(lw)
            rd = self.readers.get(w)
            if rd:
                deps.extend(rd.values())
        for d in deps:
            if d[0] == "dma":
                key = d[1]
                cnt = self.dma_cnt[key]
                if self.dwaited[eng].get(key, 0) < cnt:
                    self.dwaited[eng][key] = cnt
                    o.dwaits = [x for x in o.dwaits if x[0] != key]
                    o.dwaits.append((key, cnt))
            else:
                pe, pi = d
                if self.waited[eng][pe] < pi:
                    self.waited[eng][pe] = pi
                    o.waits = [x for x in o.waits if x[0] != pe]
                    o.waits.append((pe, pi))
        if dma_key is not None:
            self.dma_cnt[dma_key] = self.dma_cnt.get(dma_key, 0) + 16
            o.dma_key = dma_key
            me = ("dma", dma_key)
            mk = ("dma", dma_key)
        else:
            me = (eng, idx)
            mk = eng
        for r in reads:
            self.readers.setdefault(r, {})[mk] = me
        for w in writes:
            self.last_w[w] = me
            self.readers[w] = {}
        self.q[eng].append(o)
        return o

    def barrier(self):
        lasts = {}
        for p in ENGS:
            for i in range(len(self.q[p]) - 1, -1, -1):
                if self.q[p][i].dma_key is None and self.q[p][i].fn is not None:
                    lasts[p] = i
                    break
        for e in ENGS:
            o = _Op(e, None)
            for p, i in lasts.items():
                if p != e and self.waited[e][p] < i:
                    self.waited[e][p] = i
                    o.waits.append((p, i))
            for k, c in self.dma_cnt.items():
                if self.dwaited[e].get(k, 0) < c:
                    self.dwaited[e][k] = c
                    o.dwaits.append((k, c))
            self.q[e].append(o)

    def emit(self, nc, final_wait_keys=()):
        for e in ENGS:
            for o in self.q[e]:
                for (pe, pi) in o.waits:
                    self.q[pe][pi].signal = True
        cum = {}
        for e in ENGS:
            c = 0
            arr = []
            for o in self.q[e]:
                if o.signal and o.dma_key is None:
                    c += 1
                arr.append(c)
            cum[e] = arr
        with ExitStack() as es:
            esem = {e: es.enter_context(nc.semaphore("s_" + e)) for e in ENGS}
            dsem = {k: es.enter_context(nc.semaphore("d_" + str(k))) for k in self.dma_cnt}
            with nc.Block() as blk0:
                @blk0.sync
                def _(eng):
                    for s_ in list(esem.values()) + list(dsem.values()):
                        eng.sem_clear(s_)
            block = es.enter_context(nc.Block())

            def body(e):
                def _f(eng):
                    for o in self.q[e]:
                        for (pe, pi) in o.waits:
                            eng.wait_ge(esem[pe], cum[pe][pi])
                        for (k, cnt) in o.dwaits:
                            eng.wait_ge(dsem[k], cnt)
                        if o.fn is None:
                            continue
                        ins = o.fn(eng)
                        if o.dma_key is not None:
                            ins.then_inc(dsem[o.dma_key], 16)
                        elif o.signal:
                            ins.then_inc(esem[e], 1)
                    if e == "sp":
                        for k in final_wait_keys:
                            if k in dsem:
                                eng.wait_ge(dsem[k], self.dma_cnt[k])
                return _f

            for e in ENGS:
                getattr(block, ENGATTR[e])(body(e))


D = 1024
PIN = 3552
DFF = 2816
A0, B0, C0, D0 = 0, 512, 1552, 2592
NEGV = -30000.0

(C_ID, C_UT, C_LT, C_NINC_F, C_NSTR_F, C_NINC_B, C_NSTR_B, C_J, C_ONES, C_INC_F, C_STR_F, C_INC_B,
 C_STR_B) = range(13)
C_DNC_F = 13
C_DNC_B = 20
NCONST = 27


def host_consts():
    p = np.arange(128)[:, None]
    j = np.arange(128)[None, :]
    c = np.zeros((NCONST, 128, 128), np.float32)
    c[C_ID] = (p == j)
    c[C_UT] = (p <= j)
    c[C_LT] = (p >= j)
    c[C_NINC_F] = np.where(p <= j, 0.0, NEGV)
    c[C_NSTR_F] = np.where(p < j, 0.0, NEGV)
    c[C_NINC_B] = np.where(p >= j, 0.0, NEGV)
    c[C_NSTR_B] = np.where(p > j, 0.0, NEGV)
    c[C_J] = (p + j == 127)
    c[C_ONES] = 1.0
    c[C_INC_F] = (p <= j)
    c[C_STR_F] = (p < j)
    c[C_INC_B] = (p >= j)
    c[C_STR_B] = (p > j)
    for l in range(7):
        b = 1 << l
        same = (p // (2 * b)) == (j // (2 * b))
        c[C_DNC_F + l] = ((p == j) | (same & ((p % (2 * b)) >= b) & ((j % (2 * b)) < b)))
        c[C_DNC_B + l] = ((p == j) | (same & ((p % (2 * b)) < b) & ((j % (2 * b)) >= b)))
    return np.ascontiguousarray(c.transpose(1, 0, 2).reshape(128, NCONST * 128))


ROW_ITEMS = [("n_mix_post", 1024), ("n_ffn_post", 1024), ("lru_cw", 4 * 256), ("gdn_cw", 4 * 768),
             ("mu", 2 * 960), ("gdn_alog", 8), ("gdn_dtb", 8), ("gdn_norm", 64), ("ml_gb", 16),
             ("ml_norm", 256), ("rw_w0", 512), ("rw_a0", 512), ("rw_kk", 256), ("rw_ka", 256),
             ("rw_rk", 256), ("rw_gnw", 256), ("rw_gnb", 256)]
ROW_OFF = {}
_o = 0
for _n, _s in ROW_ITEMS:
    ROW_OFF[_n] = _o
    _o += _s
NROW = _o
ROW_BASE = 3072
ROWM = {k: v - ROW_BASE for k, v in ROW_OFF.items()}
COL_ITEMS = [("n_mix_pre", 8), ("n_ffn_pre", 8), ("lru_cb", 2), ("lru_gb", 8), ("lru_lam", 4)]
COL_OFF = {}
_o = 0
for _n, _s in COL_ITEMS:
    COL_OFF[_n] = _o
    _o += _s
NCOL = _o


def pack_layer_params(inp, l):
    row = np.concatenate([
        inp["norm_mix_post"][l], inp["norm_ffn_post"][l], inp["lru_conv_w"][l].reshape(-1),
        inp["gdn_conv_w"][l].reshape(-1), inp["rwkv_mu"][l].reshape(-1), inp["gdn_a_log"][l].reshape(-1),
        inp["gdn_dt_bias"][l].reshape(-1), inp["gdn_norm"][l],
        np.concatenate([inp["mlstm_gate_bias"][l][:, 0, :].reshape(-1), inp["mlstm_gate_bias"][l][:, 1, :].reshape(-1)]),
        inp["mlstm_norm"][l], inp["rwkv_w0"][l].reshape(-1), inp["rwkv_a0"][l].reshape(-1), inp["rwkv_k_k"][l],
        inp["rwkv_k_a"][l], inp["rwkv_r_k"][l].reshape(-1), inp["rwkv_gn_w"][l], inp["rwkv_gn_b"][l]]).astype(np.float32)
    assert row.shape[0] == NROW
    col = np.concatenate([
        inp["norm_mix_pre"][l].reshape(8, 128).T, inp["norm_ffn_pre"][l].reshape(8, 128).T,
        inp["lru_conv_b"][l].reshape(2, 128).T,
        inp["lru_gate_b"][l].reshape(4, 2, 128).transpose(2, 0, 1).reshape(128, 8),
        inp["lru_lambda"][l].reshape(2, 2, 128).transpose(2, 0, 1).reshape(128, 4)], axis=1).astype(np.float32)
    assert col.shape == (128, NCOL)
    gw = inp["lru_gate_w"][l]
    bd = np.zeros((2, 2, 2, 128, 128), np.float32)
    for d in range(2):
        for g in range(2):
            for n in range(4):
                ct, o = n // 2, (n % 2) * 64
                bd[d, g, ct, o:o + 64, o:o + 64] = gw[d, g, n]
    lru_gw = np.ascontiguousarray(bd.transpose(3, 0, 1, 2, 4).reshape(128, 8 * 128))
    rw_up = np.concatenate([inp["rwkv_w_up"][l].reshape(64, 256), inp["rwkv_a_up"][l].reshape(64, 256),
                            inp["rwkv_g_up"][l]], axis=1).astype(np.float32)
    return row[None, :], col, lru_gw, rw_up


class KB:
    def __init__(self, NT, NL, dbg=False):
        self.NT, self.NL, self.dbg = NT, NL, dbg
        self.SQ = NT * 128
        self.nc = bass.Bass("TRN2", target_bir_lowering=False)
        self.S = Sched()
        self.cnt = {}

    def mm(self, out, lhsT, rhs, start, stop, r, w):
        self.S.op("pe", lambda e: e.matmul(out, lhsT=lhsT, rhs=rhs, start=start, stop=stop), reads=r, writes=w)

    def tr(self, out, in_, ident, r, w):
        self.S.op("pe", lambda e: e.transpose(out=out, in_=in_, identity=ident), reads=list(r) + ["consts"], writes=w)

    def act(self, out, in_, func, r, w, bias=None, scale=None, accum=None):
        kw = {}
        if bias is not None:
            kw["bias"] = bias
        if scale is not None:
            kw["scale"] = scale
        if accum is not None:
            kw["accum_out"] = accum
        self.S.op("act", lambda e: e.activation(out=out, in_=in_, func=func, **kw), reads=r, writes=w)
        if accum is not None:
            self.S.op("act", lambda e: e.activation(out=accum, in_=accum, func=AF.Copy), reads=r, writes=w)

    def ts(self, eng, out, in0, s1, s2, op0, op1, r, w):
        if op1 is None:
            self.S.op(eng, lambda e: e.tensor_scalar(out=out, in0=in0, scalar1=s1, scalar2=None, op0=op0), reads=r, writes=w)
        else:
            self.S.op(eng, lambda e: e.tensor_scalar(out=out, in0=in0, scalar1=s1, scalar2=s2, op0=op0, op1=op1), reads=r, writes=w)

    def tt(self, eng, out, in0, in1, op, r, w):
        self.S.op(eng, lambda e: e.tensor_tensor(out=out, in0=in0, in1=in1, op=op), reads=r, writes=w)

    def stt(self, out, in0, scalar, in1, op0, op1, r, w):
        self.S.op("dve", lambda e: e.scalar_tensor_tensor(out=out, in0=in0, scalar=scalar, in1=in1, op0=op0, op1=op1), reads=r, writes=w)

    def cp(self, eng, out, in_, r, w):
        if eng == "act":
            self.S.op("act", lambda e: e.activation(out=out, in_=in_, func=AF.Copy), reads=r, writes=w)
        else:
            self.S.op(eng, lambda e: e.tensor_copy(out=out, in_=in_), reads=r, writes=w)

    def dma(self, out, in_, r, w, key, eng="sp"):
        self.S.op(eng, lambda e: e.dma_start(out=out, in_=in_), reads=r, writes=w, dma_key=key)

    def recip(self, out, in_, r, w):
        self.S.op("dve", lambda e: e.reciprocal(out=out, in_=in_), reads=r, writes=w)

    def rot(self, tag, n):
        i = self.cnt.get(tag, 0)
        self.cnt[tag] = i + 1
        return i % n

    def psb(self):
        i = self.rot("psb", 2)
        return self.ps[i], "psb%d" % i

    def psf(self):
        i = self.rot("psf", 4)
        return self.ps[i], ("psb%d" % i if i < 2 else "bank%d" % i)

    def psh(self):
        i = self.rot("psh", 4)
        b, hlf = 2 + i % 2, (i // 2) % 2
        return self.ps[b][:, hlf * 256:hlf * 256 + 256], "bank%d" % b

    def psq(self):
        if self.S.lane is not None:
            b = 4 + self.S.lane % 4
            q = self.rot("psq_l%d" % self.S.lane, 4)
            return self.ps[b][:, q * 128:q * 128 + 128], "bank%d" % b
        i = self.rot("psq", 16)
        b, q = 4 + i % 4, (i // 4) % 4
        return self.ps[b][:, q * 128:q * 128 + 128], "bank%d" % b

    def cm(self, idx, np_=128):
        return self.consts[0:np_, idx * 128:(idx + 1) * 128]

    def build(self):
        nc, NT, NL, SQ = self.nc, self.NT, self.NL, self.SQ
        dt = nc.dram_tensor
        self.x_d = dt("x", [SQ, D], F32, kind="ExternalInput").ap()
        self.out_d = dt("out", [SQ, D], F32, kind="ExternalOutput").ap()
        self.win_d = dt("w_in", [NL, D, PIN], F32, kind="ExternalInput").ap()
        self.wout_d = dt("w_out", [NL, D, D], F32, kind="ExternalInput").ap()
        self.fwi_d = dt("ffn_w_in", [NL, D, 2 * DFF], F32, kind="ExternalInput").ap()
        self.fwo_d = dt("ffn_w_out", [NL, DFF, D], F32, kind="ExternalInput").ap()
        self.row_d = dt("rowp", [NL, 1, NROW], F32, kind="ExternalInput").ap()
        self.col_d = dt("colp", [NL, 128, NCOL], F32, kind="ExternalInput").ap()
        self.lgw_d = dt("lru_gw", [NL, 128, 1024], F32, kind="ExternalInput").ap()
        self.rup_d = dt("rw_up", [NL, 64, 768], F32, kind="ExternalInput").ap()
        self.cst_d = dt("consts", [128, NCONST * 128], F32, kind="ExternalInput").ap()
        self.yT_d = dt("yT_scr", [D, SQ], BF16, kind=("ExternalOutput" if self.dbg else "Internal")).ap()
        self.rkv_d = [dt("rkv_scr%d" % i, [SQ, 192], BF16, kind="Internal").ap() for i in range(2)]
        self.lrT_d = dt("lrT_scr", [3, 64, SQ], BF16, kind="Internal").ap()
        if self.dbg:
            self.ydbg_d = dt("ydbg", [D, SQ], BF16, kind="ExternalOutput").ap()
        with ExitStack() as es:
            self.es = es
            sb = lambda n, s, d=F32: es.enter_context(nc.sbuf_tensor(n, s, d))
            self.ps = [es.enter_context(nc.psum_tensor("ps%d" % i, [128, 512], F32)) for i in range(8)]
            self.idb = sb("idb", [128, 128], BF16)
            self.nidb = sb("nidb", [128, 128], BF16)
            idf = sb("idf", [128, 128])
            self.dma(idf[:], self.cst_d[:, C_ID * 128:(C_ID + 1) * 128], [], ["idf"], "cst0")
            self.cp("dve", self.idb[:], idf[:], ["idf"], ["consts2"])
            self.ts("dve", self.nidb[:], idf[:], -1.0, None, ALU.mult, None, ["idf"], ["consts2"])
            for l in range(NL):
                self.layer(l)
            self.S.emit(nc, final_wait_keys=["xout0", "xout1"])
        return nc

    def layer(self, l):
        nc, NT, SQ = self.nc, self.NT, self.SQ
        src = self.x_d if l == 0 else self.out_d
        with ExitStack() as ls:
            sb = lambda n, s, d=F32: ls.enter_context(nc.sbuf_tensor("L%d_%s" % (l, n), s, d))
            self.colp = sb("colp", [128, NCOL])
            self.dma(self.colp[:], self.col_d[l, :, :], [], ["colp"], "par")
            with ExitStack() as ms:
                msb = lambda n, s, d=F32: ms.enter_context(nc.sbuf_tensor("L%d_%s" % (l, n), s, d))
                self.msb = msb
                self.consts = msb("consts", [128, NCONST * 128])
                self.dma(self.consts[:], self.cst_d[:, :], [], ["consts"], "cst")
                self.hT = msb("hT", [128, 8, SQ + 4], BF16)
                self.alloc_common(msb)
                import os
                if not os.environ.get("SKIP_A"):
                    self.phase_a(l, src, msb)
                    self.S.barrier()
                skip = os.environ.get("SKIP_MIX", "")
                if "lru" not in skip:
                    self.mix_lru(l)
                self.S.barrier()
                self.alloc_common2(msb)
                self.rowp = msb("rowp", [128, NROW - ROW_BASE])
                self.dma(self.rowp[:], self.row_d[l, 0, ROW_BASE:NROW].partition_broadcast(128), [], ["rowp"], "par")
                for nm_, fn in (("gdn", self.mix_gdn), ("mlstm", self.mix_mlstm), ("rwkv", self.mix_rwkv)):
                    if nm_ in skip:
                        continue
                    fn(l)
                    self.S.barrier()
            self.S.barrier()
            if "cd" not in os.environ.get("SKIP_MIX", ""):
                self.phase_cd(l, src)
            self.S.barrier()

    def phase_a(self, l, src, msb):
        nc, NT = self.nc, self.NT
        with ExitStack() as ps_:
            sb = lambda n, s, d=F32: ps_.enter_context(nc.sbuf_tensor("A%d_%s" % (l, n), s, d))
            xa = [sb("xa%d" % i, [128, D]) for i in range(2)]
            junk = sb("junk", [128, D])
            xn = [sb("xn%d" % i, [128, D], BF16) for i in range(2)]
            st = sb("st", [128, 4 * NT])
            hT = self.hT
            self.S.op("pool", lambda e: e.memset(hT[:, :, 0:2], 0.0), writes=["hT"])
            self.S.op("pool", lambda e: e.memset(hT[:, :, self.SQ + 2:self.SQ + 4], 0.0), writes=["hT"])
            gcol = self.colp[:, COL_OFF["n_mix_pre"]:COL_OFF["n_mix_pre"] + 8]
            for i in range(NT):
                s = i % 2
                self.dma(xa[s][:], src[i * 128:(i + 1) * 128, :], ["xres%d" % i], ["xa%d" % s], "xa%d" % s)
                ss, ms_, rs = st[:, 4 * i:4 * i + 1], st[:, 4 * i + 1:4 * i + 2], st[:, 4 * i + 2:4 * i + 3]
                self.act(junk[:], xa[s][:], AF.Square, ["xa%d" % s], ["junk"])
                self.S.op("dve", (lambda ss_: lambda e: e.tensor_reduce(out=ss_, in_=junk[:], axis=mybir.AxisListType.X, op=ALU.add))(ss), reads=["junk"], writes=["st"])
                self.ts("dve", ms_, ss, 1.0 / D, 1e-6, ALU.mult, ALU.add, ["st"], ["st"])
                self.act(ms_, ms_, AF.Sqrt, ["st"], ["st"])
                self.recip(rs, ms_, ["st"], ["st"])
                self.ts("dve", xn[s][:], xa[s][:], rs, None, ALU.mult, None, ["xa%d" % s, "st"], ["xn%d" % s])
                pb, pn = self.psb()
                pbb = pb[:].bitcast(BF16)
                for k in range(8):
                    self.tr(pbb[:, k * 128:(k + 1) * 128], xn[s][:, k * 128:(k + 1) * 128], self.idb[:], ["xn%d" % s, "consts2"], [pn])
                self.tt("dve", hT[:, :, 2 + i * 128:2 + (i + 1) * 128], pbb[:, 0:1024].rearrange("p (k t) -> p k t", k=8),
                        gcol.unsqueeze(2).to_broadcast([128, 8, 128]), ALU.mult, [pn, "colp"], ["hT"])

    def load_w(self, dst, dst_name, src_ap, ncols, scale_row=None, kch=8):
        stg_i = self.rot("wstg", 2)
        stg = self.wstg[stg_i]
        nm = "wstg%d" % stg_i
        self.dma(stg[:, 0:kch, 0:ncols], src_ap.rearrange("(k p) c -> p k c", p=128), [], [nm], nm)
        if scale_row is None:
            self.cp("pool", dst, stg[:, 0:kch, 0:ncols], [nm], [dst_name])
        else:
            self.tt("pool", dst, stg[:, 0:kch, 0:ncols], scale_row.unsqueeze(1).to_broadcast([128, kch, ncols]),
                    ALU.mult, [nm, "rowp", "mus"], [dst_name])

    def u_tm(self, ps_out, pn, i, taps, wname, ncols):
        n = len(taps) * 8
        c = 0
        for (sh, wt) in taps:
            for k in range(8):
                self.mm(ps_out, self.hT[:, k, 2 + i * 128 + sh:2 + i * 128 + sh + 128], wt[:, k, 0:ncols],
                        c == 0, c == n - 1, ["hT", wname], [pn])
                c += 1

    def expmat(self, out, on, gl, rev, strict, bias, r, dl=None, rows=128):
        rn = self.rot("erhs", 6)
        rhs1 = self.erhs[rn]
        rnm = "erhs%d" % rn
        tri = self.cm(C_LT if rev else C_UT)
        self.ts("dve", rhs1[:], tri, gl, None, ALU.mult, None, list(r) + ["consts"], [rnm])
        if dl is not None:
            self.stt(rhs1[:], self.cm(C_ID), dl, rhs1[:], ALU.mult, ALU.add, list(r) + ["consts", rnm], [rnm])
        pq, pn = self.psq()
        neg = self.cm({(0, 0): C_NINC_F, (0, 1): C_NSTR_F, (1, 0): C_NINC_B, (1, 1): C_NSTR_B}[(int(rev), int(strict))])
        self.mm(pq[0:rows, :], self.cm(C_ONES)[:, 0:rows], rhs1[:], True, False, [rnm, "consts"], [pn])
        self.mm(pq[0:rows, :], self.cm(C_ID)[:, 0:rows], neg, False, True, ["consts"], [pn])
        self.act(out, pq[0:rows, :], AF.Exp, [pn] + list(r), [on], bias=bias)

    def bcast_ps(self, gl, rev, r, dl=None):
        rn = self.rot("erhs", 6)
        rhs1 = self.erhs[rn]
        rnm = "erhs%d" % rn
        tri = self.cm(C_LT if rev else C_UT)
        self.ts("dve", rhs1[:], tri, gl, None, ALU.mult, None, list(r) + ["consts"], [rnm])
        if dl is not None:
            self.stt(rhs1[:], self.cm(C_ID), dl, rhs1[:], ALU.mult, ALU.add, list(r) + ["consts", rnm], [rnm])
        pq, pn = self.psq()
        self.mm(pq, self.cm(C_ONES), rhs1[:], True, True, [rnm, "consts"], [pn])
        return pq, pn

    def exp_from_bc(self, out, on, pq, pn, bias, clamp, rev, strict, r):
        self.ts("dve", out, pq, bias, clamp, ALU.add, ALU.min, [pn] + list(r), [on])
        self.act(out, out, AF.Exp, [on], [on])
        mk = self.cm({(0, 0): C_INC_F, (0, 1): C_STR_F, (1, 0): C_INC_B, (1, 1): C_STR_B}[(int(rev), int(strict))])
        self.tt("pool", out, out, mk, ALU.mult, [on, "consts"], [on])

    def rowexp(self, out, on, gl, rev, r, dl=None, rows=64):
        rn = self.rot("erhs", 6)
        rhs1 = self.erhs[rn]
        rnm = "erhs%d" % rn
        tri = self.cm(C_LT if rev else C_UT)
        self.ts("dve", rhs1[:], tri, gl, None, ALU.mult, None, list(r) + ["consts"], [rnm])
        if dl is not None:
            self.stt(rhs1[:], self.cm(C_ID), dl, rhs1[:], ALU.mult, ALU.add, list(r) + ["consts", rnm], [rnm])
        pq, pn = self.psq()
        self.mm(pq[0:rows, :], self.cm(C_ONES)[:, 0:rows], rhs1[:], True, True, [rnm, "consts"], [pn])
        self.act(out, pq[0:rows, :], AF.Exp, [pn], [on])

    def pipeline(self, units, nlanes, delay):
        active = []
        nxt, rnd = 0, 0
        busy = set()
        while nxt < len(units) or active:
            if nxt < len(units) and rnd % delay == 0 and (nxt % nlanes) not in busy:
                lane = nxt % nlanes
                active.append((lane, units[nxt](lane)))
                busy.add(lane)
                nxt += 1
            for item in list(active):
                lane, g = item
                self.S.lane = lane
                try:
                    next(g)
                except StopIteration:
                    active.remove(item)
                    busy.discard(lane)
            self.S.lane = None
            rnd += 1

    def dnc_gen(self, Bm, bn, rev):
        lane = self.S.lane or 0
        X, W = self.idb, self.idb
        xn, wn = "consts2", "consts2"
        for lv in range(7):
            pq, pn = self.psq()
            self.mm(pq, Bm, X[:], True, True, [bn, xn], [pn])
            Q, qn = self.ldq[lane][lv % 2], "dq%d" % (lv % 2)
            msk = self.cm((C_DNC_B if rev else C_DNC_F) + lv)
            self.stt(Q[:], pq, -1.0, msk, ALU.mult, ALU.mult, [pn, "consts"], [qn])
            yield
            if lv < 6:
                Xn, xnn = self.ldx[lane][lv % 2], "dx%d" % (lv % 2)
                if lv == 0:
                    self.tt("pool", Xn[:], Q[:], self.idb[:], ALU.add, [qn, "consts2"], [xnn])
                else:
                    pq2, pn2 = self.psq()
                    self.mm(pq2, W[:], Q[:], True, True, [wn, qn], [pn2])
                    self.tt("dve", Xn[:], pq2, X[:], ALU.add, [pn2, xn], [xnn])
            pq3, pn3 = self.psq()
            self.mm(pq3, Q[:], W[:], True, True, [wn, qn], [pn3])
            Wn, wnn = self.ldw[lane][lv % 2], "dw%d" % (lv % 2)
            self.tt("dve", Wn[:], pq3, W[:], ALU.add, [pn3, wn], [wnn])
            W, wn = Wn, wnn
            if lv < 6:
                X, xn = Xn, xnn
            yield
        return W, wn

    def dnc(self, Bm, bn, rev):
        X, W = self.idb, self.idb
        xn, wn = "consts2", "consts2"
        for lv in range(7):
            pq, pn = self.psq()
            self.mm(pq, Bm, X[:], True, False, [bn, xn], [pn])
            self.mm(pq, self.nidb[:], self.idb[:], False, True, ["consts2"], [pn])
            qi = self.rot("dq", 2)
            Q, qn = self.dq[qi], "dq%d" % qi
            msk = self.cm((C_DNC_B if rev else C_DNC_F) + lv)
            self.stt(Q[:], pq, -1.0, msk, ALU.mult, ALU.mult, [pn, "consts"], [qn])
            if lv < 6:
                pq2, pn2 = self.psq()
                self.mm(pq2, W[:], Q[:], True, True, [wn, qn], [pn2])
                xi = self.rot("dx", 2)
                Xn, xnn = self.dx[xi], "dx%d" % xi
                self.cp("act", Xn[:], pq2, [pn2], [xnn])
            pq3, pn3 = self.psq()
            self.mm(pq3, Q[:], W[:], True, True, [wn, qn], [pn3])
            wi = self.rot("dw", 2)
            Wn, wnn = self.dw[wi], "dw%d" % wi
            self.cp("dve" if lv % 2 else "act", Wn[:], pq3, [pn3], [wnn])
            W, wn = Wn, wnn
            if lv < 6:
                X, xn = Xn, xnn
        return W, wn

    def alloc_common(self, msb):
        self.wstg = [msb("wstg%d" % i, [128, 8, 192]) for i in range(2)]

    def alloc_common2(self, msb):
        self.erhs = [msb("erhs%d" % i, [128, 128]) for i in range(6)]
        self.dq = [msb("dq%d" % i, [128, 128], BF16) for i in range(2)]
        self.dx = [msb("dx%d" % i, [128, 128], BF16) for i in range(2)]
        self.dw = [msb("dw%d" % i, [128, 128], BF16) for i in range(2)]
        self.NLANE = 4
        self.ldq = [[msb("ldq%d_%d" % (a, i), [128, 128], BF16) for i in range(2)] for a in range(self.NLANE)]
        self.ldx = [[msb("ldx%d_%d" % (a, i), [128, 128], BF16) for i in range(2)] for a in range(self.NLANE)]
        self.ldw = [[msb("ldw%d_%d" % (a, i), [128, 128], BF16) for i in range(2)] for a in range(self.NLANE)]
        self.S.local.update(["dq0", "dq1", "dx0", "dx1", "dw0", "dw1"])

    def mix_lru(self, l):
        nc, NT, SQ = self.nc, self.NT, self.SQ
        TB = min(512, SQ)
        NB = SQ // TB
        with ExitStack() as xs:
            sb = lambda n, s, d=F32: xs.enter_context(nc.sbuf_tensor("U%d_%s" % (l, n), s, d))
            cwr = sb("cwr", [128, 1024])
            wx = [sb("wx%d" % j, [128, 8, 128], BF16) for j in range(4)]
            wy = sb("wy", [128, 8, 128], BF16)
            gstg = sb("gstg", [128, 1024]); gwb = sb("gwb", [128, 8, 128], BF16)
            cc = sb("cc", [128, 12])
            Xc = sb("Xc", [128, SQ]); A1 = sb("A1", [128, SQ]); B1 = sb("B1", [128, SQ]); Hf = sb("Hf", [128, SQ]); Rv = sb("Rv", [128, SQ])
            xbb = [sb("xbb%d" % i, [128, TB], BF16) for i in range(2)]
            rr = sb("rr", [128, TB]); ii = sb("ii", [128, TB]); t5 = sb("t5", [128, TB])
            tT = [sb("tT%d" % i, [128, 128]) for i in range(2)]
            yst = Rv[:].bitcast(BF16)
            self.dma(cwr[:], self.row_d[l, 0, ROW_OFF["lru_cw"]:ROW_OFF["lru_cw"] + 1024].partition_broadcast(128), [], ["cwr"], "par")
            self.dma(gstg[:], self.lgw_d[l, :, :], [], ["gstg"], "par")
            self.cp("pool", gwb[:].rearrange("p a b -> p (a b)"), gstg[:], ["gstg"], ["gwb"])
            lam = self.colp[:, COL_OFF["lru_lam"]:COL_OFF["lru_lam"] + 4]
            self.act(cc[:, 0:4], lam, AF.Exp, ["colp"], ["cc"], scale=-1.0)
            self.act(cc[:, 0:4], cc[:, 0:4], AF.Ln, ["cc"], ["cc"], bias=1.0)
            self.ts("dve", cc[:, 4:8], cc[:, 0:4], -8.0, None, ALU.mult, None, ["cc"], ["cc"])
            self.ts("dve", cc[:, 8:12], cc[:, 0:4], -16.0, None, ALU.mult, None, ["cc"], ["cc"])

            def reverse_into(dst, dn, srct, sn, add=False):
                for b in range(NT):
                    ti = self.rot("lrt", 2)
                    pq, pn = self.psq()
                    self.tr(pq, srct[:, b * 128:(b + 1) * 128], self.cm(C_ID), [sn], [pn])
                    self.cp("act", tT[ti][:], pq, [pn], ["tT%d" % ti])
                    pq2, pn2 = self.psq()
                    self.mm(pq2, tT[ti][:], self.cm(C_J), True, True, ["tT%d" % ti, "consts"], [pn2])
                    o = dst[:, (NT - 1 - b) * 128:(NT - b) * 128]
                    if add:
                        self.tt("dve", o, pq2, o, ALU.add, [pn2, dn], [dn])
                    else:
                        self.cp("dve", o, pq2, [pn2], [dn])

            for ct in range(2):
                for j in range(4):
                    self.load_w(wx[j][:], "lr_wx%d" % j, self.win_d[l, :, A0 + ct * 128:A0 + (ct + 1) * 128], 128,
                                scale_row=cwr[:, j * 256 + ct * 128:j * 256 + (ct + 1) * 128])
                self.load_w(wy[:], "lr_wy", self.win_d[l, :, A0 + 256 + ct * 128:A0 + 256 + (ct + 1) * 128], 128)
                cb = self.colp[:, COL_OFF["lru_cb"] + ct:COL_OFF["lru_cb"] + ct + 1]
                for nb in range(NB):
                    pf, pfn = self.psf()
                    c_ = 0
                    for j in range(4):
                        for k in range(8):
                            self.mm(pf[:, 0:TB], wx[j][:, k, :], self.hT[:, k, nb * TB + j:nb * TB + j + TB], c_ == 0, c_ == 31,
                                    ["hT", "lr_wx%d" % j, "cwr"], [pfn])
                            c_ += 1
                    self.act(Xc[:, nb * TB:(nb + 1) * TB], pf[:, 0:TB], AF.Identity, [pfn, "colp"], ["Xc"], bias=cb)
                for d in range(2):
                    if d == 1:
                        reverse_into(Rv, "Rv", Xc, "Xc")
                    xs_, xsn = (Xc, "Xc") if d == 0 else (Rv, "Rv")
                    gb_r = self.colp[:, COL_OFF["lru_gb"] + (d * 2 + 0) * 2 + ct:COL_OFF["lru_gb"] + (d * 2 + 0) * 2 + ct + 1]
                    gb_i = self.colp[:, COL_OFF["lru_gb"] + (d * 2 + 1) * 2 + ct:COL_OFF["lru_gb"] + (d * 2 + 1) * 2 + ct + 1]
                    c8 = cc[:, 4 + d * 2 + ct:4 + d * 2 + ct + 1]
                    c16 = cc[:, 8 + d * 2 + ct:8 + d * 2 + ct + 1]
                    for nb in range(NB):
                        blk = slice(nb * TB, (nb + 1) * TB)
                        xi = self.rot("xbb", 2)
                        self.cp("pool", xbb[xi][:], xs_[:, blk], [xsn], ["xbb%d" % xi])
                        pr, prn = self.psf()
                        self.mm(pr[:, 0:TB], gwb[:, (d * 2 + 0) * 2 + ct, :], xbb[xi][:], True, True, ["gwb", "xbb%d" % xi], [prn])
                        self.act(rr[:], pr[:, 0:TB], AF.Sigmoid, [prn, "colp"], ["rr"], bias=gb_r)
                        pi_, pin = self.psf()
                        self.mm(pi_[:, 0:TB], gwb[:, (d * 2 + 1) * 2 + ct, :], xbb[xi][:], True, True, ["gwb", "xbb%d" % xi], [pin])
                        self.act(ii[:], pi_[:, 0:TB], AF.Sigmoid, [pin, "colp"], ["ii"], bias=gb_i)
                        self.act(A1[:, blk], rr[:], AF.Exp, ["rr", "cc"], ["A1"], scale=c8)
                        self.act(t5[:], rr[:], AF.Exp, ["rr", "cc"], ["t5"], scale=c16)
                        self.act(t5[:], t5[:], AF.Sqrt, ["t5"], ["t5"], scale=-1.0, bias=1.0)
                        self.tt("dve", t5[:], t5[:], ii[:], ALU.mult, ["t5", "ii"], ["t5"])
                        self.tt("dve", B1[:, blk], t5[:], xs_[:, blk], ALU.mult, ["t5", xsn], ["B1"])
                    dst, dn = (Hf, "Hf") if d == 0 else (Xc, "Xc")
                    self.S.op("dve", (lambda dst_: lambda e: e.tensor_tensor_scan(out=dst_[:], data0=A1[:], data1=B1[:], initial=0.0, op0=ALU.mult, op1=ALU.add))(dst),
                              reads=["A1", "B1"], writes=[dn])
                    if d == 1:
                        reverse_into(Hf, "Hf", Xc, "Xc", add=True)
                for nb in range(NB):
                    blk = slice(nb * TB, (nb + 1) * TB)
                    pf, pfn = self.psf()
                    for k in range(8):
                        self.mm(pf[:, 0:TB], wy[:, k, :], self.hT[:, k, 2 + nb * TB:2 + nb * TB + TB], k == 0, k == 7, ["hT", "lr_wy"], [pfn])
                    self.act(A1[:, blk], pf[:, 0:TB], AF.Gelu_apprx_tanh, [pfn], ["A1"])
                    self.tt("dve", yst[:, blk], Hf[:, blk], A1[:, blk], ALU.mult, ["Hf", "A1"], ["Rv"])
                self.dma(self.yT_d[ct * 128:(ct + 1) * 128, :], yst[:, 0:SQ], ["Rv"], ["yTd"], "ytw")

    def mix_gdn(self, l):
        nc, NT, SQ = self.nc, self.NT, self.SQ
        with ExitStack() as xs:
            sb = lambda n, s, d=F32: xs.enter_context(nc.sbuf_tensor("G%d_%s" % (l, n), s, d))
            wg = sb("wg", [128, 8, 16], BF16)
            wq4 = [sb("wq4_%d" % j, [128, 8, 192], BF16) for j in range(4)]
            wz = sb("wz", [128, 8, 64], BF16)
            g_all = sb("g", [128, NT, 8]); lnb_all = sb("lnb", [128, NT, 8]); beta_all = sb("beta", [128, NT, 8])
            gcts = [sb("gct%d" % i, [128, 8]) for i in range(3)]; ngc_all = sb("ngc", [128, NT, 8]); coef_all = sb("coef", [128, NT, 8])
            kdec_all = sb("kdec", [128, NT, 8]); egend_all = sb("egend", [128, NT, 8])
            nea = sb("nea", [128, 8]); t8s = [sb("t8_%d" % i, [128, 8]) for i in range(3)]; u8s = [sb("u8_%d" % i, [128, 8]) for i in range(3)]
            qkv_fs = [sb("qkv_f%d" % i, [128, 192]) for i in (0, 1)]; sqts = [sb("sqt%d" % i, [128, 128]) for i in (0, 1)]; sm = sb("sm", [128, 8]); smps = [sb("smp%d" % i, [128, 4]) for i in (0, 1)]
            qbs = [sb("qb%d" % i, [128, 64], BF16) for i in (0, 1)]
            k_tm = sb("k_tm", [128, NT, 64], BF16); v_tm = sb("v_tm", [128, NT, 64], BF16)
            qT = sb("qT", [64, SQ], BF16); kT = sb("kT", [64, SQ], BF16)
            sz = sb("sz", [128, NT, 64], BF16); oaccs = [sb("oacc0", [128, NT, 64]), sb("oacc1", [128, NT, 64], BF16)]
            Em = [sb("Em%d" % i, [128, 128]) for i in range(3)]
            Ebm = [sb("Ebm%d" % i, [128, 128]) for i in range(3)]
            Bm = [sb("Bm%d" % i, [128, 128], BF16) for i in range(3)]
            QKm = [sb("QKm%d" % i, [128, 128], BF16) for i in range(3)]
            ebr = [sb("ebr%d" % i, [128, 128]) for i in range(3)]
            qTs = [sb("qTs%d" % i, [64, 128], BF16) for i in range(3)]
            rv = [sb("rv%d" % i, [128, 64], BF16) for i in range(3)]
            rk = [sb("rk%d" % i, [128, 64], BF16) for i in range(3)]
            Kh = [sb("Kh%d" % i, [128, 64], BF16) for i in range(3)]
            nW = [sb("nW%d" % i, [64, 128], BF16) for i in range(3)]
            vn = [sb("vn%d" % i, [128, 64], BF16) for i in range(3)]
            P_fd = [sb("P_f%d" % i, [64, 64]) for i in (0, 1)]; P_bd = [sb("P_b%d" % i, [64, 64], BF16) for i in (0, 1)]
            sqs = [sb("sq%d" % i, [128, 64]) for i in range(3)]; ytms = [sb("ytm%d" % i, [128, 64], BF16) for i in range(3)]; fsms = [sb("fsm%d" % i, [128, 4]) for i in range(3)]
            yTs2 = [sb("yTs2_%d" % i, [64, 128], BF16) for i in range(3)]
            self.load_w(wg[:], "gd_wg", self.win_d[l, :, B0 + 1024:B0 + 1040], 16)
            dtb = self.rowp[:, ROWM["gdn_dtb"]:ROWM["gdn_dtb"] + 8]
            alog = self.rowp[:, ROWM["gdn_alog"]:ROWM["gdn_alog"] + 8]
            self.act(nea[:], alog, AF.Exp, ["rowp"], ["nea"])
            self.ts("dve", nea[:], nea[:], -1.0, None, ALU.mult, None, ["nea"], ["nea"])
            def gate(lane, i):
                t8, u8, gct = t8s[lane], u8s[lane], gcts[lane]
                pq, pn = self.psq()
                self.u_tm(pq[:, 0:16], pn, i, [(0, wg)], "gd_wg", 16)
                self.tt("dve", t8[:], pq[:, 0:8], dtb, ALU.add, [pn, "rowp"], ["t8_%d" % lane])
                self.act(t8[:], t8[:], AF.Exp, ["t8_%d" % lane], ["t8_%d" % lane])
                self.act(t8[:], t8[:], AF.Ln, ["t8_%d" % lane], ["t8_%d" % lane], bias=1.0)
                self.tt("dve", g_all[:, i, :], t8[:], nea[:], ALU.mult, ["t8_%d" % lane, "nea"], ["gdg%d" % i])
                yield
                self.act(u8[:], pq[:, 8:16], AF.Exp, [pn], ["u8_%d" % lane], scale=-1.0)
                self.act(u8[:], u8[:], AF.Ln, ["u8_%d" % lane], ["u8_%d" % lane], bias=1.0)
                self.ts("dve", lnb_all[:, i, :], u8[:], -1.0, None, ALU.mult, None, ["u8_%d" % lane], ["gdg%d" % i])
                self.act(beta_all[:, i, :], lnb_all[:, i, :], AF.Exp, ["gdg%d" % i], ["gdg%d" % i])
                yield
                pq, pn = self.psq()
                self.mm(pq[:, 0:4], self.cm(C_UT), g_all[:, i, 0:4], True, True, ["gdg%d" % i, "consts"], [pn])
                self.mm(pq[:, 4:8], self.cm(C_LT), g_all[:, i, 4:8], True, True, ["gdg%d" % i, "consts"], [pn])
                self.mm(pq[:, 8:16], self.cm(C_ONES), g_all[:, i, 0:8], True, True, ["gdg%d" % i, "consts"], [pn])
                self.cp("dve", gct[:], pq[:, 0:8], [pn], ["gdg%d" % i])
                self.ts("dve", ngc_all[:, i, :], gct[:], -1.0, None, ALU.mult, None, ["gdg%d" % i], ["gdg%d" % i])
                yield
                self.act(t8[:], gct[:], AF.Exp, ["gdg%d" % i], ["t8_%d" % lane])
                self.tt("dve", coef_all[:, i, :], t8[:], beta_all[:, i, :], ALU.mult, ["t8_%d" % lane, "gdg%d" % i], ["gdg%d" % i])
                self.tt("dve", u8[:], pq[:, 8:16], gct[:], ALU.subtract, [pn, "gdg%d" % i], ["u8_%d" % lane])
                self.act(kdec_all[:, i, :], u8[:], AF.Exp, ["u8_%d" % lane], ["gdg%d" % i])
                self.act(egend_all[:, i, :], pq[:, 8:16], AF.Exp, [pn], ["gdg%d" % i])
            self.pipeline([(lambda lane, i=i: gate(lane, i)) for i in range(NT)], 3, 1)
            cw0 = ROWM["gdn_cw"]
            gnorm = self.rowp[:, ROWM["gdn_norm"]:ROWM["gdn_norm"] + 64]
            for h in range(4):
                for j in range(4):
                    for b3 in range(3):
                        c0 = b3 * 256 + h * 64
                        self.load_w(wq4[j][:, :, b3 * 64:(b3 + 1) * 64], "gd_wq%d" % j, self.win_d[l, :, B0 + c0:B0 + c0 + 64], 64,
                                    scale_row=self.rowp[:, cw0 + j * 768 + c0:cw0 + j * 768 + c0 + 64])
                self.load_w(wz[:], "gd_wz", self.win_d[l, :, B0 + 768 + h * 64:B0 + 768 + (h + 1) * 64], 64)
                def prep(lane, i, h=h):
                    qkv_f, sqt, smp, qb = qkv_fs[lane], sqts[lane], smps[lane], qbs[lane]
                    ph, pn = self.psh()
                    taps = [(j - 2, wq4[j]) for j in range(4)]
                    n = 32
                    c_ = 0
                    for j in range(4):
                        for k in range(8):
                            self.mm(ph[:, 0:192], self.hT[:, k, i * 128 + j:i * 128 + j + 128], wq4[j][:, k, :], c_ == 0, c_ == n - 1,
                                    ["hT", "gd_wq%d" % j], [pn])
                            c_ += 1
                    yield
                    self.act(qkv_f[:], ph[:, 0:192], AF.Silu, [pn], ["qkv_f%d" % lane])
                    self.act(sqt[:], qkv_f[:, 0:128], AF.Square, ["qkv_f%d" % lane], ["sqt%d" % lane])
                    self.S.op("dve", lambda e: e.tensor_reduce(out=smp[:, 0:2], in_=sqt[:].rearrange("p (a b) -> p a b", a=2), axis=mybir.AxisListType.X, op=ALU.add), reads=["sqt%d" % lane], writes=["smp%d" % lane])
                    self.ts("dve", smp[:, 0:2], smp[:, 0:2], 1e-6, None, ALU.add, None, ["smp%d" % lane], ["smp%d" % lane])
                    self.act(smp[:, 0:2], smp[:, 0:2], AF.Sqrt, ["smp%d" % lane], ["smp%d" % lane])
                    self.recip(smp[:, 2:4], smp[:, 0:2], ["smp%d" % lane], ["smp%d" % lane])
                    self.ts("dve", qb[:], qkv_f[:, 0:64], smp[:, 2:3], 0.125, ALU.mult, ALU.mult, ["qkv_f%d" % lane, "smp%d" % lane], ["qb%d" % lane])
                    self.ts("dve", k_tm[:, i, :], qkv_f[:, 64:128], smp[:, 3:4], None, ALU.mult, None, ["qkv_f%d" % lane, "smp%d" % lane], ["k_tm%d" % i])
                    self.cp("pool", v_tm[:, i, :], qkv_f[:, 128:192], ["qkv_f%d" % lane], ["v_tm%d" % i])
                    yield
                    pq, pqn = self.psq()
                    pqb = pq.bitcast(BF16)
                    self.tr(pqb[0:64, 0:128], qb[:], self.idb[:], ["qb%d" % lane, "consts2"], [pqn])
                    self.tr(pqb[0:64, 128:256], k_tm[:, i, :], self.idb[:], ["k_tm%d" % i, "consts2"], [pqn])
                    self.cp("act", qT[:, i * 128:(i + 1) * 128], pqb[0:64, 0:128], [pqn], ["qT%d" % i])
                    self.cp("dve", kT[:, i * 128:(i + 1) * 128], pqb[0:64, 128:256], [pqn], ["kT%d" % i])
                    yield
                    pq, pn = self.psq()
                    self.u_tm(pq[:, 0:64], pn, i, [(0, wz)], "gd_wz", 64)
                    self.act(sz[:, i, :], pq[:, 0:64], AF.Silu, [pn], ["sz%d" % i])
                self.pipeline([(lambda lane, i=i: prep(lane, i)) for i in range(NT)], 2, 1)
                for d in range(2):
                    self.S.op("pool", (lambda t_: lambda e: e.memset(t_[:], 0.0))(P_fd[d]), writes=["P_f%d" % d])
                    self.S.op("pool", (lambda t_: lambda e: e.memset(t_[:], 0.0))(P_bd[d]), writes=["P_b%d" % d])
                done = {0: -1, 1: -1}

                def unit(lane, d, c, pos, h=h, done=done):
                    col = d * 4 + h
                    P_f, P_b = P_fd[d], P_bd[d]
                    pfn_, pbn_ = "P_f%d" % d, "P_b%d" % d
                    blk = slice(c * 128, (c + 1) * 128)
                    gl = g_all[:, c, col:col + 1]
                    ngc = ngc_all[:, c, col:col + 1]
                    r = lane
                    pqA, pnA = self.bcast_ps(gl, d, ["gdg%d" % c])
                    self.exp_from_bc(Em[r][:], "Em%d" % r, pqA, pnA, ngc, 0.0, d, False, ["gdg%d" % c])
                    self.act(ebr[r][:], pqA, AF.Exp, [pnA], ["ebr%d" % r])
                    pqB, pnB = self.bcast_ps(gl, d, ["gdg%d" % c], dl=lnb_all[:, c, col:col + 1])
                    self.exp_from_bc(Ebm[r][:], "Ebm%d" % r, pqB, pnB, ngc, 0.0, d, True, ["gdg%d" % c])
                    yield
                    pq, pn = self.psq()
                    self.mm(pq, kT[:, blk], kT[:, blk], True, True, ["kT%d" % c], [pn])
                    self.tt("dve", Bm[r][:], pq, Ebm[r][:], ALU.mult, [pn, "Ebm%d" % r], ["Bm%d" % r])
                    yield
                    pq, pn = self.psq()
                    self.mm(pq, kT[:, blk], qT[:, blk], True, True, ["kT%d" % c, "qT%d" % c], [pn])
                    self.tt("dve", QKm[r][:], pq, Em[r][:], ALU.mult, [pn, "Em%d" % r], ["QKm%d" % r])
                    W, wn = yield from self.dnc_gen(Bm[r][:], "Bm%d" % r, d)
                    yield
                    self.ts("pool", rv[r][:], v_tm[:, c, :], beta_all[:, c, col:col + 1], None, ALU.mult, None, ["v_tm%d" % c, "gdg%d" % c], ["rv%d" % r])
                    self.ts("pool", rk[r][:], k_tm[:, c, :], coef_all[:, c, col:col + 1], None, ALU.mult, None, ["k_tm%d" % c, "gdg%d" % c], ["rk%d" % r])
                    self.ts("pool", Kh[r][:], k_tm[:, c, :], kdec_all[:, c, col:col + 1], None, ALU.mult, None, ["k_tm%d" % c, "gdg%d" % c], ["Kh%d" % r])
                    pq, pn = self.psq()
                    self.mm(pq[0:64, :], rk[r][:], W[:], True, True, ["rk%d" % r, wn], [pn])
                    self.act(nW[r][:], pq[0:64, :], AF.Identity, [pn], ["nW%d" % r], scale=-1.0)
                    while done[d] != pos - 1:
                        yield
                    pq2, pn2 = self.psq()
                    self.mm(pq2[:, 0:64], W[:], rv[r][:], True, False, [wn, "rv%d" % r], [pn2])
                    self.mm(pq2[:, 0:64], nW[r][:], P_b[:], False, True, ["nW%d" % r, pbn_], [pn2])
                    self.cp("act", vn[r][:], pq2[:, 0:64], [pn2], ["vn%d" % r])
                    yield
                    self.tt("pool", qTs[r][:], qT[:, blk], ebr[r][0:64, :], ALU.mult, ["qT%d" % c, "ebr%d" % r], ["qTs%d" % r])
                    pq3, pn3 = self.psq()
                    self.mm(pq3[:, 0:64], qTs[r][:], P_b[:], True, False, ["qTs%d" % r, pbn_], [pn3])
                    self.mm(pq3[:, 0:64], QKm[r][:], vn[r][:], False, True, ["QKm%d" % r, "vn%d" % r], [pn3])
                    self.cp("dve", oaccs[d][:, c, :], pq3[:, 0:64], [pn3], ["oacc%d" % d])
                    yield
                    pq4, pn4 = self.psq()
                    self.mm(pq4[0:64, 0:64], Kh[r][:], vn[r][:], True, True, ["Kh%d" % r, "vn%d" % r], [pn4])
                    self.stt(P_f[:], P_f[:], egend_all[0:64, c, col:col + 1], pq4[0:64, 0:64], ALU.mult, ALU.add, [pfn_, "gdg%d" % c, pn4], [pfn_])
                    self.cp("act", P_b[:], P_f[:], [pfn_], [pbn_])
                    done[d] = pos

                units = []
                for i in range(NT):
                    units.append((lambda lane, i=i: unit(lane, 0, i, i)))
                    units.append((lambda lane, i=i: unit(lane, 1, NT - 1 - i, i)))
                self.pipeline(units, 3, 3)
                oacc = oaccs[0]
                for i in range(NT):
                    self.tt("dve", oacc[:, i, :], oacc[:, i, :], oaccs[1][:, i, :], ALU.add, ["oacc0", "oacc1"], ["oacc0"])
                def fin(lane, i, h=h, oacc=oacc):
                    sq, ytm, sm = sqs[lane], ytms[lane], fsms[lane]
                    self.act(sq[:], oacc[:, i, :], AF.Square, ["oacc0"], ["sq%d" % lane])
                    self.S.op("dve", lambda e: e.tensor_reduce(out=sm[:, 0:1], in_=sq[:], axis=mybir.AxisListType.X, op=ALU.add), reads=["sq%d" % lane], writes=["fsm%d" % lane])
                    self.ts("dve", sm[:, 0:1], sm[:, 0:1], 1.0 / 64, 1e-6, ALU.mult, ALU.add, ["fsm%d" % lane], ["fsm%d" % lane])
                    self.act(sm[:, 0:1], sm[:, 0:1], AF.Sqrt, ["fsm%d" % lane], ["fsm%d" % lane])
                    self.recip(sm[:, 1:2], sm[:, 0:1], ["fsm%d" % lane], ["fsm%d" % lane])
                    yield
                    self.stt(sq[:], oacc[:, i, :], sm[:, 1:2], gnorm, ALU.mult, ALU.mult, ["oacc0", "fsm%d" % lane, "rowp"], ["sq%d" % lane])
                    self.tt("dve", ytm[:], sq[:], sz[:, i, :], ALU.mult, ["sq%d" % lane, "sz%d" % i], ["ytm%d" % lane])
                    yield
                    pq, pqn = self.psq()
                    pqb = pq.bitcast(BF16)
                    self.tr(pqb[0:64, 0:128], ytm[:], self.idb[:], ["ytm%d" % lane, "consts2"], [pqn])
                    yi = lane
                    self.cp("act", yTs2[yi][:], pqb[0:64, 0:128], [pqn], ["yTs2_%d" % yi])
                    self.dma(self.yT_d[256 + h * 64:256 + (h + 1) * 64, i * 128:(i + 1) * 128], yTs2[yi][:], ["yTs2_%d" % yi], ["yTd"], "ytw%d" % yi)
                self.pipeline([(lambda lane, i=i: fin(lane, i)) for i in range(NT)], 3, 1)

    def mix_mlstm(self, l):
        nc, NT, SQ = self.nc, self.NT, self.SQ
        with ExitStack() as xs:
            sb = lambda n, s, d=F32: xs.enter_context(nc.sbuf_tensor("M%d_%s" % (l, n), s, d))
            NLM_ = 4
            wg = sb("wg", [128, 8, 16], BF16)
            w4 = sb("w4", [128, 8, 256], BF16)
            lf_all = sb("lf", [128, NT, 8]); ig_all = sb("ig", [128, NT, 8]); bc_all = sb("bc", [128, NT, 8])
            ib_all = sb("ib", [128, NT, 8]); wts_all = sb("wts", [128, NT, 8]); ebend_all = sb("ebend", [128, NT, 8])
            g16s = [sb("g16_%d" % i, [128, 16]) for i in range(3)]; th16s = [sb("th16_%d" % i, [128, 16]) for i in range(3)]; tmp8s = [sb("tmp8_%d" % i, [128, 8]) for i in range(3)]
            k_tm = sb("k_tm", [128, NT, 64], BF16); vext = sb("vext", [128, NT, 66], BF16)
            qT = sb("qT", [64, SQ], BF16); kT = sb("kT", [64, SQ], BF16)
            sigo = sb("sigo", [128, NT, 64], BF16); haccs = [sb("hacc%d" % i, [128, NT, 64]) for i in (0, 1)]
            qbs = [sb("qb%d" % i, [128, 64], BF16) for i in range(3)]
            Dm = [sb("Dm%d" % i, [128, 128]) for i in range(NLM_)]
            Pm = [sb("Pm%d" % i, [128, 128], BF16) for i in range(NLM_)]
            ebr = [sb("ebr%d" % i, [128, 128]) for i in range(NLM_)]
            qTs = [sb("qTs%d" % i, [64, 128], BF16) for i in range(NLM_)]
            Vs = [sb("Vs%d" % i, [128, 66], BF16) for i in range(NLM_)]
            C_fd = [sb("C_f%d" % i, [64, 66]) for i in (0, 1)]; C_bd = [sb("C_b%d" % i, [64, 66], BF16) for i in (0, 1)]
            sms = [sb("sm%d" % i, [128, 8]) for i in range(NLM_)]; sm = sms[0]; sqs = [sb("sq%d" % i, [128, 64]) for i in range(3)]; ytms = [sb("ytm%d" % i, [128, 64], BF16) for i in range(3)]; fsms = [sb("fsm%d" % i, [128, 4]) for i in range(3)]
            yTs2 = [sb("yTs2_%d" % i, [64, 128], BF16) for i in range(3)]
            self.load_w(wg[:], "ml_wg", self.win_d[l, :, C0 + 1024:C0 + 1040], 16)
            gb = self.rowp[:, ROWM["ml_gb"]:ROWM["ml_gb"] + 16]
            def gate(lane, i):
                g16, th16, tmp8 = g16s[lane], th16s[lane], tmp8s[lane]
                pq, pn = self.psq()
                self.u_tm(pq[:, 0:16], pn, i, [(0, wg)], "ml_wg", 16)
                self.tt("dve", g16[:], pq[:, 0:16], gb, ALU.add, [pn, "rowp"], ["g16_%d" % lane])
                self.act(th16[:], g16[:], AF.Tanh, ["g16_%d" % lane], ["th16_%d" % lane], scale=1.0 / 15.0)
                self.ts("dve", ig_all[:, i, :], th16[:, 0:8], 15.0, None, ALU.mult, None, ["th16_%d" % lane], ["mlg%d" % i])
                yield
                self.act(tmp8[:], th16[:, 8:16], AF.Exp, ["th16_%d" % lane], ["tmp8_%d" % lane], scale=-15.0)
                self.act(tmp8[:], tmp8[:], AF.Ln, ["tmp8_%d" % lane], ["tmp8_%d" % lane], bias=1.0)
                self.ts("dve", lf_all[:, i, :], tmp8[:], -1.0, None, ALU.mult, None, ["tmp8_%d" % lane], ["mlg%d" % i])
                yield
                pq, pn = self.psq()
                self.mm(pq[:, 0:4], self.cm(C_UT), lf_all[:, i, 0:4], True, True, ["mlg%d" % i, "consts"], [pn])
                self.mm(pq[:, 4:8], self.cm(C_LT), lf_all[:, i, 4:8], True, True, ["mlg%d" % i, "consts"], [pn])
                self.mm(pq[:, 8:16], self.cm(C_ONES), lf_all[:, i, 0:8], True, True, ["mlg%d" % i, "consts"], [pn])
                self.cp("dve", bc_all[:, i, :], pq[:, 0:8], [pn], ["mlg%d" % i])
                self.tt("dve", ib_all[:, i, :], ig_all[:, i, :], bc_all[:, i, :], ALU.subtract, ["mlg%d" % i], ["mlg%d" % i])
                yield
                self.tt("dve", tmp8[:], pq[:, 8:16], ib_all[:, i, :], ALU.add, [pn, "mlg%d" % i], ["tmp8_%d" % lane])
                self.act(wts_all[:, i, :], tmp8[:], AF.Exp, ["tmp8_%d" % lane], ["mlg%d" % i])
                self.act(ebend_all[:, i, :], pq[:, 8:16], AF.Exp, [pn], ["mlg%d" % i])
            self.pipeline([(lambda lane, i=i: gate(lane, i)) for i in range(NT)], 3, 1)
            import os
            stage = int(os.environ.get("ML_STAGE", "9"))
            if stage < 2:
                return
            self.S.op("pool", lambda e: e.memset(vext[:, :, 64:66], 1.0), writes=["vext%d" % t for t in range(NT)])
            for h in range(4):
                for j in range(4):
                    self.load_w(w4[:, :, j * 64:(j + 1) * 64], "ml_w4", self.win_d[l, :, C0 + j * 256 + h * 64:C0 + j * 256 + (h + 1) * 64], 64)
                def prep(lane, i, h=h):
                    qb = qbs[lane]
                    ph, pn = self.psh()
                    self.u_tm(ph[:, 0:256], pn, i, [(0, w4)], "ml_w4", 256)
                    mo = "qkvs"
                    yield
                    if "q" in mo:
                        self.cp("dve", qb[:], ph[:, 0:64], [pn], ["qb%d" % lane])
                    if "k" in mo:
                        self.ts("dve", k_tm[:, i, :], ph[:, 64:128], 0.125, None, ALU.mult, None, [pn], ["k_tm%d" % i])
                    if "v" in mo:
                        self.cp("act", vext[:, i, 0:64], ph[:, 128:192], [pn], ["vext%d" % i])
                    if "s" in mo:
                        self.act(sigo[:, i, :], ph[:, 192:256], AF.Sigmoid, [pn], ["sigo%d" % i])
                    yield
                    pq, pqn = self.psq()
                    pqb = pq.bitcast(BF16)
                    self.tr(pqb[0:64, 0:128], qb[:], self.idb[:], ["qb%d" % lane, "consts2"], [pqn])
                    self.tr(pqb[0:64, 128:256], k_tm[:, i, :], self.idb[:], ["k_tm%d" % i, "consts2"], [pqn])
                    self.cp("act", qT[:, i * 128:(i + 1) * 128], pqb[0:64, 0:128], [pqn], ["qT%d" % i])
                    self.cp("dve", kT[:, i * 128:(i + 1) * 128], pqb[0:64, 128:256], [pqn], ["kT%d" % i])
                self.pipeline([(lambda lane, i=i: prep(lane, i)) for i in range(NT)], 3, 1)
                if stage < 3:
                    continue
                for d in range(2):
                    self.S.op("pool", (lambda t_: lambda e: e.memset(t_[:], 0.0))(C_fd[d]), writes=["C_f%d" % d])
                    self.S.op("pool", (lambda t_: lambda e: e.memset(t_[:], 0.0))(C_bd[d]), writes=["C_b%d" % d])
                done = {0: -1, 1: -1}

                def unit(lane, d, c, pos, h=h, done=done):
                    col = d * 4 + h
                    C_f, C_b = C_fd[d], C_bd[d]
                    cfn_, cbn_ = "C_f%d" % d, "C_b%d" % d
                    blk = slice(c * 128, (c + 1) * 128)
                    gl = lf_all[:, c, col:col + 1]
                    r = lane
                    sm = sms[lane]
                    pqA, pnA = self.bcast_ps(gl, d, ["mlg%d" % c])
                    self.exp_from_bc(Dm[r][:], "Dm%d" % r, pqA, pnA, ib_all[:, c, col:col + 1], 16.0, d, False, ["mlg%d" % c])
                    self.act(ebr[r][:], pqA, AF.Exp, [pnA], ["ebr%d" % r])
                    yield
                    pq, pn = self.psq()
                    self.mm(pq, kT[:, blk], qT[:, blk], True, True, ["kT%d" % c, "qT%d" % c], [pn])
                    self.tt("dve", Pm[r][:], pq, Dm[r][:], ALU.mult, [pn, "Dm%d" % r], ["Pm%d" % r])
                    yield
                    self.tt("pool", qTs[r][:], qT[:, blk], ebr[r][0:64, :], ALU.mult, ["qT%d" % c, "ebr%d" % r], ["qTs%d" % r])
                    while done[d] != pos - 1:
                        yield
                    pq2, pn2 = self.psq()
                    self.mm(pq2[:, 0:65], Pm[r][:], vext[:, c, 0:65], True, False, ["Pm%d" % r, "vext%d" % c], [pn2])
                    self.mm(pq2[:, 0:65], qTs[r][:], C_b[:, 0:65], False, True, ["qTs%d" % r, cbn_], [pn2])
                    self.act(sm[:, 0:1], pq2[:, 64:65], AF.Abs, [pn2], ["sm%d" % lane])
                    self.ts("dve", sm[:, 0:1], sm[:, 0:1], 1.0, None, ALU.max, None, ["sm%d" % lane], ["sm%d" % lane])
                    self.recip(sm[:, 1:2], sm[:, 0:1], ["sm%d" % lane], ["sm%d" % lane])
                    self.ts("dve", haccs[d][:, c, :], pq2[:, 0:64], sm[:, 1:2], None, ALU.mult, None, [pn2, "sm%d" % lane], ["hacc%d" % d])
                    yield
                    self.ts("pool", Vs[r][:], vext[:, c, :], wts_all[:, c, col:col + 1], None, ALU.mult, None, ["vext%d" % c, "mlg%d" % c], ["Vs%d" % r])
                    pq3, pn3 = self.psq()
                    self.mm(pq3[0:64, 0:65], k_tm[:, c, :], Vs[r][:, 0:65], True, True, ["k_tm%d" % c, "Vs%d" % r], [pn3])
                    self.stt(C_f[:, 0:65], C_f[:, 0:65], ebend_all[0:64, c, col:col + 1], pq3[0:64, 0:65], ALU.mult, ALU.add, [cfn_, "mlg%d" % c, pn3], [cfn_])
                    self.cp("act", C_b[:], C_f[:], [cfn_], [cbn_])
                    done[d] = pos

                units = []
                for i in range(NT):
                    units.append((lambda lane, i=i: unit(lane, 0, i, i)))
                    units.append((lambda lane, i=i: unit(lane, 1, NT - 1 - i, i)))
                self.pipeline(units, NLM_, 2)
                hacc = haccs[0]
                sm = sms[0]
                for i in range(NT):
                    self.tt("dve", hacc[:, i, :], hacc[:, i, :], haccs[1][:, i, :], ALU.add, ["hacc0", "hacc1"], ["hacc0"])
                if stage < 4:
                    continue
                ng = self.rowp[:, ROWM["ml_norm"] + h * 64:ROWM["ml_norm"] + (h + 1) * 64]
                def fin(lane, i, h=h, ng=ng, hacc=hacc):
                    sq, ytm, sm = sqs[lane], ytms[lane], fsms[lane]
                    self.act(sq[:], hacc[:, i, :], AF.Square, ["hacc0"], ["sq%d" % lane])
                    self.S.op("dve", lambda e: e.tensor_reduce(out=sm[:, 0:1], in_=sq[:], axis=mybir.AxisListType.X, op=ALU.add), reads=["sq%d" % lane], writes=["fsm%d" % lane])
                    self.ts("dve", sm[:, 0:1], sm[:, 0:1], 1.0 / 64, 1e-6, ALU.mult, ALU.add, ["fsm%d" % lane], ["fsm%d" % lane])
                    self.act(sm[:, 0:1], sm[:, 0:1], AF.Sqrt, ["fsm%d" % lane], ["fsm%d" % lane])
                    self.recip(sm[:, 1:2], sm[:, 0:1], ["fsm%d" % lane], ["fsm%d" % lane])
                    yield
                    self.stt(sq[:], hacc[:, i, :], sm[:, 1:2], ng, ALU.mult, ALU.mult, ["hacc0", "fsm%d" % lane, "rowp"], ["sq%d" % lane])
                    self.tt("dve", ytm[:], sq[:], sigo[:, i, :], ALU.mult, ["sq%d" % lane, "sigo%d" % i], ["ytm%d" % lane])
                    yield
                    pq, pqn = self.psq()
                    pqb = pq.bitcast(BF16)
                    self.tr(pqb[0:64, 0:128], ytm[:], self.idb[:], ["ytm%d" % lane, "consts2"], [pqn])
                    yi = lane
                    self.cp("act", yTs2[yi][:], pqb[0:64, 0:128], [pqn], ["yTs2_%d" % yi])
                    self.dma(self.yT_d[512 + h * 64:512 + (h + 1) * 64, i * 128:(i + 1) * 128], yTs2[yi][:], ["yTs2_%d" % yi], ["yTd"], "ytw%d" % yi)
                self.pipeline([(lambda lane, i=i: fin(lane, i)) for i in range(NT)], 3, 1)

    def mix_rwkv(self, l):
        nc, NT, SQ = self.nc, self.NT, self.SQ
        X = mybir.AxisListType.X
        with ExitStack() as xs:
            sb = lambda n, s, d=F32: xs.enter_context(nc.sbuf_tensor("R%d_%s" % (l, n), s, d))
            mus1 = sb("mus1", [128, 960])
            wt = [sb("wt%d" % j, [128, 8, 192], BF16) for j in range(3)]
            rupw2 = sb("rupw2", [64, 256], BF16); rupa2 = sb("rupa2", [64, 256], BF16)
            ustg = [sb("ustg%d" % i, [128, 192], BF16) for i in range(2)]
            lrbs = [sb("lrb%d" % i, [128, 192], BF16) for i in (0, 1)]; ptmps = [sb("ptmp%d" % i, [128, 64]) for i in (0, 1)]
            lrs = [sb("lrs%d" % i, [64, 3, 128], BF16) for i in range(2)]
            rupg = sb("rupg", [64, 256], BF16)
            rstg = self.wstg[0][0:64].rearrange("p k c -> p (k c)")[:, 0:768]
            NL_ = 4
            lane_specs = [("u_f", [128, 192], BF16), ("lr", [128, 2], BF16), ("lrT", [64, 3, 128], BF16),
                          ("logw", [128, 64], F32), ("a_t", [128, 64], F32), ("kk", [128, 64], F32), ("kd", [128, 64], F32),
                          ("kka", [128, 64], F32), ("tmp", [128, 64], F32), ("tmp2", [128, 64], F32), ("cum_s", [128, 64], F32),
                          ("Gt", [128, 64], F32), ("iG", [128, 64], F32), ("Gw", [128, 64], F32), ("Gh", [128, 64], F32),
                          ("ops4", [128, 4, 64], BF16), ("opsT", [64, 4, 128], BF16), ("Bh", [128, 64], BF16), ("Kh", [128, 64], BF16),
                          ("v_b", [128, 64], BF16), ("GLc", [64, 2], F32), ("Bm", [128, 128], BF16), ("ArbT", [128, 128], BF16),
                          ("AakT", [128, 128], BF16), ("ArkT", [128, 128], BF16), ("Gm", [128, 64], BF16), ("Ub", [128, 64], BF16),
                          ("sm", [128, 8], F32)]
            lanes = [{n: sb("%s_%d" % (n, a), s, d_) for (n, s, d_) in lane_specs} for a in range(NL_)]
            self.S.local.update(n for (n, _, _) in lane_specs)
            P_fd = [sb("P_f%d" % d, [64, 64]) for d in range(2)]
            P_bd = [sb("P_b%d" % d, [64, 64], BF16) for d in range(2)]
            yaccs = [sb("yacc%d" % d, [128, NT, 64]) for d in range(2)]
            bacc = sb("bacc", [128, NT, 64], BF16); gacc = sb("gacc", [128, NT, 64], BF16)
            fsms = [sb("fsm%d" % i, [128, 8]) for i in range(3)]; ftmps = [sb("ftmp%d" % i, [128, 64]) for i in range(3)]; ftmp2s = [sb("ftmp2_%d" % i, [128, 64]) for i in range(3)]
            ytms = [sb("ytm%d" % i, [128, 64], BF16) for i in range(3)]; yTs2 = [sb("yTs2_%d" % i, [64, 128], BF16) for i in range(3)]
            mu0 = self.rowp[:, ROWM["mu"]:ROWM["mu"] + 960]
            mu1 = self.rowp[:, ROWM["mu"] + 960:ROWM["mu"] + 1920]
            self.tt("dve", mus1[:], mu0, mu1, ALU.add, ["rowp"], ["mus"])
            self.ts("dve", mus1[:], mus1[:], -1.0, 1.0, ALU.mult, ALU.add, ["mus"], ["mus"])
            musl = [mu0, mus1, mu1]
            self.dma(rstg, self.rup_d[l, :, :], [], ["wstg0"], "wstg0")
            self.cp("pool", rupw2[:], rstg[:, 0:256], ["wstg0"], ["rup"])
            self.cp("pool", rupa2[:], rstg[:, 256:512], ["wstg0"], ["rup"])
            self.cp("pool", rupg[:], rstg[:, 512:768], ["wstg0"], ["rup"])
            ro = lambda nm, a, b: self.rowp[:, ROWM[nm] + a:ROWM[nm] + b]
            for j in range(3):
                self.load_w(wt[j][:], "rw_wt%d" % j, self.win_d[l, :, D0 + 768:D0 + 960], 192, scale_row=musl[j][:, 768:960])
            def lrprep(lane, i):
                lrb, ptmp = lrbs[lane], ptmps[lane]
                ph, phn = self.psh()
                cc = 0
                for j in range(3):
                    for k in range(8):
                        self.mm(ph[:, 0:192], self.hT[:, k, 1 + i * 128 + j:1 + i * 128 + j + 128], wt[j][:, k, :], cc == 0, cc == 23,
                                ["hT", "rw_wt%d" % j, "mus"], [phn])
                        cc += 1
                yield
                self.act(ptmp[:], ph[:, 0:64], AF.Exp, [phn], ["ptmp%d" % lane], scale=2.0)
                self.ts("dve", ptmp[:], ptmp[:], 1.0, None, ALU.add, None, ["ptmp%d" % lane], ["ptmp%d" % lane])
                self.recip(ptmp[:], ptmp[:], ["ptmp%d" % lane], ["ptmp%d" % lane])
                self.ts("dve", lrb[:, 0:64], ptmp[:], -2.0, 1.0, ALU.mult, ALU.add, ["ptmp%d" % lane], ["lrb%d" % lane])
                self.cp("dve", lrb[:, 64:128], ph[:, 64:128], [phn], ["lrb%d" % lane])
                self.act(ptmp[:], ph[:, 128:192], AF.Exp, [phn], ["ptmp%d" % lane], scale=-1.0)
                self.ts("dve", ptmp[:], ptmp[:], 1.0, None, ALU.add, None, ["ptmp%d" % lane], ["ptmp%d" % lane])
                self.recip(ptmp[:], ptmp[:], ["ptmp%d" % lane], ["ptmp%d" % lane])
                self.cp("dve", lrb[:, 128:192], ptmp[:], ["ptmp%d" % lane], ["lrb%d" % lane])
                yield
                p2, p2n = self.psh()
                p2b = p2.bitcast(BF16)
                for a3 in range(3):
                    self.tr(p2b[0:64, a3 * 128:(a3 + 1) * 128], lrb[:, a3 * 64:(a3 + 1) * 64], self.idb[:], ["lrb%d" % lane, "consts2"], [p2n])
                si = lane
                self.cp("act", lrs[si][:], p2b[0:64, 0:384].rearrange("p (a t) -> p a t", a=3), [p2n], ["lrs%d" % si])
                self.dma(self.lrT_d[:, :, i * 128:(i + 1) * 128].rearrange("a p t -> p a t"), lrs[si][:], ["lrs%d" % si], ["lrTd"], "lrs%d" % si)
            self.pipeline([(lambda lane, i=i: lrprep(lane, i)) for i in range(NT)], 2, 1)
            for h in range(4):
                hc = slice(h * 64, (h + 1) * 64)
                for j in range(3):
                    for b3 in range(3):
                        c0 = b3 * 256 + h * 64
                        self.load_w(wt[j][:, :, b3 * 64:(b3 + 1) * 64], "rw_wt%d" % j, self.win_d[l, :, D0 + c0:D0 + c0 + 64], 64,
                                    scale_row=musl[j][:, c0:c0 + 64])
                rkvd = self.rkv_d[h % 2]
                rkn = "rkvd%d" % (h % 2)
                for i in range(NT):
                    ph, phn = self.psh()
                    cc = 0
                    for j in range(3):
                        for k in range(8):
                            self.mm(ph[:, 0:192], self.hT[:, k, 1 + i * 128 + j:1 + i * 128 + j + 128], wt[j][:, k, :], cc == 0, cc == 23,
                                    ["hT", "rw_wt%d" % j, "mus"], [phn])
                            cc += 1
                    si = self.rot("ustg", 2)
                    self.cp("act", ustg[si][:], ph[:, 0:192], [phn], ["ustg%d" % si])
                    self.dma(rkvd[i * 128:(i + 1) * 128, :], ustg[si][:], ["ustg%d" % si], [rkn], "ustg%d" % si)
                for d in range(2):
                    self.S.op("pool", (lambda t_: lambda e: e.memset(t_[:], 0.0))(P_fd[d]), writes=["P_f%d" % d])
                    self.S.op("pool", (lambda t_: lambda e: e.memset(t_[:], 0.0))(P_bd[d]), writes=["P_b%d" % d])
                done = {0: -1, 1: -1}

                def unit(lane, d, c, pos, h=h, hc=hc, done=done, rkvd=rkvd, rkn=rkn):
                    T_ = lanes[lane]
                    (u_f, lr, lrT, logw, a_t, kk, kd, kka, tmp, tmp2, cum_s, Gt, iG, Gw, Gh, ops4, opsT, Bh, Kh, v_b, GLc,
                     Bm, ArbT, AakT, ArkT, Gm, Ub, sm) = [T_[n] for (n, _, _) in lane_specs]
                    P_f, P_b = P_fd[d], P_bd[d]
                    pfn_, pbn_ = "P_f%d" % d, "P_b%d" % d
                    yacc = yaccs[d]
                    self.dma(u_f[:], rkvd[c * 128:(c + 1) * 128, :], [rkn], ["u_f"], "rwu%d" % lane)
                    self.dma(lrT[:], self.lrT_d[:, :, c * 128:(c + 1) * 128].rearrange("a p t -> p a t"), ["lrTd"], ["lrT"], "rwl%d" % lane)
                    r_ = u_f[:, 0:64]; k_ = u_f[:, 64:128]
                    self.cp("pool", v_b[:], u_f[:, 128:192], ["u_f"], ["v_b"])
                    yield
                    pq, pn = self.psq()
                    self.mm(pq[:, 0:64], lrT[d * 32:(d + 1) * 32, 0, :], rupw2[d * 32:(d + 1) * 32, hc], True, True, ["lrT", "rup"], [pn])
                    self.tt("dve", tmp[:], pq[:, 0:64], ro("rw_w0", d * 256 + h * 64, d * 256 + h * 64 + 64), ALU.add, [pn, "rowp"], ["tmp"])
                    self.act(tmp[:], tmp[:], AF.Exp, ["tmp"], ["tmp"], scale=-1.0)
                    self.ts("dve", tmp[:], tmp[:], 1.0, None, ALU.add, None, ["tmp"], ["tmp"])
                    self.recip(tmp[:], tmp[:], ["tmp"], ["tmp"])
                    self.ts("dve", logw[:], tmp[:], -0.6065306597126334, None, ALU.mult, None, ["tmp"], ["logw"])
                    pq, pn = self.psq()
                    self.mm(pq[:, 0:64], lrT[d * 32:(d + 1) * 32, 1, :], rupa2[d * 32:(d + 1) * 32, hc], True, True, ["lrT", "rup"], [pn])
                    self.tt("dve", tmp[:], pq[:, 0:64], ro("rw_a0", d * 256 + h * 64, d * 256 + h * 64 + 64), ALU.add, [pn, "rowp"], ["tmp"])
                    self.act(tmp[:], tmp[:], AF.Exp, ["tmp"], ["tmp"], scale=-1.0)
                    self.ts("dve", tmp[:], tmp[:], 1.0, None, ALU.add, None, ["tmp"], ["tmp"])
                    self.recip(a_t[:], tmp[:], ["tmp"], ["a_t"])
                    if d == 0:
                        pq, pn = self.psq()
                        self.mm(pq[:, 0:64], lrT[:, 2, :], rupg[:, hc], True, True, ["lrT", "rup"], [pn])
                        self.cp("act", gacc[:, c, :], pq[:, 0:64], [pn], ["gacc"])
                        self.tt("pool", tmp2[:], r_, k_, ALU.mult, ["u_f"], ["tmp2"])
                        self.tt("pool", tmp2[:], tmp2[:], ro("rw_rk", h * 64, h * 64 + 64), ALU.mult, ["tmp2", "rowp"], ["tmp2"])
                        self.S.op("dve", lambda e: e.tensor_reduce(out=sm[:, 6:7], in_=tmp2[:], axis=X, op=ALU.add), reads=["tmp2"], writes=["sm"])
                        self.ts("dve", bacc[:, c, :], u_f[:, 128:192], sm[:, 6:7], None, ALU.mult, None, ["u_f", "sm"], ["bacc"])
                    yield
                    self.tt("dve", kk[:], k_, ro("rw_kk", h * 64, h * 64 + 64), ALU.mult, ["u_f", "rowp"], ["kk"])
                    self.act(tmp2[:], kk[:], AF.Square, ["kk"], ["tmp2"])
                    self.S.op("dve", lambda e: e.tensor_reduce(out=sm[:, 0:1], in_=tmp2[:], axis=X, op=ALU.add), reads=["tmp2"], writes=["sm"])
                    self.act(sm[:, 0:1], sm[:, 0:1], AF.Ln, ["sm"], ["sm"], bias=1e-6)
                    self.act(sm[:, 1:2], sm[:, 0:1], AF.Exp, ["sm"], ["sm"], scale=-0.5)
                    self.ts("dve", kk[:], kk[:], sm[:, 1:2], None, ALU.mult, None, ["kk", "sm"], ["kk"])
                    self.stt(tmp[:], a_t[:], -1.0, ro("rw_ka", h * 64, h * 64 + 64), ALU.add, ALU.mult, ["a_t", "rowp"], ["tmp"])
                    self.stt(kd[:], tmp[:], 1.0, k_, ALU.add, ALU.mult, ["tmp", "u_f"], ["kd"])
                    self.tt("pool", kka[:], kk[:], a_t[:], ALU.mult, ["kk", "a_t"], ["kka"])
                    yield
                    pq, pn = self.psq()
                    self.mm(pq[:, 0:64], self.cm(C_LT if d else C_UT), logw[:], True, True, ["logw", "consts"], [pn])
                    self.cp("dve", cum_s[:], pq[:, 0:64], [pn], ["cum_s"])
                    self.mm(pq[:, 64:128], self.cm(C_ONES), logw[:], True, True, ["logw", "consts"], [pn])
                    self.tt("dve", tmp[:], pq[:, 64:128], cum_s[:], ALU.subtract, [pn, "cum_s"], ["tmp"])
                    self.act(Gh[:], tmp[:], AF.Exp, ["tmp"], ["Gh"])
                    pq2, pn2 = self.psq()
                    self.mm(pq2[0:64, 0:2], logw[:], self.cm(C_ONES)[:, 0:2], True, True, ["logw", "consts"], [pn2])
                    self.act(GLc[:], pq2[0:64, 0:2], AF.Exp, [pn2], ["GLc"])
                    self.act(Gt[:], cum_s[:], AF.Exp, ["cum_s"], ["Gt"])
                    self.act(iG[:], cum_s[:], AF.Exp, ["cum_s"], ["iG"], scale=-1.0)
                    self.tt("pool", tmp2[:], cum_s[:], logw[:], ALU.subtract, ["cum_s", "logw"], ["tmp2"])
                    self.act(Gw[:], tmp2[:], AF.Exp, ["tmp2"], ["Gw"])
                    yield
                    self.stt(ops4[:, 0, :], kk[:], -1.0, Gw[:], ALU.mult, ALU.mult, ["kk", "Gw"], ["ops4"])
                    self.tt("dve", ops4[:, 1, :], r_, Gt[:], ALU.mult, ["u_f", "Gt"], ["ops4"])
                    self.tt("pool", ops4[:, 2, :], kka[:], iG[:], ALU.mult, ["kka", "iG"], ["ops4"])
                    self.tt("pool", ops4[:, 3, :], kd[:], iG[:], ALU.mult, ["kd", "iG"], ["ops4"])
                    self.tt("pool", Bh[:], kka[:], Gh[:], ALU.mult, ["kka", "Gh"], ["Bh"])
                    self.tt("dve", Kh[:], kd[:], Gh[:], ALU.mult, ["kd", "Gh"], ["Kh"])
                    yield
                    ph, phn = self.psh()
                    phb = ph.bitcast(BF16)
                    for q4 in range(4):
                        self.tr(phb[0:64, q4 * 128:(q4 + 1) * 128], ops4[:, q4, :], self.idb[:], ["ops4", "consts2"], [phn])
                    self.cp("act", opsT[:], phb[0:64, 0:512].rearrange("p (a t) -> p a t", a=4), [phn], ["opsT"])
                    ar = opsT[:, 0:2, :].rearrange("p a t -> p (a t)")
                    m1, m1n = self.psh()
                    self.mm(m1, opsT[:, 2, :], ar, True, True, ["opsT"], [m1n])
                    sF, iF = (C_STR_B, C_INC_B) if d else (C_STR_F, C_INC_F)
                    self.stt(Bm[:], m1[:, 0:128], -1.0, self.cm(sF), ALU.mult, ALU.mult, [m1n, "consts"], ["Bm"])
                    self.tt("dve", ArbT[:], m1[:, 128:256], self.cm(iF), ALU.mult, [m1n, "consts"], ["ArbT"])
                    yield
                    m2, m2n = self.psh()
                    self.mm(m2, opsT[:, 3, :], ar, True, True, ["opsT"], [m2n])
                    self.tt("dve", AakT[:], m2[:, 0:128], self.cm(sF), ALU.mult, [m2n, "consts"], ["AakT"])
                    self.tt("dve", ArkT[:], m2[:, 128:256], self.cm(iF), ALU.mult, [m2n, "consts"], ["ArkT"])
                    W, wn = yield from self.dnc_gen(Bm[:], "Bm", d)
                    while done[d] != pos - 1:
                        yield
                    pq, pn = self.psq()
                    self.mm(pq[:, 0:64], opsT[:, 0, :], P_b[:], True, False, ["opsT", pbn_], [pn])
                    self.mm(pq[:, 0:64], AakT[:], v_b[:], False, True, ["AakT", "v_b"], [pn])
                    self.cp("act", Gm[:], pq[:, 0:64], [pn], ["Gm"])
                    yield
                    pq, pn = self.psq()
                    self.mm(pq[:, 0:64], W[:], Gm[:], True, True, [wn, "Gm"], [pn])
                    self.cp("act", Ub[:], pq[:, 0:64], [pn], ["Ub"])
                    yield
                    pq, pn = self.psq()
                    self.mm(pq[:, 0:64], opsT[:, 1, :], P_b[:], True, False, ["opsT", pbn_], [pn])
                    self.mm(pq[:, 0:64], ArbT[:], Ub[:], False, False, ["ArbT", "Ub"], [pn])
                    self.mm(pq[:, 0:64], ArkT[:], v_b[:], False, True, ["ArkT", "v_b"], [pn])
                    self.cp("dve", yacc[:, c, :], pq[:, 0:64], [pn], ["yacc%d" % d])
                    yield
                    pq4, pn4 = self.psq()
                    self.mm(pq4[0:64, 0:64], Bh[:], Ub[:], True, False, ["Bh", "Ub"], [pn4])
                    self.mm(pq4[0:64, 0:64], Kh[:], v_b[:], False, True, ["Kh", "v_b"], [pn4])
                    self.stt(P_f[:], P_f[:], GLc[:, 0:1], pq4[0:64, 0:64], ALU.mult, ALU.add, [pfn_, "GLc", pn4], [pfn_])
                    self.cp("act", P_b[:], P_f[:], [pfn_], [pbn_])
                    done[d] = pos

                units = []
                for i in range(NT):
                    units.append((lambda lane, i=i: unit(lane, 0, i, i)))
                    units.append((lambda lane, i=i: unit(lane, 1, NT - 1 - i, i)))
                self.pipeline(units, NL_, 2)
                def fin(lane, c, h=h):
                    tmp, tmp2, sm, ytm = ftmps[lane], ftmp2s[lane], fsms[lane], ytms[lane]
                    self.tt("dve", yaccs[0][:, c, :], yaccs[0][:, c, :], yaccs[1][:, c, :], ALU.add, ["yacc0", "yacc1"], ["yacc0"])
                    yacc = yaccs[0]
                    yv = yacc[:, c, :]
                    self.S.op("dve", (lambda yv_: lambda e: e.tensor_reduce(out=sm[:, 2:3], in_=yv_, axis=X, op=ALU.add))(yv), reads=["yacc0"], writes=["fsm%d" % lane])
                    self.ts("dve", sm[:, 2:3], sm[:, 2:3], 1.0 / 64, None, ALU.mult, None, ["fsm%d" % lane], ["fsm%d" % lane])
                    self.ts("dve", tmp[:], yv, sm[:, 2:3], None, ALU.subtract, None, ["yacc0", "fsm%d" % lane], ["ftmp%d" % lane])
                    yield
                    self.act(tmp2[:], tmp[:], AF.Square, ["ftmp%d" % lane], ["ftmp2_%d" % lane])
                    self.S.op("dve", lambda e: e.tensor_reduce(out=sm[:, 3:4], in_=tmp2[:], axis=X, op=ALU.add), reads=["ftmp2_%d" % lane], writes=["fsm%d" % lane])
                    self.ts("dve", sm[:, 3:4], sm[:, 3:4], 1.0 / 64, 64e-5, ALU.mult, ALU.add, ["fsm%d" % lane], ["fsm%d" % lane])
                    self.act(sm[:, 3:4], sm[:, 3:4], AF.Sqrt, ["fsm%d" % lane], ["fsm%d" % lane])
                    self.recip(sm[:, 4:5], sm[:, 3:4], ["fsm%d" % lane], ["fsm%d" % lane])
                    yield
                    self.stt(tmp[:], tmp[:], sm[:, 4:5], ro("rw_gnw", h * 64, h * 64 + 64), ALU.mult, ALU.mult, ["ftmp%d" % lane, "fsm%d" % lane, "rowp"], ["ftmp%d" % lane])
                    self.tt("dve", tmp[:], tmp[:], ro("rw_gnb", h * 64, h * 64 + 64), ALU.add, ["ftmp%d" % lane, "rowp"], ["ftmp%d" % lane])
                    self.tt("dve", tmp[:], tmp[:], bacc[:, c, :], ALU.add, ["ftmp%d" % lane, "bacc"], ["ftmp%d" % lane])
                    self.tt("dve", ytm[:], tmp[:], gacc[:, c, :], ALU.mult, ["ftmp%d" % lane, "gacc"], ["ytm%d" % lane])
                    yield
                    pq, pqn = self.psq()
                    pqb = pq.bitcast(BF16)
                    self.tr(pqb[0:64, 0:128], ytm[:], self.idb[:], ["ytm%d" % lane, "consts2"], [pqn])
                    yi = lane
                    self.cp("act", yTs2[yi][:], pqb[0:64, 0:128], [pqn], ["yTs2_%d" % yi])
                    self.dma(self.yT_d[768 + h * 64:768 + (h + 1) * 64, c * 128:(c + 1) * 128], yTs2[yi][:], ["yTs2_%d" % yi], ["yTd"], "ytw%d" % yi)
                self.pipeline([(lambda lane, c=c: fin(lane, c)) for c in range(NT)], 3, 1)

    def phase_cd(self, l, src):
        nc, NT = self.nc, self.NT
        with ExitStack() as cs:
            sb = lambda n, s, d=F32: cs.enter_context(nc.sbuf_tensor("C%d_%s" % (l, n), s, d))
            wo = sb("wo", [128, 8, D], BF16)
            fwi = sb("fwi", [128, 8, 2 * DFF], BF16)
            fwo = sb("fwo", [128, 22, D], BF16)
            rowcd = sb("rowcd", [128, 2048])
            self.dma(rowcd[:], self.row_d[l, 0, 0:2048].partition_broadcast(128), [], ["rowcd"], "par")
            self.dma(wo[:], self.wout_d[l, :, :].rearrange("(k p) c -> p k c", p=128), [], ["wo"], "wcd", eng="pool")
            for c in range(4):
                self.dma(fwi[:, :, c * 1408:(c + 1) * 1408], self.fwi_d[l, :, c * 1408:(c + 1) * 1408].rearrange("(k p) c -> p k c", p=128),
                         [], ["fwi"], "wcd", eng="pool")
            for k0 in (0, 8, 16):
                kn = min(8, 22 - k0)
                self.dma(fwo[:, k0:k0 + kn, :], self.fwo_d[l, k0 * 128:(k0 + kn) * 128, :].rearrange("(k p) c -> p k c", p=128),
                         [], ["fwo"], "wcd", eng="pool")
            xa_ = [sb("xa%d" % i, [128, D]) for i in range(2)]
            t1_ = [sb("t1%d" % i, [128, D]) for i in range(2)]
            yt_ = [sb("yt%d" % i, [128, 8, 128], BF16) for i in range(2)]
            xn_ = [sb("xn%d" % i, [128, D], BF16) for i in range(2)]
            h2T_ = [sb("h2T%d" % i, [128, 8, 128], BF16) for i in range(2)]
            sg_ = [sb("sg%d" % i, [128, 512]) for i in range(2)]
            hact_ = [sb("hact0", [128, DFF], BF16)] * 2
            actT_ = [sb("actT%d" % i, [128, 22, 128], BF16) for i in range(2)]
            st_ = [sb("st%d" % i, [128, 8]) for i in range(2)]
            self.S.local.update(["xa", "t1", "yt", "xn", "h2T", "sg", "actT", "st"])
            gpost = rowcd[:, 0:1024]
            gfpost = rowcd[:, 1024:2048]
            gcol2 = self.colp[:, COL_OFF["n_ffn_pre"]:COL_OFF["n_ffn_pre"] + 8]
            pf = lambda: (lambda i: (self.ps[i], "pf%d" % i))(self.rot("pf", 8))

            def rms_from_psum(pairs, ncol0, outs, grow, addin, t1, st):
                a, b = st[:, ncol0:ncol0 + 1], st[:, ncol0 + 1:ncol0 + 2]
                for h, (pb, pn) in enumerate(pairs):
                    self.act(t1[:, h * 512:(h + 1) * 512], pb[:], AF.Square, [pn], ["t1"])
                self.S.op("dve", lambda e: e.tensor_reduce(out=a, in_=t1[:], axis=mybir.AxisListType.X, op=ALU.add), reads=["t1"], writes=["st"])
                self.ts("dve", a, a, 1.0 / D, 1e-6, ALU.mult, ALU.add, ["st"], ["st"])
                self.act(a, a, AF.Sqrt, ["st"], ["st"])
                self.recip(b, a, ["st"], ["st"])
                for h, (pb, pn) in enumerate(pairs):
                    self.stt(t1[:, h * 512:(h + 1) * 512], pb[:], b, grow[:, h * 512:(h + 1) * 512], ALU.mult, ALU.mult,
                             [pn, "st", "rowcd"], ["t1"])
                self.tt("dve", outs, t1[:], addin, ALU.add, ["t1", "xa"], ["xa"])

            for i in range(NT):
                self.S.lane = i % 2
                xa, t1, yt, xn, h2T, sg, hact, actT, st = (xa_[i % 2], t1_[i % 2], yt_[i % 2], xn_[i % 2], h2T_[i % 2], sg_[i % 2],
                                                           hact_[i % 2], actT_[i % 2], st_[i % 2])
                self.dma(yt[:], self.yT_d[:, i * 128:(i + 1) * 128].rearrange("(k p) t -> p k t", p=128), ["yTd"], ["yt"], "yt%d" % (i % 2))
                self.dma(xa[:], src[i * 128:(i + 1) * 128, :], ["xres%d" % i], ["xa"], "xa%d" % (i % 2))
                pairs = []
                for nh in range(2):
                    pb, pn = pf()
                    for k in range(8):
                        self.mm(pb[:], yt[:, k, :], wo[:, k, nh * 512:(nh + 1) * 512], k == 0, k == 7, ["yt", "wo"], [pn])
                    pairs.append((pb, pn))
                rms_from_psum(pairs, 0, xa[:], gpost, xa[:], t1, st)
                self.act(t1[:], xa[:], AF.Square, ["xa"], ["t1"])
                self.S.op("dve", (lambda st_, t1_: lambda e: e.tensor_reduce(out=st_[:, 2:3], in_=t1_[:], axis=mybir.AxisListType.X, op=ALU.add))(st, t1), reads=["t1"], writes=["st"])
                self.ts("dve", st[:, 2:3], st[:, 2:3], 1.0 / D, 1e-6, ALU.mult, ALU.add, ["st"], ["st"])
                self.act(st[:, 2:3], st[:, 2:3], AF.Sqrt, ["st"], ["st"])
                self.recip(st[:, 3:4], st[:, 2:3], ["st"], ["st"])
                self.ts("dve", xn[:], xa[:], st[:, 3:4], None, ALU.mult, None, ["xa", "st"], ["xn"])
                pb, pn = pf()
                pbb = pb[:].bitcast(BF16)
                for k in range(8):
                    self.tr(pbb[:, k * 128:(k + 1) * 128], xn[:, k * 128:(k + 1) * 128], self.idb[:], ["xn", "consts2"], [pn])
                self.tt("dve", h2T[:], pbb[:, 0:1024].rearrange("p (k t) -> p k t", k=8),
                        gcol2.unsqueeze(2).to_broadcast([128, 8, 128]), ALU.mult, [pn, "colp"], ["h2T"])
                for j in range(6):
                    wj = 512 if j < 5 else 256
                    pg, pgn = pf()
                    for k in range(8):
                        self.mm(pg[:, 0:wj], h2T[:, k, :], fwi[:, k, j * 512:j * 512 + wj], k == 0, k == 7, ["h2T", "fwi"], [pgn])
                    pu, pun = pf()
                    for k in range(8):
                        self.mm(pu[:, 0:wj], h2T[:, k, :], fwi[:, k, DFF + j * 512:DFF + j * 512 + wj], k == 0, k == 7, ["h2T", "fwi"], [pun])
                    self.act(sg[:, 0:wj], pg[:, 0:wj], AF.Silu, [pgn], ["sg"])
                    self.tt("dve", hact[:, j * 512:j * 512 + wj], sg[:, 0:wj], pu[:, 0:wj], ALU.mult, ["sg", pun], ["hact"])
                for g in range(3):
                    k0 = g * 8
                    kn = min(8, 22 - k0)
                    pb, pn = pf()
                    pbb = pb[:].bitcast(BF16)
                    for k in range(kn):
                        self.tr(pbb[:, k * 128:(k + 1) * 128], hact[:, (k0 + k) * 128:(k0 + k + 1) * 128], self.idb[:], ["hact", "consts2"], [pn])
                    self.cp("act", actT[:, k0:k0 + kn, :], pbb[:, 0:kn * 128].rearrange("p (k t) -> p k t", k=kn), [pn], ["actT"])
                pairs = []
                for nh in range(2):
                    pb, pn = pf()
                    for k in range(22):
                        self.mm(pb[:], actT[:, k, :], fwo[:, k, nh * 512:(nh + 1) * 512], k == 0, k == 21, ["actT", "fwo"], [pn])
                    pairs.append((pb, pn))
                rms_from_psum(pairs, 4, xa[:], gfpost, xa[:], t1, st)
                self.dma(self.out_d[i * 128:(i + 1) * 128, :], xa[:], ["xa"], ["xres%d" % i], "xout%d" % (i % 2))
            self.S.lane = None


_PROG = {}


def _get_prog(NT, NL, dbg=False):
    key = (NT, NL, dbg)
    if key not in _PROG:
        kb = KB(NT, NL, dbg)
        _PROG[key] = kb.build()
    return _PROG[key]


def make_in_maps(inputs, NT, NL, ncores):
    packs = [pack_layer_params(inputs, l) for l in range(NL)]
    shared = {
        "w_in": np.ascontiguousarray(inputs["w_in"][:NL], dtype=np.float32),
        "w_out": np.ascontiguousarray(inputs["w_out"][:NL], dtype=np.float32),
        "ffn_w_in": np.ascontiguousarray(inputs["ffn_w_in"][:NL], dtype=np.float32),
        "ffn_w_out": np.ascontiguousarray(inputs["ffn_w_out"][:NL], dtype=np.float32),
        "rowp": np.stack([p[0] for p in packs]),
        "colp": np.stack([p[1] for p in packs]),
        "lru_gw": np.stack([p[2] for p in packs]),
        "rw_up": np.stack([p[3] for p in packs]),
        "consts": host_consts(),
    }
    x = np.asarray(inputs["x"], dtype=np.float32)
    maps = []
    for c in range(ncores):
        m = dict(shared)
        m["x"] = np.ascontiguousarray(x[c, :NT * 128])
        maps.append(m)
    return maps


def kernel(**inputs):
    inputs = {k: np.asarray(v) for k, v in inputs.items()}
    NT, NL, ncores = 32, 2, 8
    nc = _get_prog(NT, NL)
    maps = make_in_maps(inputs, NT, NL, ncores)
    res = run_bass_kernel_spmd(nc, maps, core_ids=list(range(ncores)))
    return np.stack([res.results[c]["out"] for c in range(ncores)]).astype(np.float32)
```

```python
import os
import numpy as np
from contextlib import ExitStack
import concourse.bass as bass
import concourse.mybir as mybir
from concourse.bass_utils import run_bass_kernel_spmd

F32 = mybir.dt.float32
BF16 = mybir.dt.bfloat16
AF = mybir.ActivationFunctionType
ALU = mybir.AluOpType

ENGS = ("pe", "act", "dve", "pool", "sp")
ENGATTR = {"pe": "tensor", "act": "scalar", "dve": "vector", "pool": "gpsimd", "sp": "sync"}


class _Op:
    __slots__ = ("eng", "fn", "waits", "signal", "dma_key", "dwaits")

    def __init__(self, eng, fn):
        self.eng = eng
        self.fn = fn
        self.waits = []
        self.dwaits = []
        self.signal = False
        self.dma_key = None


class Sched:
    def __init__(self):
        self.q = {e: [] for e in ENGS}
        self.last_w = {}
        self.readers = {}
        self.waited = {e: {p: -1 for p in ENGS} for e in ENGS}
        self.dwaited = {e: {} for e in ENGS}
        self.dma_cnt = {}
        self.lane = None
        self.local = set()

    def op(self, eng, fn, reads=(), writes=(), dma_key=None):
        if self.lane is not None:
            sfx = "@%d" % self.lane
            reads = [r + sfx if r in self.local else r for r in reads]
            writes = [w + sfx if w in self.local else w for w in writes]
        o = _Op(eng, fn)
        idx = len(self.q[eng])
        pr = [r for r in reads if r.startswith(("bank", "psb", "pf"))]
        if pr:
            writes = list(writes) + [r for r in pr if r not in writes]
        deps = []
        raw_same = -1
        for r in reads:
            lw = self.last_w.get(r)
            if lw is not None:
                deps.append(lw)
                if lw[0] == eng and lw[1] > raw_same:
                    raw_same = lw[1]
        for w in writes:
            lw = self.last_w.get(w)
            if lw is not None:
                deps.append(lw)
            rd = self.readers.get(w)
            if rd:
                deps.extend(rd.values())
        for d in deps:
            if d[0] == "dma":
                key = d[1]
                cnt = self.dma_cnt[key]
                if self.dwaited[eng].get(key, 0) < cnt:
                    self.dwaited[eng][key] = cnt
                    o.dwaits = [x for x in o.dwaits if x[0] != key]
                    o.dwaits.append((key, cnt))
            else:
                pe, pi = d
                if self.waited[eng][pe] < pi:
                    self.waited[eng][pe] = pi
                    o.waits = [x for x in o.waits if x[0] != pe]
                    o.waits.append((pe, pi))
        if dma_key is not None:
            self.dma_cnt[dma_key] = self.dma_cnt.get(dma_key, 0) + 16
            o.dma_key = dma_key
            me = ("dma", dma_key)
            mk = ("dma", dma_key)
        else:
            me = (eng, idx)
            mk = eng
        for r in reads:
            self.readers.setdefault(r, {})[mk] = me
        for w in writes:
            self.last_w[w] = me
            self.readers[w] = {}
        self.q[eng].append(o)
        return o

    def barrier(self):
        lasts = {}
        for p in ENGS:
            for i in range(len(self.q[p]) - 1, -1, -1):
                if self.q[p][i].dma_key is None and self.q[p][i].fn is not None:
                    lasts[p] = i
                    break
        for e in ENGS:
            o = _Op(e, None)
            for p, i in lasts.items():
                if p != e and self.waited[e][p] < i:
                    self.waited[e][p] = i
                    o.waits.append((p, i))
            for k, c in self.dma_cnt.items():
                if self.dwaited[e].get(k, 0) < c:
                    self.dwaited[e][k] = c
                    o.dwaits.append((k, c))
            self.q[e].append(o)

    def emit(self, nc, final_wait_keys=()):
        for e in ENGS:
            for o in self.q[e]:
                for (pe, pi) in o.waits:
                    self.q[pe][pi].signal = True
        cum = {}
        for e in ENGS:
            c = 0
            arr = []
            for o in self.q[e]:
                if o.signal and o.dma_key is None:
                    c += 1
                arr.append(c)
            cum[e] = arr
        with ExitStack() as es:
            esem = {e: es.enter_context(nc.semaphore("s_" + e)) for e in ENGS}
            dsem = {k: es.enter_context(nc.semaphore("d_" + str(k))) for k in self.dma_cnt}
            with nc.Block() as blk0:
                @blk0.sync
                def _(eng):
                    for s_ in list(esem.values()) + list(dsem.values()):
                        eng.sem_clear(s_)
            block = es.enter_context(nc.Block())

            def body(e):
                def _f(eng):
                    for o in self.q[e]:
                        for (pe, pi) in o.waits:
                            eng.wait_ge(esem[pe], cum[pe][pi])
                        for (k, cnt) in o.dwaits:
                            eng.wait_ge(dsem[k], cnt)
                        if o.fn is None:
                            continue
                        ins = o.fn(eng)
                        if o.dma_key is not None:
                            ins.then_inc(dsem[o.dma_key], 16)
                        elif o.signal:
                            ins.then_inc(esem[e], 1)
                    if e == "sp":
                        for k in final_wait_keys:
                            if k in dsem:
                                eng.wait_ge(dsem[k], self.dma_cnt[k])
                return _f

            for e in ENGS:
                getattr(block, ENGATTR[e])(body(e))


D = 1024
PIN = 3552
DFF = 2816
A0, B0, C0, D0 = 0, 512, 1552, 2592
NEGV = -30000.0

(C_ID, C_UT, C_LT, C_NINC_F, C_NSTR_F, C_NINC_B, C_NSTR_B, C_J, C_ONES, C_INC_F, C_STR_F, C_INC_B,
 C_STR_B) = range(13)
C_DNC_F = 13
C_DNC_B = 20
NCONST = 27


def host_consts():
    p = np.arange(128)[:, None]
    j = np.arange(128)[None, :]
    c = np.zeros((NCONST, 128, 128), np.float32)
    c[C_ID] = (p == j)
    c[C_UT] = (p <= j)
    c[C_LT] = (p >= j)
    c[C_NINC_F] = np.where(p <= j, 0.0, NEGV)
    c[C_NSTR_F] = np.where(p < j, 0.0, NEGV)
    c[C_NINC_B] = np.where(p >= j, 0.0, NEGV)
    c[C_NSTR_B] = np.where(p > j, 0.0, NEGV)
    c[C_J] = (p + j == 127)
    c[C_ONES] = 1.0
    c[C_INC_F] = (p <= j)
    c[C_STR_F] = (p < j)
    c[C_INC_B] = (p >= j)
    c[C_STR_B] = (p > j)
    for l in range(7):
        b = 1 << l
        same = (p // (2 * b)) == (j // (2 * b))
        c[C_DNC_F + l] = ((p == j) | (same & ((p % (2 * b)) >= b) & ((j % (2 * b)) < b)))
        c[C_DNC_B + l] = ((p == j) | (same & ((p % (2 * b)) < b) & ((j % (2 * b)) >= b)))
    return np.ascontiguousarray(c.transpose(1, 0, 2).reshape(128, NCONST * 128))


ROW_ITEMS = [("n_mix_post", 1024), ("n_ffn_post", 1024), ("lru_cw", 4 * 256), ("gdn_cw", 4 * 768),
             ("mu", 2 * 960), ("gdn_alog", 8), ("gdn_dtb", 8), ("gdn_norm", 64), ("ml_gb", 16),
             ("ml_norm", 256), ("rw_w0", 512), ("rw_a0", 512), ("rw_kk", 256), ("rw_ka", 256),
             ("rw_rk", 256), ("rw_gnw", 256), ("rw_gnb", 256)]
ROW_OFF = {}
_o = 0
for _n, _s in ROW_ITEMS:
    ROW_OFF[_n] = _o
    _o += _s
NROW = _o
ROW_BASE = 3072
ROWM = {k: v - ROW_BASE for k, v in ROW_OFF.items()}
COL_ITEMS = [("n_mix_pre", 8), ("n_ffn_pre", 8), ("lru_cb", 2), ("lru_gb", 8), ("lru_lam", 4)]
COL_OFF = {}
_o = 0
for _n, _s in COL_ITEMS:
    COL_OFF[_n] = _o
    _o += _s
NCOL = _o


def pack_layer_params(inp, l):
    row = np.concatenate([
        inp["norm_mix_post"][l], inp["norm_ffn_post"][l], inp["lru_conv_w"][l].reshape(-1),
        inp["gdn_conv_w"][l].reshape(-1), inp["rwkv_mu"][l].reshape(-1), inp["gdn_a_log"][l].reshape(-1),
        inp["gdn_dt_bias"][l].reshape(-1), inp["gdn_norm"][l],
        np.concatenate([inp["mlstm_gate_bias"][l][:, 0, :].reshape(-1), inp["mlstm_gate_bias"][l][:, 1, :].reshape(-1)]),
        inp["mlstm_norm"][l], inp["rwkv_w0"][l].reshape(-1), inp["rwkv_a0"][l].reshape(-1), inp["rwkv_k_k"][l],
        inp["rwkv_k_a"][l], inp["rwkv_r_k"][l].reshape(-1), inp["rwkv_gn_w"][l], inp["rwkv_gn_b"][l]]).astype(np.float32)
    assert row.shape[0] == NROW
    col = np.concatenate([
        inp["norm_mix_pre"][l].reshape(8, 128).T, inp["norm_ffn_pre"][l].reshape(8, 128).T,
        inp["lru_conv_b"][l].reshape(2, 128).T,
        inp["lru_gate_b"][l].reshape(4, 2, 128).transpose(2, 0, 1).reshape(128, 8),
        inp["lru_lambda"][l].reshape(2, 2, 128).transpose(2, 0, 1).reshape(128, 4)], axis=1).astype(np.float32)
    assert col.shape == (128, NCOL)
    gw = inp["lru_gate_w"][l]
    bd = np.zeros((2, 2, 2, 128, 128), np.float32)
    for d in range(2):
        for g in range(2):
            for n in range(4):
                ct, o = n // 2, (n % 2) * 64
                bd[d, g, ct, o:o + 64, o:o + 64] = gw[d, g, n]
    lru_gw = np.ascontiguousarray(bd.transpose(3, 0, 1, 2, 4).reshape(128, 8 * 128))
    rw_up = np.concatenate([inp["rwkv_w_up"][l].reshape(64, 256), inp["rwkv_a_up"][l].reshape(64, 256),
                            inp["rwkv_g_up"][l]], axis=1).astype(np.float32)
    return row[None, :], col, lru_gw, rw_up


class KB:
    def __init__(self, NT, NL, dbg=False):
        self.NT, self.NL, self.dbg = NT, NL, dbg
        self.SQ = NT * 128
        self.nc = bass.Bass("TRN2", target_bir_lowering=False)
        self.S = Sched()
        self.cnt = {}

    def mm(self, out, lhsT, rhs, start, stop, r, w):
        self.S.op("pe", lambda e: e.matmul(out, lhsT=lhsT, rhs=rhs, start=start, stop=stop), reads=r, writes=w)

    def tr(self, out, in_, ident, r, w):
        self.S.op("pe", lambda e: e.transpose(out=out, in_=in_, identity=ident), reads=list(r) + ["consts"], writes=w)

    def act(self, out, in_, func, r, w, bias=None, scale=None, accum=None):
        kw = {}
        if bias is not None:
            kw["bias"] = bias
        if scale is not None:
            kw["scale"] = scale
        if accum is not None:
            kw["accum_out"] = accum
        self.S.op("act", lambda e: e.activation(out=out, in_=in_, func=func, **kw), reads=r, writes=w)
        if accum is not None:
            self.S.op("act", lambda e: e.activation(out=accum, in_=accum, func=AF.Copy), reads=r, writes=w)

    def ts(self, eng, out, in0, s1, s2, op0, op1, r, w):
        if op1 is None:
            self.S.op(eng, lambda e: e.tensor_scalar(out=out, in0=in0, scalar1=s1, scalar2=None, op0=op0), reads=r, writes=w)
        else:
            self.S.op(eng, lambda e: e.tensor_scalar(out=out, in0=in0, scalar1=s1, scalar2=s2, op0=op0, op1=op1), reads=r, writes=w)

    def tt(self, eng, out, in0, in1, op, r, w):
        self.S.op(eng, lambda e: e.tensor_tensor(out=out, in0=in0, in1=in1, op=op), reads=r, writes=w)

    def stt(self, out, in0, scalar, in1, op0, op1, r, w):
        self.S.op("dve", lambda e: e.scalar_tensor_tensor(out=out, in0=in0, scalar=scalar, in1=in1, op0=op0, op1=op1), reads=r, writes=w)

    def cp(self, eng, out, in_, r, w):
        if eng == "act":
            self.S.op("act", lambda e: e.activation(out=out, in_=in_, func=AF.Copy), reads=r, writes=w)
        else:
            self.S.op(eng, lambda e: e.tensor_copy(out=out, in_=in_), reads=r, writes=w)

    def dma(self, out, in_, r, w, key, eng="sp"):
        self.S.op(eng, lambda e: e.dma_start(out=out, in_=in_), reads=r, writes=w, dma_key=key)

    def recip(self, out, in_, r, w):
        self.S.op("dve", lambda e: e.reciprocal(out=out, in_=in_), reads=r, writes=w)

    def rot(self, tag, n):
        i = self.cnt.get(tag, 0)
        self.cnt[tag] = i + 1
        return i % n

    def psb(self):
        i = self.rot("psb", 2)
        return self.ps[i], "psb%d" % i

    def psf(self):
        i = self.rot("psf", 4)
        return self.ps[i], ("psb%d" % i if i < 2 else "bank%d" % i)

    def psh(self):
        i = self.rot("psh", 4)
        b, hlf = 2 + i % 2, (i // 2) % 2
        return self.ps[b][:, hlf * 256:hlf * 256 + 256], "bank%d" % b

    def psq(self):
        i = self.rot("psq", 16)
        b, q = 4 + i % 4, (i // 4) % 4
        return self.ps[b][:, q * 128:q * 128 + 128], "bank%d" % b

    def cm(self, idx, np_=128):
        return self.consts[0:np_, idx * 128:(idx + 1) * 128]

    def build(self):
        nc, NT, NL, SQ = self.nc, self.NT, self.NL, self.SQ
        dt = nc.dram_tensor
        self.x_d = dt("x", [SQ, D], F32, kind="ExternalInput").ap()
        self.out_d = dt("out", [SQ, D], F32, kind="ExternalOutput").ap()
        self.win_d = dt("w_in", [NL, D, PIN], F32, kind="ExternalInput").ap()
        self.wout_d = dt("w_out", [NL, D, D], F32, kind="ExternalInput").ap()
        self.fwi_d = dt("ffn_w_in", [NL, D, 2 * DFF], F32, kind="ExternalInput").ap()
        self.fwo_d = dt("ffn_w_out", [NL, DFF, D], F32, kind="ExternalInput").ap()
        self.row_d = dt("rowp", [NL, 1, NROW], F32, kind="ExternalInput").ap()
        self.col_d = dt("colp", [NL, 128, NCOL], F32, kind="ExternalInput").ap()
        self.lgw_d = dt("lru_gw", [NL, 128, 1024], F32, kind="ExternalInput").ap()
        self.rup_d = dt("rw_up", [NL, 64, 768], F32, kind="ExternalInput").ap()
        self.cst_d = dt("consts", [128, NCONST * 128], F32, kind="ExternalInput").ap()
        self.yT_d = dt("yT_scr", [D, SQ], BF16, kind=("ExternalOutput" if self.dbg else "Internal")).ap()
        self.rkv_d = [dt("rkv_scr%d" % i, [SQ, 192], BF16, kind="Internal").ap() for i in range(2)]
        self.lrT_d = dt("lrT_scr", [3, 64, SQ], BF16, kind="Internal").ap()
        if self.dbg:
            self.ydbg_d = dt("ydbg", [D, SQ], BF16, kind="ExternalOutput").ap()
        with ExitStack() as es:
            self.es = es
            sb = lambda n, s, d=F32: es.enter_context(nc.sbuf_tensor(n, s, d))
            self.ps = [es.enter_context(nc.psum_tensor("ps%d" % i, [128, 512], F32)) for i in range(8)]
            self.idb = sb("idb", [128, 128], BF16)
            self.nidb = sb("nidb", [128, 128], BF16)
            idf = sb("idf", [128, 128])
            self.dma(idf[:], self.cst_d[:, C_ID * 128:(C_ID + 1) * 128], [], ["idf"], "cst0")
            self.cp("dve", self.idb[:], idf[:], ["idf"], ["consts2"])
            self.ts("dve", self.nidb[:], idf[:], -1.0, None, ALU.mult, None, ["idf"], ["consts2"])
            for l in range(NL):
                self.layer(l)
            self.S.emit(nc, final_wait_keys=["xout0", "xout1"])
        return nc

    def layer(self, l):
        nc, NT, SQ = self.nc, self.NT, self.SQ
        src = self.x_d if l == 0 else self.out_d
        with ExitStack() as ls:
            sb = lambda n, s, d=F32: ls.enter_context(nc.sbuf_tensor("L%d_%s" % (l, n), s, d))
            self.colp = sb("colp", [128, NCOL])
            self.dma(self.colp[:], self.col_d[l, :, :], [], ["colp"], "par")
            with ExitStack() as ms:
                msb = lambda n, s, d=F32: ms.enter_context(nc.sbuf_tensor("L%d_%s" % (l, n), s, d))
                self.msb = msb
                self.consts = msb("consts", [128, NCONST * 128])
                self.dma(self.consts[:], self.cst_d[:, :], [], ["consts"], "cst")
                self.hT = msb("hT", [128, 8, SQ + 4], BF16)
                self.alloc_common(msb)
                import os
                if not os.environ.get("SKIP_A"):
                    self.phase_a(l, src, msb)
                    self.S.barrier()
                skip = os.environ.get("SKIP_MIX", "")
                if "lru" not in skip:
                    self.mix_lru(l)
                self.S.barrier()
                self.alloc_common2(msb)
                self.rowp = msb("rowp", [128, NROW - ROW_BASE])
                self.dma(self.rowp[:], self.row_d[l, 0, ROW_BASE:NROW].partition_broadcast(128), [], ["rowp"], "par")
                for nm_, fn in (("gdn", self.mix_gdn), ("mlstm", self.mix_mlstm), ("rwkv", self.mix_rwkv)):
                    if nm_ in skip:
                        continue
                    fn(l)
                    self.S.barrier()
            self.S.barrier()
            if "cd" not in os.environ.get("SKIP_MIX", ""):
                self.phase_cd(l, src)
            self.S.barrier()

    def phase_a(self, l, src, msb):
        nc, NT = self.nc, self.NT
        with ExitStack() as ps_:
            sb = lambda n, s, d=F32: ps_.enter_context(nc.sbuf_tensor("A%d_%s" % (l, n), s, d))
            xa = [sb("xa%d" % i, [128, D]) for i in range(2)]
            junk = sb("junk", [128, D])
            xn = [sb("xn%d" % i, [128, D], BF16) for i in range(2)]
            st = sb("st", [128, 4 * NT])
            hT = self.hT
            self.S.op("pool", lambda e: e.memset(hT[:, :, 0:2], 0.0), writes=["hT"])
            self.S.op("pool", lambda e: e.memset(hT[:, :, self.SQ + 2:self.SQ + 4], 0.0), writes=["hT"])
            gcol = self.colp[:, COL_OFF["n_mix_pre"]:COL_OFF["n_mix_pre"] + 8]
            for i in range(NT):
                s = i % 2
                self.dma(xa[s][:], src[i * 128:(i + 1) * 128, :], ["xres%d" % i], ["xa%d" % s], "xa%d" % s)
                ss, ms_, rs = st[:, 4 * i:4 * i + 1], st[:, 4 * i + 1:4 * i + 2], st[:, 4 * i + 2:4 * i + 3]
                self.act(junk[:], xa[s][:], AF.Square, ["xa%d" % s], ["junk"])
                self.S.op("dve", (lambda ss_: lambda e: e.tensor_reduce(out=ss_, in_=junk[:], axis=mybir.AxisListType.X, op=ALU.add))(ss), reads=["junk"], writes=["st"])
                self.ts("dve", ms_, ss, 1.0 / D, 1e-6, ALU.mult, ALU.add, ["st"], ["st"])
                self.act(ms_, ms_, AF.Sqrt, ["st"], ["st"])
                self.recip(rs, ms_, ["st"], ["st"])
                self.ts("dve", xn[s][:], xa[s][:], rs, None, ALU.mult, None, ["xa%d" % s, "st"], ["xn%d" % s])
                pb, pn = self.psb()
                pbb = pb[:].bitcast(BF16)
                for k in range(8):
                    self.tr(pbb[:, k * 128:(k + 1) * 128], xn[s][:, k * 128:(k + 1) * 128], self.idb[:], ["xn%d" % s, "consts2"], [pn])
                self.tt("dve", hT[:, :, 2 + i * 128:2 + (i + 1) * 128], pbb[:, 0:1024].rearrange("p (k t) -> p k t", k=8),
                        gcol.unsqueeze(2).to_broadcast([128, 8, 128]), ALU.mult, [pn, "colp"], ["hT"])

    def load_w(self, dst, dst_name, src_ap, ncols, scale_row=None, kch=8):
        stg_i = self.rot("wstg", 2)
        stg = self.wstg[stg_i]
        nm = "wstg%d" % stg_i
        self.dma(stg[:, 0:kch, 0:ncols], src_ap.rearrange("(k p) c -> p k c", p=128), [], [nm], nm)
        if scale_row is None:
            self.cp("pool", dst, stg[:, 0:kch, 0:ncols], [nm], [dst_name])
        else:
            self.tt("pool", dst, stg[:, 0:kch, 0:ncols], scale_row.unsqueeze(1).to_broadcast([128, kch, ncols]),
                    ALU.mult, [nm, "rowp", "mus"], [dst_name])

    def u_tm(self, ps_out, pn, i, taps, wname, ncols):
        n = len(taps) * 8
        c = 0
        for (sh, wt) in taps:
            for k in range(8):
                self.mm(ps_out, self.hT[:, k, 2 + i * 128 + sh:2 + i * 128 + sh + 128], wt[:, k, 0:ncols],
                        c == 0, c == n - 1, ["hT", wname], [pn])
                c += 1

    def expmat(self, out, on, gl, rev, strict, bias, r, dl=None, rows=128):
        rn = self.rot("erhs", 6)
        rhs1 = self.erhs[rn]
        rnm = "erhs%d" % rn
        tri = self.cm(C_LT if rev else C_UT)
        self.ts("dve", rhs1[:], tri, gl, None, ALU.mult, None, list(r) + ["consts"], [rnm])
        if dl is not None:
            self.stt(rhs1[:], self.cm(C_ID), dl, rhs1[:], ALU.mult, ALU.add, list(r) + ["consts", rnm], [rnm])
        pq, pn = self.psq()
        neg = self.cm({(0, 0): C_NINC_F, (0, 1): C_NSTR_F, (1, 0): C_NINC_B, (1, 1): C_NSTR_B}[(int(rev), int(strict))])
        self.mm(pq[0:rows, :], self.cm(C_ONES)[:, 0:rows], rhs1[:], True, False, [rnm, "consts"], [pn])
        self.mm(pq[0:rows, :], self.cm(C_ID)[:, 0:rows], neg, False, True, ["consts"], [pn])
        self.act(out, pq[0:rows, :], AF.Exp, [pn] + list(r), [on], bias=bias)

    def bcast_ps(self, gl, rev, r, dl=None):
        rn = self.rot("erhs", 6)
        rhs1 = self.erhs[rn]
        rnm = "erhs%d" % rn
        tri = self.cm(C_LT if rev else C_UT)
        self.ts("dve", rhs1[:], tri, gl, None, ALU.mult, None, list(r) + ["consts"], [rnm])
        if dl is not None:
            self.stt(rhs1[:], self.cm(C_ID), dl, rhs1[:], ALU.mult, ALU.add, list(r) + ["consts", rnm], [rnm])
        pq, pn = self.psq()
        self.mm(pq, self.cm(C_ONES), rhs1[:], True, True, [rnm, "consts"], [pn])
        return pq, pn

    def exp_from_bc(self, out, on, pq, pn, bias, clamp, rev, strict, r):
        self.ts("dve", out, pq, bias, clamp, ALU.add, ALU.min, [pn] + list(r), [on])
        self.act(out, out, AF.Exp, [on], [on])
        mk = self.cm({(0, 0): C_INC_F, (0, 1): C_STR_F, (1, 0): C_INC_B, (1, 1): C_STR_B}[(int(rev), int(strict))])
        self.tt("pool", out, out, mk, ALU.mult, [on, "consts"], [on])

    def rowexp(self, out, on, gl, rev, r, dl=None, rows=64):
        rn = self.rot("erhs", 6)
        rhs1 = self.erhs[rn]
        rnm = "erhs%d" % rn
        tri = self.cm(C_LT if rev else C_UT)
        self.ts("dve", rhs1[:], tri, gl, None, ALU.mult, None, list(r) + ["consts"], [rnm])
        if dl is not None:
            self.stt(rhs1[:], self.cm(C_ID), dl, rhs1[:], ALU.mult, ALU.add, list(r) + ["consts", rnm], [rnm])
        pq, pn = self.psq()
        self.mm(pq[0:rows, :], self.cm(C_ONES)[:, 0:rows], rhs1[:], True, True, [rnm, "consts"], [pn])
        self.act(out, pq[0:rows, :], AF.Exp, [pn], [on])

    def pipeline(self, units, nlanes, delay):
        active = []
        nxt, rnd = 0, 0
        busy = set()
        while nxt < len(units) or active:
            if nxt < len(units) and rnd % delay == 0 and (nxt % nlanes) not in busy:
                lane = nxt % nlanes
                active.append((lane, units[nxt](lane)))
                busy.add(lane)
                nxt += 1
            for item in list(active):
                lane, g = item
                self.S.lane = lane
                try:
                    next(g)
                except StopIteration:
                    active.remove(item)
                    busy.discard(lane)
            self.S.lane = None
            rnd += 1

    def dnc_gen(self, Bm, bn, rev):
        lane = self.S.lane or 0
        X, W = self.idb, self.idb
        xn, wn = "consts2", "consts2"
        for lv in range(7):
            pq, pn = self.psq()
            self.mm(pq, Bm, X[:], True, True, [bn, xn], [pn])
            Q, qn = self.ldq[lane][lv % 2], "dq%d" % (lv % 2)
            msk = self.cm((C_DNC_B if rev else C_DNC_F) + lv)
            self.stt(Q[:], pq, -1.0, msk, ALU.mult, ALU.mult, [pn, "consts"], [qn])
            yield
            if lv < 6:
                Xn, xnn = self.ldx[lane][lv % 2], "dx%d" % (lv % 2)
                if lv == 0:
                    self.tt("pool", Xn[:], Q[:], self.idb[:], ALU.add, [qn, "consts2"], [xnn])
                else:
                    pq2, pn2 = self.psq()
                    self.mm(pq2, W[:], Q[:], True, True, [wn, qn], [pn2])
                    self.tt("dve", Xn[:], pq2, X[:], ALU.add, [pn2, xn], [xnn])
            pq3, pn3 = self.psq()
            self.mm(pq3, Q[:], W[:], True, True, [wn, qn], [pn3])
            Wn, wnn = self.ldw[lane][lv % 2], "dw%d" % (lv % 2)
            self.tt("dve", Wn[:], pq3, W[:], ALU.add, [pn3, wn], [wnn])
            W, wn = Wn, wnn
            if lv < 6:
                X, xn = Xn, xnn
            yield
        return W, wn

    def dnc(self, Bm, bn, rev):
        X, W = self.idb, self.idb
        xn, wn = "consts2", "consts2"
        for lv in range(7):
            pq, pn = self.psq()
            self.mm(pq, Bm, X[:], True, False, [bn, xn], [pn])
            self.mm(pq, self.nidb[:], self.idb[:], False, True, ["consts2"], [pn])
            qi = self.rot("dq", 2)
            Q, qn = self.dq[qi], "dq%d" % qi
            msk = self.cm((C_DNC_B if rev else C_DNC_F) + lv)
            self.stt(Q[:], pq, -1.0, msk, ALU.mult, ALU.mult, [pn, "consts"], [qn])
            if lv < 6:
                pq2, pn2 = self.psq()
                self.mm(pq2, W[:], Q[:], True, True, [wn, qn], [pn2])
                xi = self.rot("dx", 2)
                Xn, xnn = self.dx[xi], "dx%d" % xi
                self.cp("act", Xn[:], pq2, [pn2], [xnn])
            pq3, pn3 = self.psq()
            self.mm(pq3, Q[:], W[:], True, True, [wn, qn], [pn3])
            wi = self.rot("dw", 2)
            Wn, wnn = self.dw[wi], "dw%d" % wi
            self.cp("dve" if lv % 2 else "act", Wn[:], pq3, [pn3], [wnn])
            W, wn = Wn, wnn
            if lv < 6:
                X, xn = Xn, xnn
        return W, wn

    def alloc_common(self, msb):
        self.wstg = [msb("wstg%d" % i, [128, 8, 192]) for i in range(2)]

    def alloc_common2(self, msb):
        self.erhs = [msb("erhs%d" % i, [128, 128]) for i in range(6)]
        self.dq = [msb("dq%d" % i, [128, 128], BF16) for i in range(2)]
        self.dx = [msb("dx%d" % i, [128, 128], BF16) for i in range(2)]
        self.dw = [msb("dw%d" % i, [128, 128], BF16) for i in range(2)]
        self.NLANE = 4
        self.ldq = [[msb("ldq%d_%d" % (a, i), [128, 128], BF16) for i in range(2)] for a in range(self.NLANE)]
        self.ldx = [[msb("ldx%d_%d" % (a, i), [128, 128], BF16) for i in range(2)] for a in range(self.NLANE)]
        self.ldw = [[msb("ldw%d_%d" % (a, i), [128, 128], BF16) for i in range(2)] for a in range(self.NLANE)]
        self.S.local.update(["dq0", "dq1", "dx0", "dx1", "dw0", "dw1"])

    def mix_lru(self, l):
        nc, NT, SQ = self.nc, self.NT, self.SQ
        TB = min(512, SQ)
        NB = SQ // TB
        with ExitStack() as xs:
            sb = lambda n, s, d=F32: xs.enter_context(nc.sbuf_tensor("U%d_%s" % (l, n), s, d))
            cwr = sb("cwr", [128, 1024])
            wx = [sb("wx%d" % j, [128, 8, 128], BF16) for j in range(4)]
            wy = sb("wy", [128, 8, 128], BF16)
            gstg = sb("gstg", [128, 1024]); gwb = sb("gwb", [128, 8, 128], BF16)
            cc = sb("cc", [128, 12])
            Xc = sb("Xc", [128, SQ]); A1 = sb("A1", [128, SQ]); B1 = sb("B1", [128, SQ]); Hf = sb("Hf", [128, SQ]); Rv = sb("Rv", [128, SQ])
            xbb = [sb("xbb%d" % i, [128, TB], BF16) for i in range(2)]
            rr = sb("rr", [128, TB]); ii = sb("ii", [128, TB]); t5 = sb("t5", [128, TB])
            tT = [sb("tT%d" % i, [128, 128]) for i in range(2)]
            yst = Rv[:].bitcast(BF16)
            self.dma(cwr[:], self.row_d[l, 0, ROW_OFF["lru_cw"]:ROW_OFF["lru_cw"] + 1024].partition_broadcast(128), [], ["cwr"], "par")
            self.dma(gstg[:], self.lgw_d[l, :, :], [], ["gstg"], "par")
            self.cp("pool", gwb[:].rearrange("p a b -> p (a b)"), gstg[:], ["gstg"], ["gwb"])
            lam = self.colp[:, COL_OFF["lru_lam"]:COL_OFF["lru_lam"] + 4]
            self.act(cc[:, 0:4], lam, AF.Exp, ["colp"], ["cc"], scale=-1.0)
            self.act(cc[:, 0:4], cc[:, 0:4], AF.Ln, ["cc"], ["cc"], bias=1.0)
            self.ts("dve", cc[:, 4:8], cc[:, 0:4], -8.0, None, ALU.mult, None, ["cc"], ["cc"])
            self.ts("dve", cc[:, 8:12], cc[:, 0:4], -16.0, None, ALU.mult, None, ["cc"], ["cc"])

            def reverse_into(dst, dn, srct, sn, add=False):
                for b in range(NT):
                    ti = self.rot("lrt", 2)
                    pq, pn = self.psq()
                    self.tr(pq, srct[:, b * 128:(b + 1) * 128], self.cm(C_ID), [sn], [pn])
                    self.cp("act", tT[ti][:], pq, [pn], ["tT%d" % ti])
                    pq2, pn2 = self.psq()
                    self.mm(pq2, tT[ti][:], self.cm(C_J), True, True, ["tT%d" % ti, "consts"], [pn2])
                    o = dst[:, (NT - 1 - b) * 128:(NT - b) * 128]
                    if add:
                        self.tt("dve", o, pq2, o, ALU.add, [pn2, dn], [dn])
                    else:
                        self.cp("dve", o, pq2, [pn2], [dn])

            for ct in range(2):
                for j in range(4):
                    self.load_w(wx[j][:], "lr_wx%d" % j, self.win_d[l, :, A0 + ct * 128:A0 + (ct + 1) * 128], 128,
                                scale_row=cwr[:, j * 256 + ct * 128:j * 256 + (ct + 1) * 128])
                self.load_w(wy[:], "lr_wy", self.win_d[l, :, A0 + 256 + ct * 128:A0 + 256 + (ct + 1) * 128], 128)
                cb = self.colp[:, COL_OFF["lru_cb"] + ct:COL_OFF["lru_cb"] + ct + 1]
                for nb in range(NB):
                    pf, pfn = self.psf()
                    c_ = 0
                    for j in range(4):
                        for k in range(8):
                            self.mm(pf[:, 0:TB], wx[j][:, k, :], self.hT[:, k, nb * TB + j:nb * TB + j + TB], c_ == 0, c_ == 31,
                                    ["hT", "lr_wx%d" % j, "cwr"], [pfn])
                            c_ += 1
                    self.act(Xc[:, nb * TB:(nb + 1) * TB], pf[:, 0:TB], AF.Identity, [pfn, "colp"], ["Xc"], bias=cb)
                for d in range(2):
                    if d == 1:
                        reverse_into(Rv, "Rv", Xc, "Xc")
                    xs_, xsn = (Xc, "Xc") if d == 0 else (Rv, "Rv")
                    gb_r = self.colp[:, COL_OFF["lru_gb"] + (d * 2 + 0) * 2 + ct:COL_OFF["lru_gb"] + (d * 2 + 0) * 2 + ct + 1]
                    gb_i = self.colp[:, COL_OFF["lru_gb"] + (d * 2 + 1) * 2 + ct:COL_OFF["lru_gb"] + (d * 2 + 1) * 2 + ct + 1]
                    c8 = cc[:, 4 + d * 2 + ct:4 + d * 2 + ct + 1]
                    c16 = cc[:, 8 + d * 2 + ct:8 + d * 2 + ct + 1]
                    for nb in range(NB):
                        blk = slice(nb * TB, (nb + 1) * TB)
                        xi = self.rot("xbb", 2)
                        self.cp("pool", xbb[xi][:], xs_[:, blk], [xsn], ["xbb%d" % xi])
                        pr, prn = self.psf()
                        self.mm(pr[:, 0:TB], gwb[:, (d * 2 + 0) * 2 + ct, :], xbb[xi][:], True, True, ["gwb", "xbb%d" % xi], [prn])
                        self.act(rr[:], pr[:, 0:TB], AF.Sigmoid, [prn, "colp"], ["rr"], bias=gb_r)
                        pi_, pin = self.psf()
                        self.mm(pi_[:, 0:TB], gwb[:, (d * 2 + 1) * 2 + ct, :], xbb[xi][:], True, True, ["gwb", "xbb%d" % xi], [pin])
                        self.act(ii[:], pi_[:, 0:TB], AF.Sigmoid, [pin, "colp"], ["ii"], bias=gb_i)
                        self.act(A1[:, blk], rr[:], AF.Exp, ["rr", "cc"], ["A1"], scale=c8)
                        self.act(t5[:], rr[:], AF.Exp, ["rr", "cc"], ["t5"], scale=c16)
                        self.act(t5[:], t5[:], AF.Sqrt, ["t5"], ["t5"], scale=-1.0, bias=1.0)
                        self.tt("dve", t5[:], t5[:], ii[:], ALU.mult, ["t5", "ii"], ["t5"])
                        self.tt("dve", B1[:, blk], t5[:], xs_[:, blk], ALU.mult, ["t5", xsn], ["B1"])
                    dst, dn = (Hf, "Hf") if d == 0 else (Xc, "Xc")
                    self.S.op("dve", (lambda dst_: lambda e: e.tensor_tensor_scan(out=dst_[:], data0=A1[:], data1=B1[:], initial=0.0, op0=ALU.mult, op1=ALU.add))(dst),
                              reads=["A1", "B1"], writes=[dn])
                    if d == 1:
                        reverse_into(Hf, "Hf", Xc, "Xc", add=True)
                for nb in range(NB):
                    blk = slice(nb * TB, (nb + 1) * TB)
                    pf, pfn = self.psf()
                    for k in range(8):
                        self.mm(pf[:, 0:TB], wy[:, k, :], self.hT[:, k, 2 + nb * TB:2 + nb * TB + TB], k == 0, k == 7, ["hT", "lr_wy"], [pfn])
                    self.act(A1[:, blk], pf[:, 0:TB], AF.Gelu_apprx_tanh, [pfn], ["A1"])
                    self.tt("dve", yst[:, blk], Hf[:, blk], A1[:, blk], ALU.mult, ["Hf", "A1"], ["Rv"])
                self.dma(self.yT_d[ct * 128:(ct + 1) * 128, :], yst[:, 0:SQ], ["Rv"], ["yTd"], "ytw")

    def mix_gdn(self, l):
        nc, NT, SQ = self.nc, self.NT, self.SQ
        with ExitStack() as xs:
            sb = lambda n, s, d=F32: xs.enter_context(nc.sbuf_tensor("G%d_%s" % (l, n), s, d))
            wg = sb("wg", [128, 8, 16], BF16)
            wq4 = [sb("wq4_%d" % j, [128, 8, 192], BF16) for j in range(4)]
            wz = sb("wz", [128, 8, 64], BF16)
            g_all = sb("g", [128, NT, 8]); lnb_all = sb("lnb", [128, NT, 8]); beta_all = sb("beta", [128, NT, 8])
            gcts = [sb("gct%d" % i, [128, 8]) for i in range(3)]; ngc_all = sb("ngc", [128, NT, 8]); coef_all = sb("coef", [128, NT, 8])
            kdec_all = sb("kdec", [128, NT, 8]); egend_all = sb("egend", [128, NT, 8])
            nea = sb("nea", [128, 8]); t8s = [sb("t8_%d" % i, [128, 8]) for i in range(3)]; u8s = [sb("u8_%d" % i, [128, 8]) for i in range(3)]
            qkv_fs = [sb("qkv_f%d" % i, [128, 192]) for i in (0, 1)]; sqts = [sb("sqt%d" % i, [128, 128]) for i in (0, 1)]; sm = sb("sm", [128, 8]); smps = [sb("smp%d" % i, [128, 4]) for i in (0, 1)]
            qbs = [sb("qb%d" % i, [128, 64], BF16) for i in (0, 1)]
            k_tm = sb("k_tm", [128, NT, 64], BF16); v_tm = sb("v_tm", [128, NT, 64], BF16)
            qT = sb("qT", [64, SQ], BF16); kT = sb("kT", [64, SQ], BF16)
            sz = sb("sz", [128, NT, 64], BF16); oaccs = [sb("oacc0", [128, NT, 64]), sb("oacc1", [128, NT, 64], BF16)]
            Em = [sb("Em%d" % i, [128, 128]) for i in range(3)]
            Ebm = [sb("Ebm%d" % i, [128, 128]) for i in range(3)]
            Bm = [sb("Bm%d" % i, [128, 128], BF16) for i in range(3)]
            QKm = [sb("QKm%d" % i, [128, 128], BF16) for i in range(3)]
            ebr = [sb("ebr%d" % i, [128, 128]) for i in range(3)]
            qTs = [sb("qTs%d" % i, [64, 128], BF16) for i in range(3)]
            rv = [sb("rv%d" % i, [128, 64], BF16) for i in range(3)]
            rk = [sb("rk%d" % i, [128, 64], BF16) for i in range(3)]
            Kh = [sb("Kh%d" % i, [128, 64], BF16) for i in range(3)]
            nW = [sb("nW%d" % i, [64, 128], BF16) for i in range(3)]
            vn = [sb("vn%d" % i, [128, 64], BF16) for i in range(3)]
            P_fd = [sb("P_f%d" % i, [64, 64]) for i in (0, 1)]; P_bd = [sb("P_b%d" % i, [64, 64], BF16) for i in (0, 1)]
            sqs = [sb("sq%d" % i, [128, 64]) for i in range(3)]; ytms = [sb("ytm%d" % i, [128, 64], BF16) for i in range(3)]; fsms = [sb("fsm%d" % i, [128, 4]) for i in range(3)]
            yTs2 = [sb("yTs2_%d" % i, [64, 128], BF16) for i in range(3)]
            self.load_w(wg[:], "gd_wg", self.win_d[l, :, B0 + 1024:B0 + 1040], 16)
            dtb = self.rowp[:, ROWM["gdn_dtb"]:ROWM["gdn_dtb"] + 8]
            alog = self.rowp[:, ROWM["gdn_alog"]:ROWM["gdn_alog"] + 8]
            self.act(nea[:], alog, AF.Exp, ["rowp"], ["nea"])
            self.ts("dve", nea[:], nea[:], -1.0, None, ALU.mult, None, ["nea"], ["nea"])
            def gate(lane, i):
                t8, u8, gct = t8s[lane], u8s[lane], gcts[lane]
                pq, pn = self.psq()
                self.u_tm(pq[:, 0:16], pn, i, [(0, wg)], "gd_wg", 16)
                self.tt("dve", t8[:], pq[:, 0:8], dtb, ALU.add, [pn, "rowp"], ["t8_%d" % lane])
                self.act(t8[:], t8[:], AF.Exp, ["t8_%d" % lane], ["t8_%d" % lane])
                self.act(t8[:], t8[:], AF.Ln, ["t8_%d" % lane], ["t8_%d" % lane], bias=1.0)
                self.tt("dve", g_all[:, i, :], t8[:], nea[:], ALU.mult, ["t8_%d" % lane, "nea"], ["gdg%d" % i])
                yield
                self.act(u8[:], pq[:, 8:16], AF.Exp, [pn], ["u8_%d" % lane], scale=-1.0)
                self.act(u8[:], u8[:], AF.Ln, ["u8_%d" % lane], ["u8_%d" % lane], bias=1.0)
                self.ts("dve", lnb_all[:, i, :], u8[:], -1.0, None, ALU.mult, None, ["u8_%d" % lane], ["gdg%d" % i])
                self.act(beta_all[:, i, :], lnb_all[:, i, :], AF.Exp, ["gdg%d" % i], ["gdg%d" % i])
                yield
                pq, pn = self.psq()
                self.mm(pq[:, 0:4], self.cm(C_UT), g_all[:, i, 0:4], True, True, ["gdg%d" % i, "consts"], [pn])
                self.mm(pq[:, 4:8], self.cm(C_LT), g_all[:, i, 4:8], True, True, ["gdg%d" % i, "consts"], [pn])
                self.mm(pq[:, 8:16], self.cm(C_ONES), g_all[:, i, 0:8], True, True, ["gdg%d" % i, "consts"], [pn])
                self.cp("dve", gct[:], pq[:, 0:8], [pn], ["gdg%d" % i])
                self.ts("dve", ngc_all[:, i, :], gct[:], -1.0, None, ALU.mult, None, ["gdg%d" % i], ["gdg%d" % i])
                yield
                self.act(t8[:], gct[:], AF.Exp, ["gdg%d" % i], ["t8_%d" % lane])
                self.tt("dve", coef_all[:, i, :], t8[:], beta_all[:, i, :], ALU.mult, ["t8_%d" % lane, "gdg%d" % i], ["gdg%d" % i])
                self.tt("dve", u8[:], pq[:, 8:16], gct[:], ALU.subtract, [pn, "gdg%d" % i], ["u8_%d" % lane])
                self.act(kdec_all[:, i, :], u8[:], AF.Exp, ["u8_%d" % lane], ["gdg%d" % i])
                self.act(egend_all[:, i, :], pq[:, 8:16], AF.Exp, [pn], ["gdg%d" % i])
            self.pipeline([(lambda lane, i=i: gate(lane, i)) for i in range(NT)], 3, 1)
            cw0 = ROWM["gdn_cw"]
            gnorm = self.rowp[:, ROWM["gdn_norm"]:ROWM["gdn_norm"] + 64]
            for h in range(4):
                for j in range(4):
                    for b3 in range(3):
                        c0 = b3 * 256 + h * 64
                        self.load_w(wq4[j][:, :, b3 * 64:(b3 + 1) * 64], "gd_wq%d" % j, self.win_d[l, :, B0 + c0:B0 + c0 + 64], 64,
                                    scale_row=self.rowp[:, cw0 + j * 768 + c0:cw0 + j * 768 + c0 + 64])
                self.load_w(wz[:], "gd_wz", self.win_d[l, :, B0 + 768 + h * 64:B0 + 768 + (h + 1) * 64], 64)
                def prep(lane, i, h=h):
                    qkv_f, sqt, smp, qb = qkv_fs[lane], sqts[lane], smps[lane], qbs[lane]
                    ph, pn = self.psh()
                    taps = [(j - 2, wq4[j]) for j in range(4)]
                    n = 32
                    c_ = 0
                    for j in range(4):
                        for k in range(8):
                            self.mm(ph[:, 0:192], self.hT[:, k, i * 128 + j:i * 128 + j + 128], wq4[j][:, k, :], c_ == 0, c_ == n - 1,
                                    ["hT", "gd_wq%d" % j], [pn])
                            c_ += 1
                    yield
                    self.act(qkv_f[:], ph[:, 0:192], AF.Silu, [pn], ["qkv_f%d" % lane])
                    self.act(sqt[:], qkv_f[:, 0:128], AF.Square, ["qkv_f%d" % lane], ["sqt%d" % lane])
                    self.S.op("dve", lambda e: e.tensor_reduce(out=smp[:, 0:2], in_=sqt[:].rearrange("p (a b) -> p a b", a=2), axis=mybir.AxisListType.X, op=ALU.add), reads=["sqt%d" % lane], writes=["smp%d" % lane])
                    self.ts("dve", smp[:, 0:2], smp[:, 0:2], 1e-6, None, ALU.add, None, ["smp%d" % lane], ["smp%d" % lane])
                    self.act(smp[:, 0:2], smp[:, 0:2], AF.Sqrt, ["smp%d" % lane], ["smp%d" % lane])
                    self.recip(smp[:, 2:4], smp[:, 0:2], ["smp%d" % lane], ["smp%d" % lane])
                    self.ts("dve", qb[:], qkv_f[:, 0:64], smp[:, 2:3], 0.125, ALU.mult, ALU.mult, ["qkv_f%d" % lane, "smp%d" % lane], ["qb%d" % lane])
                    self.ts("dve", k_tm[:, i, :], qkv_f[:, 64:128], smp[:, 3:4], None, ALU.mult, None, ["qkv_f%d" % lane, "smp%d" % lane], ["k_tm%d" % i])
                    self.cp("pool", v_tm[:, i, :], qkv_f[:, 128:192], ["qkv_f%d" % lane], ["v_tm%d" % i])
                    yield
                    pq, pqn = self.psq()
                    pqb = pq.bitcast(BF16)
                    self.tr(pqb[0:64, 0:128], qb[:], self.idb[:], ["qb%d" % lane, "consts2"], [pqn])
                    self.tr(pqb[0:64, 128:256], k_tm[:, i, :], self.idb[:], ["k_tm%d" % i, "consts2"], [pqn])
                    self.cp("act", qT[:, i * 128:(i + 1) * 128], pqb[0:64, 0:128], [pqn], ["qT%d" % i])
                    self.cp("dve", kT[:, i * 128:(i + 1) * 128], pqb[0:64, 128:256], [pqn], ["kT%d" % i])
                    yield
                    pq, pn = self.psq()
                    self.u_tm(pq[:, 0:64], pn, i, [(0, wz)], "gd_wz", 64)
                    self.act(sz[:, i, :], pq[:, 0:64], AF.Silu, [pn], ["sz%d" % i])
                self.pipeline([(lambda lane, i=i: prep(lane, i)) for i in range(NT)], 2, 1)
                for d in range(2):
                    self.S.op("pool", (lambda t_: lambda e: e.memset(t_[:], 0.0))(P_fd[d]), writes=["P_f%d" % d])
                    self.S.op("pool", (lambda t_: lambda e: e.memset(t_[:], 0.0))(P_bd[d]), writes=["P_b%d" % d])
                done = {0: -1, 1: -1}

                def unit(lane, d, c, pos, h=h, done=done):
                    col = d * 4 + h
                    P_f, P_b = P_fd[d], P_bd[d]
                    pfn_, pbn_ = "P_f%d" % d, "P_b%d" % d
                    blk = slice(c * 128, (c + 1) * 128)
                    gl = g_all[:, c, col:col + 1]
                    ngc = ngc_all[:, c, col:col + 1]
                    r = lane
                    pqA, pnA = self.bcast_ps(gl, d, ["gdg%d" % c])
                    self.exp_from_bc(Em[r][:], "Em%d" % r, pqA, pnA, ngc, 0.0, d, False, ["gdg%d" % c])
                    self.act(ebr[r][:], pqA, AF.Exp, [pnA], ["ebr%d" % r])
                    pqB, pnB = self.bcast_ps(gl, d, ["gdg%d" % c], dl=lnb_all[:, c, col:col + 1])
                    self.exp_from_bc(Ebm[r][:], "Ebm%d" % r, pqB, pnB, ngc, 0.0, d, True, ["gdg%d" % c])
                    yield
                    pq, pn = self.psq()
                    self.mm(pq, kT[:, blk], kT[:, blk], True, True, ["kT%d" % c], [pn])
                    self.tt("dve", Bm[r][:], pq, Ebm[r][:], ALU.mult, [pn, "Ebm%d" % r], ["Bm%d" % r])
                    yield
                    pq, pn = self.psq()
                    self.mm(pq, kT[:, blk], qT[:, blk], True, True, ["kT%d" % c, "qT%d" % c], [pn])
                    self.tt("dve", QKm[r][:], pq, Em[r][:], ALU.mult, [pn, "Em%d" % r], ["QKm%d" % r])
                    W, wn = yield from self.dnc_gen(Bm[r][:], "Bm%d" % r, d)
                    yield
                    self.ts("pool", rv[r][:], v_tm[:, c, :], beta_all[:, c, col:col + 1], None, ALU.mult, None, ["v_tm%d" % c, "gdg%d" % c], ["rv%d" % r])
                    self.ts("pool", rk[r][:], k_tm[:, c, :], coef_all[:, c, col:col + 1], None, ALU.mult, None, ["k_tm%d" % c, "gdg%d" % c], ["rk%d" % r])
                    self.ts("pool", Kh[r][:], k_tm[:, c, :], kdec_all[:, c, col:col + 1], None, ALU.mult, None, ["k_tm%d" % c, "gdg%d" % c], ["Kh%d" % r])
                    pq, pn = self.psq()
                    self.mm(pq[0:64, :], rk[r][:], W[:], True, True, ["rk%d" % r, wn], [pn])
                    self.act(nW[r][:], pq[0:64, :], AF.Identity, [pn], ["nW%d" % r], scale=-1.0)
                    while done[d] != pos - 1:
                        yield
                    pq2, pn2 = self.psq()
                    self.mm(pq2[:, 0:64], W[:], rv[r][:], True, False, [wn, "rv%d" % r], [pn2])
                    self.mm(pq2[:, 0:64], nW[r][:], P_b[:], False, True, ["nW%d" % r, pbn_], [pn2])
                    self.cp("act", vn[r][:], pq2[:, 0:64], [pn2], ["vn%d" % r])
                    yield
                    self.tt("pool", qTs[r][:], qT[:, blk], ebr[r][0:64, :], ALU.mult, ["qT%d" % c, "ebr%d" % r], ["qTs%d" % r])
                    pq3, pn3 = self.psq()
                    self.mm(pq3[:, 0:64], qTs[r][:], P_b[:], True, False, ["qTs%d" % r, pbn_], [pn3])
                    self.mm(pq3[:, 0:64], QKm[r][:], vn[r][:], False, True, ["QKm%d" % r, "vn%d" % r], [pn3])
                    self.cp("dve", oaccs[d][:, c, :], pq3[:, 0:64], [pn3], ["oacc%d" % d])
                    yield
                    pq4, pn4 = self.psq()
                    self.mm(pq4[0:64, 0:64], Kh[r][:], vn[r][:], True, True, ["Kh%d" % r, "vn%d" % r], [pn4])
                    self.stt(P_f[:], P_f[:], egend_all[0:64, c, col:col + 1], pq4[0:64, 0:64], ALU.mult, ALU.add, [pfn_, "gdg%d" % c, pn4], [pfn_])
                    self.cp("act", P_b[:], P_f[:], [pfn_], [pbn_])
                    done[d] = pos

                units = []
                for i in range(NT):
                    units.append((lambda lane, i=i: unit(lane, 0, i, i)))
                    units.append((lambda lane, i=i: unit(lane, 1, NT - 1 - i, i)))
                self.pipeline(units, 3, 3)
                oacc = oaccs[0]
                for i in range(NT):
                    self.tt("dve", oacc[:, i, :], oacc[:, i, :], oaccs[1][:, i, :], ALU.add, ["oacc0", "oacc1"], ["oacc0"])
                def fin(lane, i, h=h, oacc=oacc):
                    sq, ytm, sm = sqs[lane], ytms[lane], fsms[lane]
                    self.act(sq[:], oacc[:, i, :], AF.Square, ["oacc0"], ["sq%d" % lane])
                    self.S.op("dve", lambda e: e.tensor_reduce(out=sm[:, 0:1], in_=sq[:], axis=mybir.AxisListType.X, op=ALU.add), reads=["sq%d" % lane], writes=["fsm%d" % lane])
                    self.ts("dve", sm[:, 0:1], sm[:, 0:1], 1.0 / 64, 1e-6, ALU.mult, ALU.add, ["fsm%d" % lane], ["fsm%d" % lane])
                    self.act(sm[:, 0:1], sm[:, 0:1], AF.Sqrt, ["fsm%d" % lane], ["fsm%d" % lane])
                    self.recip(sm[:, 1:2], sm[:, 0:1], ["fsm%d" % lane], ["fsm%d" % lane])
                    yield
                    self.stt(sq[:], oacc[:, i, :], sm[:, 1:2], gnorm, ALU.mult, ALU.mult, ["oacc0", "fsm%d" % lane, "rowp"], ["sq%d" % lane])
                    self.tt("dve", ytm[:], sq[:], sz[:, i, :], ALU.mult, ["sq%d" % lane, "sz%d" % i], ["ytm%d" % lane])
                    yield
                    pq, pqn = self.psq()
                    pqb = pq.bitcast(BF16)
                    self.tr(pqb[0:64, 0:128], ytm[:], self.idb[:], ["ytm%d" % lane, "consts2"], [pqn])
                    yi = lane
                    self.cp("act", yTs2[yi][:], pqb[0:64, 0:128], [pqn], ["yTs2_%d" % yi])
                    self.dma(self.yT_d[256 + h * 64:256 + (h + 1) * 64, i * 128:(i + 1) * 128], yTs2[yi][:], ["yTs2_%d" % yi], ["yTd"], "ytw%d" % yi)
                self.pipeline([(lambda lane, i=i: fin(lane, i)) for i in range(NT)], 3, 1)

    def mix_mlstm(self, l):
        nc, NT, SQ = self.nc, self.NT, self.SQ
        with ExitStack() as xs:
            sb = lambda n, s, d=F32: xs.enter_context(nc.sbuf_tensor("M%d_%s" % (l, n), s, d))
            NLM_ = 4
            wg = sb("wg", [128, 8, 16], BF16)
            w4 = sb("w4", [128, 8, 256], BF16)
            lf_all = sb("lf", [128, NT, 8]); ig_all = sb("ig", [128, NT, 8]); bc_all = sb("bc", [128, NT, 8])
            ib_all = sb("ib", [128, NT, 8]); wts_all = sb("wts", [128, NT, 8]); ebend_all = sb("ebend", [128, NT, 8])
            g16s = [sb("g16_%d" % i, [128, 16]) for i in range(3)]; th16s = [sb("th16_%d" % i, [128, 16]) for i in range(3)]; tmp8s = [sb("tmp8_%d" % i, [128, 8]) for i in range(3)]
            k_tm = sb("k_tm", [128, NT, 64], BF16); vext = sb("vext", [128, NT, 66], BF16)
            qT = sb("qT", [64, SQ], BF16); kT = sb("kT", [64, SQ], BF16)
            sigo = sb("sigo", [128, NT, 64], BF16); haccs = [sb("hacc%d" % i, [128, NT, 64]) for i in (0, 1)]
            qbs = [sb("qb%d" % i, [128, 64], BF16) for i in range(3)]
            Dm = [sb("Dm%d" % i, [128, 128]) for i in range(NLM_)]
            Pm = [sb("Pm%d" % i, [128, 128], BF16) for i in range(NLM_)]
            ebr = [sb("ebr%d" % i, [128, 128]) for i in range(NLM_)]
            qTs = [sb("qTs%d" % i, [64, 128], BF16) for i in range(NLM_)]
            Vs = [sb("Vs%d" % i, [128, 66], BF16) for i in range(NLM_)]
            C_fd = [sb("C_f%d" % i, [64, 66]) for i in (0, 1)]; C_bd = [[sb("C_b%d_%d" % (i, p_), [64, 66], BF16) for p_ in (0, 1)] for i in (0, 1)]
            sms = [sb("sm%d" % i, [128, 8]) for i in range(NLM_)]; sm = sms[0]; sqs = [sb("sq%d" % i, [128, 64]) for i in range(3)]; ytms = [sb("ytm%d" % i, [128, 64], BF16) for i in range(3)]; fsms = [sb("fsm%d" % i, [128, 4]) for i in range(3)]
            yTs2 = [sb("yTs2_%d" % i, [64, 128], BF16) for i in range(3)]
            self.load_w(wg[:], "ml_wg", self.win_d[l, :, C0 + 1024:C0 + 1040], 16)
            gb = self.rowp[:, ROWM["ml_gb"]:ROWM["ml_gb"] + 16]
            def gate(lane, i):
                g16, th16, tmp8 = g16s[lane], th16s[lane], tmp8s[lane]
                pq, pn = self.psq()
                self.u_tm(pq[:, 0:16], pn, i, [(0, wg)], "ml_wg", 16)
                self.tt("dve", g16[:], pq[:, 0:16], gb, ALU.add, [pn, "rowp"], ["g16_%d" % lane])
                self.act(th16[:], g16[:], AF.Tanh, ["g16_%d" % lane], ["th16_%d" % lane], scale=1.0 / 15.0)
                self.ts("dve", ig_all[:, i, :], th16[:, 0:8], 15.0, None, ALU.mult, None, ["th16_%d" % lane], ["mlg%d" % i])
                yield
                self.act(tmp8[:], th16[:, 8:16], AF.Exp, ["th16_%d" % lane], ["tmp8_%d" % lane], scale=-15.0)
                self.act(tmp8[:], tmp8[:], AF.Ln, ["tmp8_%d" % lane], ["tmp8_%d" % lane], bias=1.0)
                self.ts("dve", lf_all[:, i, :], tmp8[:], -1.0, None, ALU.mult, None, ["tmp8_%d" % lane], ["mlg%d" % i])
                yield
                pq, pn = self.psq()
                self.mm(pq[:, 0:4], self.cm(C_UT), lf_all[:, i, 0:4], True, True, ["mlg%d" % i, "consts"], [pn])
                self.mm(pq[:, 4:8], self.cm(C_LT), lf_all[:, i, 4:8], True, True, ["mlg%d" % i, "consts"], [pn])
                self.mm(pq[:, 8:16], self.cm(C_ONES), lf_all[:, i, 0:8], True, True, ["mlg%d" % i, "consts"], [pn])
                self.cp("dve", bc_all[:, i, :], pq[:, 0:8], [pn], ["mlg%d" % i])
                self.tt("dve", ib_all[:, i, :], ig_all[:, i, :], bc_all[:, i, :], ALU.subtract, ["mlg%d" % i], ["mlg%d" % i])
                yield
                self.tt("dve", tmp8[:], pq[:, 8:16], ib_all[:, i, :], ALU.add, [pn, "mlg%d" % i], ["tmp8_%d" % lane])
                self.act(wts_all[:, i, :], tmp8[:], AF.Exp, ["tmp8_%d" % lane], ["mlg%d" % i])
                self.act(ebend_all[:, i, :], pq[:, 8:16], AF.Exp, [pn], ["mlg%d" % i])
            self.pipeline([(lambda lane, i=i: gate(lane, i)) for i in range(NT)], 3, 1)
            import os
            stage = int(os.environ.get("ML_STAGE", "9"))
            if stage < 2:
                return
            self.S.op("pool", lambda e: e.memset(vext[:, :, 64:66], 1.0), writes=["vext%d" % t for t in range(NT)])
            for h in range(4):
                for j in range(4):
                    self.load_w(w4[:, :, j * 64:(j + 1) * 64], "ml_w4", self.win_d[l, :, C0 + j * 256 + h * 64:C0 + j * 256 + (h + 1) * 64], 64)
                def prep(lane, i, h=h):
                    qb = qbs[lane]
                    ph, pn = self.psh()
                    self.u_tm(ph[:, 0:256], pn, i, [(0, w4)], "ml_w4", 256)
                    mo = "qkvs"
                    yield
                    if "q" in mo:
                        self.cp("dve", qb[:], ph[:, 0:64], [pn], ["qb%d" % lane])
                    if "k" in mo:
                        self.ts("dve", k_tm[:, i, :], ph[:, 64:128], 0.125, None, ALU.mult, None, [pn], ["k_tm%d" % i])
                    if "v" in mo:
                        self.cp("act", vext[:, i, 0:64], ph[:, 128:192], [pn], ["vext%d" % i])
                    if "s" in mo:
                        self.act(sigo[:, i, :], ph[:, 192:256], AF.Sigmoid, [pn], ["sigo%d" % i])
                    yield
                    pq, pqn = self.psq()
                    pqb = pq.bitcast(BF16)
                    self.tr(pqb[0:64, 0:128], qb[:], self.idb[:], ["qb%d" % lane, "consts2"], [pqn])
                    self.tr(pqb[0:64, 128:256], k_tm[:, i, :], self.idb[:], ["k_tm%d" % i, "consts2"], [pqn])
                    self.cp("act", qT[:, i * 128:(i + 1) * 128], pqb[0:64, 0:128], [pqn], ["qT%d" % i])
                    self.cp("dve", kT[:, i * 128:(i + 1) * 128], pqb[0:64, 128:256], [pqn], ["kT%d" % i])
                self.pipeline([(lambda lane, i=i: prep(lane, i)) for i in range(NT)], 3, 1)
                if stage < 3:
                    continue
                for d in range(2):
                    self.S.op("pool", (lambda t_: lambda e: e.memset(t_[:], 0.0))(C_fd[d]), writes=["C_f%d" % d])
                    self.S.op("pool", (lambda t_: lambda e: e.memset(t_[:], 0.0))(C_bd[d][0]), writes=["C_b%d_0" % d])
                done = {0: -1, 1: -1}

                def unit(lane, d, c, pos, h=h, done=done):
                    col = d * 4 + h
                    C_f = C_fd[d]
                    cfn_ = "C_f%d" % d
                    C_rd, crn_ = C_bd[d][pos % 2], "C_b%d_%d" % (d, pos % 2)
                    C_wr, cwn_ = C_bd[d][(pos + 1) % 2], "C_b%d_%d" % (d, (pos + 1) % 2)
                    blk = slice(c * 128, (c + 1) * 128)
                    gl = lf_all[:, c, col:col + 1]
                    r = lane
                    sm = sms[lane]
                    pqA, pnA = self.bcast_ps(gl, d, ["mlg%d" % c])
                    self.exp_from_bc(Dm[r][:], "Dm%d" % r, pqA, pnA, ib_all[:, c, col:col + 1], 16.0, d, False, ["mlg%d" % c])
                    self.act(ebr[r][:], pqA, AF.Exp, [pnA], ["ebr%d" % r])
                    yield
                    pq, pn = self.psq()
                    self.mm(pq, kT[:, blk], qT[:, blk], True, True, ["kT%d" % c, "qT%d" % c], [pn])
                    self.tt("dve", Pm[r][:], pq, Dm[r][:], ALU.mult, [pn, "Dm%d" % r], ["Pm%d" % r])
                    yield
                    self.tt("pool", qTs[r][:], qT[:, blk], ebr[r][0:64, :], ALU.mult, ["qT%d" % c, "ebr%d" % r], ["qTs%d" % r])
                    self.ts("pool", Vs[r][:], vext[:, c, :], wts_all[:, c, col:col + 1], None, ALU.mult, None, ["vext%d" % c, "mlg%d" % c], ["Vs%d" % r])
                    pq3, pn3 = self.psq()
                    self.mm(pq3[0:64, 0:65], k_tm[:, c, :], Vs[r][:, 0:65], True, True, ["k_tm%d" % c, "Vs%d" % r], [pn3])
                    while done[d] != pos - 1:
                        yield
                    self.stt(C_f[:, 0:65], C_f[:, 0:65], ebend_all[0:64, c, col:col + 1], pq3[0:64, 0:65], ALU.mult, ALU.add, [cfn_, "mlg%d" % c, pn3], [cfn_])
                    self.cp("act", C_wr[:], C_f[:], [cfn_], [cwn_])
                    done[d] = pos
                    yield
                    pq2, pn2 = self.psq()
                    self.mm(pq2[:, 0:65], Pm[r][:], vext[:, c, 0:65], True, False, ["Pm%d" % r, "vext%d" % c], [pn2])
                    self.mm(pq2[:, 0:65], qTs[r][:], C_rd[:, 0:65], False, True, ["qTs%d" % r, crn_], [pn2])
                    self.act(sm[:, 0:1], pq2[:, 64:65], AF.Abs, [pn2], ["sm%d" % lane])
                    self.ts("dve", sm[:, 0:1], sm[:, 0:1], 1.0, None, ALU.max, None, ["sm%d" % lane], ["sm%d" % lane])
                    self.recip(sm[:, 1:2], sm[:, 0:1], ["sm%d" % lane], ["sm%d" % lane])
                    self.ts("dve", haccs[d][:, c, :], pq2[:, 0:64], sm[:, 1:2], None, ALU.mult, None, [pn2, "sm%d" % lane], ["hacc%d" % d])

                units = []
                for i in range(NT):
                    units.append((lambda lane, i=i: unit(lane, 0, i, i)))
                    units.append((lambda lane, i=i: unit(lane, 1, NT - 1 - i, i)))
                self.pipeline(units, NLM_, 2)
                hacc = haccs[0]
                sm = sms[0]
                for i in range(NT):
                    self.tt("dve", hacc[:, i, :], hacc[:, i, :], haccs[1][:, i, :], ALU.add, ["hacc0", "hacc1"], ["hacc0"])
                if stage < 4:
                    continue
                ng = self.rowp[:, ROWM["ml_norm"] + h * 64:ROWM["ml_norm"] + (h + 1) * 64]
                def fin(lane, i, h=h, ng=ng, hacc=hacc):
                    sq, ytm, sm = sqs[lane], ytms[lane], fsms[lane]
                    self.act(sq[:], hacc[:, i, :], AF.Square, ["hacc0"], ["sq%d" % lane])
                    self.S.op("dve", lambda e: e.tensor_reduce(out=sm[:, 0:1], in_=sq[:], axis=mybir.AxisListType.X, op=ALU.add), reads=["sq%d" % lane], writes=["fsm%d" % lane])
                    self.ts("dve", sm[:, 0:1], sm[:, 0:1], 1.0 / 64, 1e-6, ALU.mult, ALU.add, ["fsm%d" % lane], ["fsm%d" % lane])
                    self.act(sm[:, 0:1], sm[:, 0:1], AF.Sqrt, ["fsm%d" % lane], ["fsm%d" % lane])
                    self.recip(sm[:, 1:2], sm[:, 0:1], ["fsm%d" % lane], ["fsm%d" % lane])
                    yield
                    self.stt(sq[:], hacc[:, i, :], sm[:, 1:2], ng, ALU.mult, ALU.mult, ["hacc0", "fsm%d" % lane, "rowp"], ["sq%d" % lane])
                    self.tt("dve", ytm[:], sq[:], sigo[:, i, :], ALU.mult, ["sq%d" % lane, "sigo%d" % i], ["ytm%d" % lane])
                    yield
                    pq, pqn = self.psq()
                    pqb = pq.bitcast(BF16)
                    self.tr(pqb[0:64, 0:128], ytm[:], self.idb[:], ["ytm%d" % lane, "consts2"], [pqn])
                    yi = lane
                    self.cp("act", yTs2[yi][:], pqb[0:64, 0:128], [pqn], ["yTs2_%d" % yi])
                    self.dma(self.yT_d[512 + h * 64:512 + (h + 1) * 64, i * 128:(i + 1) * 128], yTs2[yi][:], ["yTs2_%d" % yi], ["yTd"], "ytw%d" % yi)
                self.pipeline([(lambda lane, i=i: fin(lane, i)) for i in range(NT)], 3, 1)

    def mix_rwkv(self, l):
        nc, NT, SQ = self.nc, self.NT, self.SQ
        X = mybir.AxisListType.X
        with ExitStack() as xs:
            sb = lambda n, s, d=F32: xs.enter_context(nc.sbuf_tensor("R%d_%s" % (l, n), s, d))
            mus1 = sb("mus1", [128, 960])
            wt = [sb("wt%d" % j, [128, 8, 192], BF16) for j in range(3)]
            rupw2 = sb("rupw2", [64, 256], BF16); rupa2 = sb("rupa2", [64, 256], BF16)
            ustg = [sb("ustg%d" % i, [128, 192], BF16) for i in range(2)]
            lrbs = [sb("lrb%d" % i, [128, 192], BF16) for i in (0, 1)]; ptmps = [sb("ptmp%d" % i, [128, 64]) for i in (0, 1)]
            lrs = [sb("lrs%d" % i, [64, 3, 128], BF16) for i in range(2)]
            rupg = sb("rupg", [64, 256], BF16)
            rstg = self.wstg[0][0:64].rearrange("p k c -> p (k c)")[:, 0:768]
            NL_ = 4
            lane_specs = [("u_f", [128, 192], BF16), ("lr", [128, 2], BF16), ("lrT", [64, 3, 128], BF16),
                          ("logw", [128, 64], F32), ("a_t", [128, 64], F32), ("kk", [128, 64], F32), ("kd", [128, 64], F32),
                          ("kka", [128, 64], F32), ("tmp", [128, 64], F32), ("tmp2", [128, 64], F32), ("cum_s", [128, 64], F32),
                          ("Gt", [128, 64], F32), ("iG", [128, 64], F32), ("Gw", [128, 64], F32), ("Gh", [128, 64], F32),
                          ("ops4", [128, 4, 64], BF16), ("opsT", [64, 4, 128], BF16), ("Bh", [128, 64], BF16), ("Kh", [128, 64], BF16),
                          ("v_b", [128, 64], BF16), ("GLc", [64, 2], F32), ("Bm", [128, 128], BF16), ("ArbT", [128, 128], BF16),
                          ("AakT", [128, 128], BF16), ("ArkT", [128, 128], BF16), ("Gm", [128, 64], BF16), ("Ub", [128, 64], BF16),
                          ("sm", [128, 8], F32)]
            lanes = [{n: sb("%s_%d" % (n, a), s, d_) for (n, s, d_) in lane_specs} for a in range(NL_)]
            self.S.local.update(n for (n, _, _) in lane_specs)
            P_fd = [sb("P_f%d" % d, [64, 64]) for d in range(2)]
            P_bd = [sb("P_b%d" % d, [64, 64], BF16) for d in range(2)]
            yaccs = [sb("yacc%d" % d, [128, NT, 64]) for d in range(2)]
            bacc = sb("bacc", [128, NT, 64], BF16); gacc = sb("gacc", [128, NT, 64], BF16)
            fsms = [sb("fsm%d" % i, [128, 8]) for i in range(3)]; ftmps = [sb("ftmp%d" % i, [128, 64]) for i in range(3)]; ftmp2s = [sb("ftmp2_%d" % i, [128, 64]) for i in range(3)]
            ytms = [sb("ytm%d" % i, [128, 64], BF16) for i in range(3)]; yTs2 = [sb("yTs2_%d" % i, [64, 128], BF16) for i in range(3)]
            mu0 = self.rowp[:, ROWM["mu"]:ROWM["mu"] + 960]
            mu1 = self.rowp[:, ROWM["mu"] + 960:ROWM["mu"] + 1920]
            self.tt("dve", mus1[:], mu0, mu1, ALU.add, ["rowp"], ["mus"])
            self.ts("dve", mus1[:], mus1[:], -1.0, 1.0, ALU.mult, ALU.add, ["mus"], ["mus"])
            musl = [mu0, mus1, mu1]
            self.dma(rstg, self.rup_d[l, :, :], [], ["wstg0"], "wstg0")
            self.cp("pool", rupw2[:], rstg[:, 0:256], ["wstg0"], ["rup"])
            self.cp("pool", rupa2[:], rstg[:, 256:512], ["wstg0"], ["rup"])
            self.cp("pool", rupg[:], rstg[:, 512:768], ["wstg0"], ["rup"])
            ro = lambda nm, a, b: self.rowp[:, ROWM[nm] + a:ROWM[nm] + b]
            for j in range(3):
                self.load_w(wt[j][:], "rw_wt%d" % j, self.win_d[l, :, D0 + 768:D0 + 960], 192, scale_row=musl[j][:, 768:960])
            def lrprep(lane, i):
                lrb, ptmp = lrbs[lane], ptmps[lane]
                ph, phn = self.psh()
                cc = 0
                for j in range(3):
                    for k in range(8):
                        self.mm(ph[:, 0:192], self.hT[:, k, 1 + i * 128 + j:1 + i * 128 + j + 128], wt[j][:, k, :], cc == 0, cc == 23,
                                ["hT", "rw_wt%d" % j, "mus"], [phn])
                        cc += 1
                yield
                self.act(ptmp[:], ph[:, 0:64], AF.Exp, [phn], ["ptmp%d" % lane], scale=2.0)
                self.ts("dve", ptmp[:], ptmp[:], 1.0, None, ALU.add, None, ["ptmp%d" % lane], ["ptmp%d" % lane])
                self.recip(ptmp[:], ptmp[:], ["ptmp%d" % lane], ["ptmp%d" % lane])
                self.ts("dve", lrb[:, 0:64], ptmp[:], -2.0, 1.0, ALU.mult, ALU.add, ["ptmp%d" % lane], ["lrb%d" % lane])
                self.cp("dve", lrb[:, 64:128], ph[:, 64:128], [phn], ["lrb%d" % lane])
                self.act(ptmp[:], ph[:, 128:192], AF.Exp, [phn], ["ptmp%d" % lane], scale=-1.0)
                self.ts("dve", ptmp[:], ptmp[:], 1.0, None, ALU.add, None, ["ptmp%d" % lane], ["ptmp%d" % lane])
                self.recip(ptmp[:], ptmp[:], ["ptmp%d" % lane], ["ptmp%d" % lane])
                self.cp("dve", lrb[:, 128:192], ptmp[:], ["ptmp%d" % lane], ["lrb%d" % lane])
                yield
                p2, p2n = self.psh()
                p2b = p2.bitcast(BF16)
                for a3 in range(3):
                    self.tr(p2b[0:64, a3 * 128:(a3 + 1) * 128], lrb[:, a3 * 64:(a3 + 1) * 64], self.idb[:], ["lrb%d" % lane, "consts2"], [p2n])
                si = lane
                self.cp("act", lrs[si][:], p2b[0:64, 0:384].rearrange("p (a t) -> p a t", a=3), [p2n], ["lrs%d" % si])
                self.dma(self.lrT_d[:, :, i * 128:(i + 1) * 128].rearrange("a p t -> p a t"), lrs[si][:], ["lrs%d" % si], ["lrTd"], "lrs%d" % si)
            self.pipeline([(lambda lane, i=i: lrprep(lane, i)) for i in range(NT)], 2, 1)
            for h in range(4):
                hc = slice(h * 64, (h + 1) * 64)
                for j in range(3):
                    for b3 in range(3):
                        c0 = b3 * 256 + h * 64
                        self.load_w(wt[j][:, :, b3 * 64:(b3 + 1) * 64], "rw_wt%d" % j, self.win_d[l, :, D0 + c0:D0 + c0 + 64], 64,
                                    scale_row=musl[j][:, c0:c0 + 64])
                rkvd = self.rkv_d[h % 2]
                rkn = "rkvd%d" % (h % 2)
                for i in range(NT):
                    ph, phn = self.psh()
                    cc = 0
                    for j in range(3):
                        for k in range(8):
                            self.mm(ph[:, 0:192], self.hT[:, k, 1 + i * 128 + j:1 + i * 128 + j + 128], wt[j][:, k, :], cc == 0, cc == 23,
                                    ["hT", "rw_wt%d" % j, "mus"], [phn])
                            cc += 1
                    si = self.rot("ustg", 2)
                    self.cp("act", ustg[si][:], ph[:, 0:192], [phn], ["ustg%d" % si])
                    self.dma(rkvd[i * 128:(i + 1) * 128, :], ustg[si][:], ["ustg%d" % si], [rkn], "ustg%d" % si)
                for d in range(2):
                    self.S.op("pool", (lambda t_: lambda e: e.memset(t_[:], 0.0))(P_fd[d]), writes=["P_f%d" % d])
                    self.S.op("pool", (lambda t_: lambda e: e.memset(t_[:], 0.0))(P_bd[d]), writes=["P_b%d" % d])
                done = {0: -1, 1: -1}

                def unit(lane, d, c, pos, h=h, hc=hc, done=done, rkvd=rkvd, rkn=rkn):
                    T_ = lanes[lane]
                    (u_f, lr, lrT, logw, a_t, kk, kd, kka, tmp, tmp2, cum_s, Gt, iG, Gw, Gh, ops4, opsT, Bh, Kh, v_b, GLc,
                     Bm, ArbT, AakT, ArkT, Gm, Ub, sm) = [T_[n] for (n, _, _) in lane_specs]
                    P_f, P_b = P_fd[d], P_bd[d]
                    pfn_, pbn_ = "P_f%d" % d, "P_b%d" % d
                    yacc = yaccs[d]
                    self.dma(u_f[:], rkvd[c * 128:(c + 1) * 128, :], [rkn], ["u_f"], "rwu%d" % lane)
                    self.dma(lrT[:], self.lrT_d[:, :, c * 128:(c + 1) * 128].rearrange("a p t -> p a t"), ["lrTd"], ["lrT"], "rwl%d" % lane)
                    r_ = u_f[:, 0:64]; k_ = u_f[:, 64:128]
                    self.cp("pool", v_b[:], u_f[:, 128:192], ["u_f"], ["v_b"])
                    yield
                    pq, pn = self.psq()
                    self.mm(pq[:, 0:64], lrT[d * 32:(d + 1) * 32, 0, :], rupw2[d * 32:(d + 1) * 32, hc], True, True, ["lrT", "rup"], [pn])
                    self.tt("dve", tmp[:], pq[:, 0:64], ro("rw_w0", d * 256 + h * 64, d * 256 + h * 64 + 64), ALU.add, [pn, "rowp"], ["tmp"])
                    self.act(tmp[:], tmp[:], AF.Exp, ["tmp"], ["tmp"], scale=-1.0)
                    self.ts("dve", tmp[:], tmp[:], 1.0, None, ALU.add, None, ["tmp"], ["tmp"])
                    self.recip(tmp[:], tmp[:], ["tmp"], ["tmp"])
                    self.ts("dve", logw[:], tmp[:], -0.6065306597126334, None, ALU.mult, None, ["tmp"], ["logw"])
                    pq, pn = self.psq()
                    self.mm(pq[:, 0:64], lrT[d * 32:(d + 1) * 32, 1, :], rupa2[d * 32:(d + 1) * 32, hc], True, True, ["lrT", "rup"], [pn])
                    self.tt("dve", tmp[:], pq[:, 0:64], ro("rw_a0", d * 256 + h * 64, d * 256 + h * 64 + 64), ALU.add, [pn, "rowp"], ["tmp"])
                    self.act(tmp[:], tmp[:], AF.Exp, ["tmp"], ["tmp"], scale=-1.0)
                    self.ts("dve", tmp[:], tmp[:], 1.0, None, ALU.add, None, ["tmp"], ["tmp"])
                    self.recip(a_t[:], tmp[:], ["tmp"], ["a_t"])
                    if d == 0:
                        pq, pn = self.psq()
                        self.mm(pq[:, 0:64], lrT[:, 2, :], rupg[:, hc], True, True, ["lrT", "rup"], [pn])
                        self.cp("act", gacc[:, c, :], pq[:, 0:64], [pn], ["gacc"])
                        self.tt("pool", tmp2[:], r_, k_, ALU.mult, ["u_f"], ["tmp2"])
                        self.tt("pool", tmp2[:], tmp2[:], ro("rw_rk", h * 64, h * 64 + 64), ALU.mult, ["tmp2", "rowp"], ["tmp2"])
                        self.S.op("dve", lambda e: e.tensor_reduce(out=sm[:, 6:7], in_=tmp2[:], axis=X, op=ALU.add), reads=["tmp2"], writes=["sm"])
                        self.ts("dve", bacc[:, c, :], u_f[:, 128:192], sm[:, 6:7], None, ALU.mult, None, ["u_f", "sm"], ["bacc"])
                    yield
                    self.tt("dve", kk[:], k_, ro("rw_kk", h * 64, h * 64 + 64), ALU.mult, ["u_f", "rowp"], ["kk"])
                    self.act(tmp2[:], kk[:], AF.Square, ["kk"], ["tmp2"])
                    self.S.op("dve", lambda e: e.tensor_reduce(out=sm[:, 0:1], in_=tmp2[:], axis=X, op=ALU.add), reads=["tmp2"], writes=["sm"])
                    self.act(sm[:, 0:1], sm[:, 0:1], AF.Ln, ["sm"], ["sm"], bias=1e-6)
                    self.act(sm[:, 1:2], sm[:, 0:1], AF.Exp, ["sm"], ["sm"], scale=-0.5)
                    self.ts("dve", kk[:], kk[:], sm[:, 1:2], None, ALU.mult, None, ["kk", "sm"], ["kk"])
                    self.stt(tmp[:], a_t[:], -1.0, ro("rw_ka", h * 64, h * 64 + 64), ALU.add, ALU.mult, ["a_t", "rowp"], ["tmp"])
                    self.stt(kd[:], tmp[:], 1.0, k_, ALU.add, ALU.mult, ["tmp", "u_f"], ["kd"])
                    self.tt("pool", kka[:], kk[:], a_t[:], ALU.mult, ["kk", "a_t"], ["kka"])
                    yield
                    pq, pn = self.psq()
                    self.mm(pq[:, 0:64], self.cm(C_LT if d else C_UT), logw[:], True, True, ["logw", "consts"], [pn])
                    self.cp("dve", cum_s[:], pq[:, 0:64], [pn], ["cum_s"])
                    self.mm(pq[:, 64:128], self.cm(C_ONES), logw[:], True, True, ["logw", "consts"], [pn])
                    self.tt("dve", tmp[:], pq[:, 64:128], cum_s[:], ALU.subtract, [pn, "cum_s"], ["tmp"])
                    self.act(Gh[:], tmp[:], AF.Exp, ["tmp"], ["Gh"])
                    pq2, pn2 = self.psq()
                    self.mm(pq2[0:64, 0:2], logw[:], self.cm(C_ONES)[:, 0:2], True, True, ["logw", "consts"], [pn2])
                    self.act(GLc[:], pq2[0:64, 0:2], AF.Exp, [pn2], ["GLc"])
                    self.act(Gt[:], cum_s[:], AF.Exp, ["cum_s"], ["Gt"])
                    self.act(iG[:], cum_s[:], AF.Exp, ["cum_s"], ["iG"], scale=-1.0)
                    self.tt("pool", tmp2[:], cum_s[:], logw[:], ALU.subtract, ["cum_s", "logw"], ["tmp2"])
                    self.act(Gw[:], tmp2[:], AF.Exp, ["tmp2"], ["Gw"])
                    yield
                    self.stt(ops4[:, 0, :], kk[:], -1.0, Gw[:], ALU.mult, ALU.mult, ["kk", "Gw"], ["ops4"])
                    self.tt("dve", ops4[:, 1, :], r_, Gt[:], ALU.mult, ["u_f", "Gt"], ["ops4"])
                    self.tt("pool", ops4[:, 2, :], kka[:], iG[:], ALU.mult, ["kka", "iG"], ["ops4"])
                    self.tt("pool", ops4[:, 3, :], kd[:], iG[:], ALU.mult, ["kd", "iG"], ["ops4"])
                    self.tt("pool", Bh[:], kka[:], Gh[:], ALU.mult, ["kka", "Gh"], ["Bh"])
                    self.tt("dve", Kh[:], kd[:], Gh[:], ALU.mult, ["kd", "Gh"], ["Kh"])
                    yield
                    ph, phn = self.psh()
                    phb = ph.bitcast(BF16)
                    for q4 in range(4):
                        self.tr(phb[0:64, q4 * 128:(q4 + 1) * 128], ops4[:, q4, :], self.idb[:], ["ops4", "consts2"], [phn])
                    self.cp("act", opsT[:], phb[0:64, 0:512].rearrange("p (a t) -> p a t", a=4), [phn], ["opsT"])
                    ar = opsT[:, 0:2, :].rearrange("p a t -> p (a t)")
                    m1, m1n = self.psh()
                    self.mm(m1, opsT[:, 2, :], ar, True, True, ["opsT"], [m1n])
                    sF, iF = (C_STR_B, C_INC_B) if d else (C_STR_F, C_INC_F)
                    self.stt(Bm[:], m1[:, 0:128], -1.0, self.cm(sF), ALU.mult, ALU.mult, [m1n, "consts"], ["Bm"])
                    self.tt("dve", ArbT[:], m1[:, 128:256], self.cm(iF), ALU.mult, [m1n, "consts"], ["ArbT"])
                    yield
                    m2, m2n = self.psh()
                    self.mm(m2, opsT[:, 3, :], ar, True, True, ["opsT"], [m2n])
                    self.tt("dve", AakT[:], m2[:, 0:128], self.cm(sF), ALU.mult, [m2n, "consts"], ["AakT"])
                    self.tt("dve", ArkT[:], m2[:, 128:256], self.cm(iF), ALU.mult, [m2n, "consts"], ["ArkT"])
                    W, wn = yield from self.dnc_gen(Bm[:], "Bm", d)
                    while done[d] != pos - 1:
                        yield
                    pq, pn = self.psq()
                    self.mm(pq[:, 0:64], opsT[:, 0, :], P_b[:], True, False, ["opsT", pbn_], [pn])
                    self.mm(pq[:, 0:64], AakT[:], v_b[:], False, True, ["AakT", "v_b"], [pn])
                    self.cp("act", Gm[:], pq[:, 0:64], [pn], ["Gm"])
                    yield
                    pq, pn = self.psq()
                    self.mm(pq[:, 0:64], W[:], Gm[:], True, True, [wn, "Gm"], [pn])
                    self.cp("act", Ub[:], pq[:, 0:64], [pn], ["Ub"])
                    yield
                    pq, pn = self.psq()
                    self.mm(pq[:, 0:64], opsT[:, 1, :], P_b[:], True, False, ["opsT", pbn_], [pn])
                    self.mm(pq[:, 0:64], ArbT[:], Ub[:], False, False, ["ArbT", "Ub"], [pn])
                    self.mm(pq[:, 0:64], ArkT[:], v_b[:], False, True, ["ArkT", "v_b"], [pn])
                    self.cp("dve", yacc[:, c, :], pq[:, 0:64], [pn], ["yacc%d" % d])
                    yield
                    pq4, pn4 = self.psq()
                    self.mm(pq4[0:64, 0:64], Bh[:], Ub[:], True, False, ["Bh", "Ub"], [pn4])
                    self.mm(pq4[0:64, 0:64], Kh[:], v_b[:], False, True, ["Kh", "v_b"], [pn4])
                    self.stt(P_f[:], P_f[:], GLc[:, 0:1], pq4[0:64, 0:64], ALU.mult, ALU.add, [pfn_, "GLc", pn4], [pfn_])
                    self.cp("act", P_b[:], P_f[:], [pfn_], [pbn_])
                    done[d] = pos

                units = []
                for i in range(NT):
                    units.append((lambda lane, i=i: unit(lane, 0, i, i)))
                    units.append((lambda lane, i=i: unit(lane, 1, NT - 1 - i, i)))
                self.pipeline(units, NL_, 2)
                def fin(lane, c, h=h):
                    tmp, tmp2, sm, ytm = ftmps[lane], ftmp2s[lane], fsms[lane], ytms[lane]
                    self.tt("dve", yaccs[0][:, c, :], yaccs[0][:, c, :], yaccs[1][:, c, :], ALU.add, ["yacc0", "yacc1"], ["yacc0"])
                    yacc = yaccs[0]
                    yv = yacc[:, c, :]
                    self.S.op("dve", (lambda yv_: lambda e: e.tensor_reduce(out=sm[:, 2:3], in_=yv_, axis=X, op=ALU.add))(yv), reads=["yacc0"], writes=["fsm%d" % lane])
                    self.ts("dve", sm[:, 2:3], sm[:, 2:3], 1.0 / 64, None, ALU.mult, None, ["fsm%d" % lane], ["fsm%d" % lane])
                    self.ts("dve", tmp[:], yv, sm[:, 2:3], None, ALU.subtract, None, ["yacc0", "fsm%d" % lane], ["ftmp%d" % lane])
                    yield
                    self.act(tmp2[:], tmp[:], AF.Square, ["ftmp%d" % lane], ["ftmp2_%d" % lane])
                    self.S.op("dve", lambda e: e.tensor_reduce(out=sm[:, 3:4], in_=tmp2[:], axis=X, op=ALU.add), reads=["ftmp2_%d" % lane], writes=["fsm%d" % lane])
                    self.ts("dve", sm[:, 3:4], sm[:, 3:4], 1.0 / 64, 64e-5, ALU.mult, ALU.add, ["fsm%d" % lane], ["fsm%d" % lane])
                    self.act(sm[:, 3:4], sm[:, 3:4], AF.Sqrt, ["fsm%d" % lane], ["fsm%d" % lane])
                    self.recip(sm[:, 4:5], sm[:, 3:4], ["fsm%d" % lane], ["fsm%d" % lane])
                    yield
                    self.stt(tmp[:], tmp[:], sm[:, 4:5], ro("rw_gnw", h * 64, h * 64 + 64), ALU.mult, ALU.mult, ["ftmp%d" % lane, "fsm%d" % lane, "rowp"], ["ftmp%d" % lane])
                    self.tt("dve", tmp[:], tmp[:], ro("rw_gnb", h * 64, h * 64 + 64), ALU.add, ["ftmp%d" % lane, "rowp"], ["ftmp%d" % lane])
                    self.tt("dve", tmp[:], tmp[:], bacc[:, c, :], ALU.add, ["ftmp%d" % lane, "bacc"], ["ftmp%d" % lane])
                    self.tt("dve", ytm[:], tmp[:], gacc[:, c, :], ALU.mult, ["ftmp%d" % lane, "gacc"], ["ytm%d" % lane])
                    yield
                    pq, pqn = self.psq()
                    pqb = pq.bitcast(BF16)
                    self.tr(pqb[0:64, 0:128], ytm[:], self.idb[:], ["ytm%d" % lane, "consts2"], [pqn])
                    yi = lane
                    self.cp("act", yTs2[yi][:], pqb[0:64, 0:128], [pqn], ["yTs2_%d" % yi])
                    self.dma(self.yT_d[768 + h * 64:768 + (h + 1) * 64, c * 128:(c + 1) * 128], yTs2[yi][:], ["yTs2_%d" % yi], ["yTd"], "ytw%d" % yi)
                self.pipeline([(lambda lane, c=c: fin(lane, c)) for c in range(NT)], 3, 1)

    def phase_cd(self, l, src):
        nc, NT = self.nc, self.NT
        with ExitStack() as cs:
            sb = lambda n, s, d=F32: cs.enter_context(nc.sbuf_tensor("C%d_%s" % (l, n), s, d))
            wo = sb("wo", [128, 8, D], BF16)
            fwi = sb("fwi", [128, 8, 2 * DFF], BF16)
            fwo = sb("fwo", [128, 22, D], BF16)
            rowcd = sb("rowcd", [128, 2048])
            self.dma(rowcd[:], self.row_d[l, 0, 0:2048].partition_broadcast(128), [], ["rowcd"], "par")
            self.dma(wo[:], self.wout_d[l, :, :].rearrange("(k p) c -> p k c", p=128), [], ["wo"], "wcd", eng="pool")
            for c in range(4):
                self.dma(fwi[:, :, c * 1408:(c + 1) * 1408], self.fwi_d[l, :, c * 1408:(c + 1) * 1408].rearrange("(k p) c -> p k c", p=128),
                         [], ["fwi"], "wcd", eng="pool")
            for k0 in (0, 8, 16):
                kn = min(8, 22 - k0)
                self.dma(fwo[:, k0:k0 + kn, :], self.fwo_d[l, k0 * 128:(k0 + kn) * 128, :].rearrange("(k p) c -> p k c", p=128),
                         [], ["fwo"], "wcd", eng="pool")
            xa_ = [sb("xa%d" % i, [128, D]) for i in range(2)]
            t1_ = [sb("t1%d" % i, [128, D]) for i in range(2)]
            yt_ = [sb("yt%d" % i, [128, 8, 128], BF16) for i in range(2)]
            xn_ = [sb("xn%d" % i, [128, D], BF16) for i in range(2)]
            h2T_ = [sb("h2T%d" % i, [128, 8, 128], BF16) for i in range(2)]
            sg_ = [sb("sg%d" % i, [128, 512]) for i in range(2)]
            hact_ = [sb("hact0", [128, DFF], BF16)] * 2
            actT_ = [sb("actT%d" % i, [128, 22, 128], BF16) for i in range(2)]
            st_ = [sb("st%d" % i, [128, 8]) for i in range(2)]
            self.S.local.update(["xa", "t1", "yt", "xn", "h2T", "sg", "actT", "st"])
            gpost = rowcd[:, 0:1024]
            gfpost = rowcd[:, 1024:2048]
            gcol2 = self.colp[:, COL_OFF["n_ffn_pre"]:COL_OFF["n_ffn_pre"] + 8]
            pf = lambda: (lambda i: (self.ps[i], "pf%d" % i))(self.rot("pf", 8))

            def rms_from_psum(pairs, ncol0, outs, grow, addin, t1, st):
                a, b = st[:, ncol0:ncol0 + 1], st[:, ncol0 + 1:ncol0 + 2]
                for h, (pb, pn) in enumerate(pairs):
                    self.act(t1[:, h * 512:(h + 1) * 512], pb[:], AF.Square, [pn], ["t1"])
                self.S.op("dve", lambda e: e.tensor_reduce(out=a, in_=t1[:], axis=mybir.AxisListType.X, op=ALU.add), reads=["t1"], writes=["st"])
                self.ts("dve", a, a, 1.0 / D, 1e-6, ALU.mult, ALU.add, ["st"], ["st"])
                self.act(a, a, AF.Sqrt, ["st"], ["st"])
                self.recip(b, a, ["st"], ["st"])
                for h, (pb, pn) in enumerate(pairs):
                    self.stt(t1[:, h * 512:(h + 1) * 512], pb[:], b, grow[:, h * 512:(h + 1) * 512], ALU.mult, ALU.mult,
                             [pn, "st", "rowcd"], ["t1"])
                self.tt("dve", outs, t1[:], addin, ALU.add, ["t1", "xa"], ["xa"])

            for i in range(NT):
                self.S.lane = i % 2
                xa, t1, yt, xn, h2T, sg, hact, actT, st = (xa_[i % 2], t1_[i % 2], yt_[i % 2], xn_[i % 2], h2T_[i % 2], sg_[i % 2],
                                                           hact_[i % 2], actT_[i % 2], st_[i % 2])
                self.dma(yt[:], self.yT_d[:, i * 128:(i + 1) * 128].rearrange("(k p) t -> p k t", p=128), ["yTd"], ["yt"], "yt%d" % (i % 2))
                self.dma(xa[:], src[i * 128:(i + 1) * 128, :], ["xres%d" % i], ["xa"], "xa%d" % (i % 2))
                pairs = []
                for nh in range(2):
                    pb, pn = pf()
                    for k in range(8):
                        self.mm(pb[:], yt[:, k, :], wo[:, k, nh * 512:(nh + 1) * 512], k == 0, k == 7, ["yt", "wo"], [pn])
                    pairs.append((pb, pn))
                rms_from_psum(pairs, 0, xa[:], gpost, xa[:], t1, st)
                self.act(t1[:], xa[:], AF.Square, ["xa"], ["t1"])
                self.S.op("dve", (lambda st_, t1_: lambda e: e.tensor_reduce(out=st_[:, 2:3], in_=t1_[:], axis=mybir.AxisListType.X, op=ALU.add))(st, t1), reads=["t1"], writes=["st"])
                self.ts("dve", st[:, 2:3], st[:, 2:3], 1.0 / D, 1e-6, ALU.mult, ALU.add, ["st"], ["st"])
                self.act(st[:, 2:3], st[:, 2:3], AF.Sqrt, ["st"], ["st"])
                self.recip(st[:, 3:4], st[:, 2:3], ["st"], ["st"])
                self.ts("dve", xn[:], xa[:], st[:, 3:4], None, ALU.mult, None, ["xa", "st"], ["xn"])
                pb, pn = pf()
                pbb = pb[:].bitcast(BF16)
                for k in range(8):
                    self.tr(pbb[:, k * 128:(k + 1) * 128], xn[:, k * 128:(k + 1) * 128], self.idb[:], ["xn", "consts2"], [pn])
                self.tt("dve", h2T[:], pbb[:, 0:1024].rearrange("p (k t) -> p k t", k=8),
                        gcol2.unsqueeze(2).to_broadcast([128, 8, 128]), ALU.mult, [pn, "colp"], ["h2T"])
                for j in range(6):
                    wj = 512 if j < 5 else 256
                    pg, pgn = pf()
                    for k in range(8):
                        self.mm(pg[:, 0:wj], h2T[:, k, :], fwi[:, k, j * 512:j * 512 + wj], k == 0, k == 7, ["h2T", "fwi"], [pgn])
                    pu, pun = pf()
                    for k in range(8):
                        self.mm(pu[:, 0:wj], h2T[:, k, :], fwi[:, k, DFF + j * 512:DFF + j * 512 + wj], k == 0, k == 7, ["h2T", "fwi"], [pun])
                    self.act(sg[:, 0:wj], pg[:, 0:wj], AF.Silu, [pgn], ["sg"])
                    self.tt("dve", hact[:, j * 512:j * 512 + wj], sg[:, 0:wj], pu[:, 0:wj], ALU.mult, ["sg", pun], ["hact"])
                for g in range(3):
                    k0 = g * 8
                    kn = min(8, 22 - k0)
                    pb, pn = pf()
                    pbb = pb[:].bitcast(BF16)
                    for k in range(kn):
                        self.tr(pbb[:, k * 128:(k + 1) * 128], hact[:, (k0 + k) * 128:(k0 + k + 1) * 128], self.idb[:], ["hact", "consts2"], [pn])
                    self.cp("act", actT[:, k0:k0 + kn, :], pbb[:, 0:kn * 128].rearrange("p (k t) -> p k t", k=kn), [pn], ["actT"])
                pairs = []
                for nh in range(2):
                    pb, pn = pf()
                    for k in range(22):
                        self.mm(pb[:], actT[:, k, :], fwo[:, k, nh * 512:(nh + 1) * 512], k == 0, k == 21, ["actT", "fwo"], [pn])
                    pairs.append((pb, pn))
                rms_from_psum(pairs, 4, xa[:], gfpost, xa[:], t1, st)
                self.dma(self.out_d[i * 128:(i + 1) * 128, :], xa[:], ["xa"], ["xres%d" % i], "xout%d" % (i % 2))
            self.S.lane = None


_PROG = {}


def _get_prog(NT, NL, dbg=False):
    key = (NT, NL, dbg)
    if key not in _PROG:
        kb = KB(NT, NL, dbg)
        _PROG[key] = kb.build()
    return _PROG[key]


def make_in_maps(inputs, NT, NL, ncores):
    packs = [pack_layer_params(inputs, l) for l in range(NL)]
    shared = {
        "w_in": np.ascontiguousarray(inputs["w_in"][:NL], dtype=np.float32),
        "w_out": np.ascontiguousarray(inputs["w_out"][:NL], dtype=np.float32),
        "ffn_w_in": np.ascontiguousarray(inputs["ffn_w_in"][:NL], dtype=np.float32),
        "ffn_w_out": np.ascontiguousarray(inputs["ffn_w_out"][:NL], dtype=np.float32),
        "rowp": np.stack([p[0] for p in packs]),
        "colp": np.stack([p[1] for p in packs]),
        "lru_gw": np.stack([p[2] for p in packs]),
        "rw_up": np.stack([p[3] for p in packs]),
        "consts": host_consts(),
    }
    x = np.asarray(inputs["x"], dtype=np.float32)
    maps = []
    for c in range(ncores):
        m = dict(shared)
        m["x"] = np.ascontiguousarray(x[c, :NT * 128])
        maps.append(m)
    return maps


def kernel(**inputs):
    inputs = {k: np.asarray(v) for k, v in inputs.items()}
    NT, NL, ncores = 32, 2, 8
    nc = _get_prog(NT, NL)
    maps = make_in_maps(inputs, NT, NL, ncores)
    res = run_bass_kernel_spmd(nc, maps, core_ids=list(range(ncores)))
    return np.stack([res.results[c]["out"] for c in range(ncores)]).astype(np.float32)
```
